# Optimizing a Trainium2 kernel written in Bass

```python
import math
import jax, jax.numpy as jnp
from jax import lax
import numpy as np

D_MODEL = 2048
BATCH = 2
SEQ = 4096
DEPTH = 2
DEC_BATCH = 4
DEC_SEQ = 2048
PAST_LEN = 128

D_MIX = D_MODEL
D_GROUP = D_MIX // 4
POOL_WINDOWS = (2, 4, 8, 16)
N_POOL = len(POOL_WINDOWS)
POOL_CH = D_GROUP // N_POOL
HEAD_DIM = 64
N_Q_HEADS = D_GROUP // HEAD_DIM
N_KV_HEADS = 2
GQA_GROUP = N_Q_HEADS // N_KV_HEADS
WINDOW = 128
BLOCK = 128
ROPE_DIM = HEAD_DIM // 4
ROPE_THETA = 500000.0
CONV_W = 3
HY_ORDER = 2
HY_EMB = 33
HY_BANDS = (HY_EMB - 1) // 2
HY_HIDDEN = 64
HY_FILTERS = HY_ORDER * 2 * D_GROUP
HY_FAST_DECAY_PCT = 0.3
HY_SLOW_DECAY_PCT = 1.5
HY_DECAY_TARGET = 1e-2
D_FF = ((8 * D_MODEL // 3 + 255) // 256) * 256
NORM_EPS = 1e-6

OFF_A = 0
OFF_Q = OFF_A + D_GROUP
OFF_K = OFF_Q + N_Q_HEADS * HEAD_DIM
OFF_V = OFF_K + N_KV_HEADS * HEAD_DIM
OFF_C = OFF_V + N_KV_HEADS * HEAD_DIM
OFF_D = OFF_C + 3 * D_GROUP
D_IN = OFF_D + 3 * D_GROUP

kernel_name = "hymba_parallel_pool_swa_conv_hyena_encoder"


def rms_norm(x, g):
    xf = x.astype(jnp.float32)
    y = xf * lax.rsqrt(jnp.mean(xf * xf, axis=-1, keepdims=True) + NORM_EPS)
    return (y * g.astype(jnp.float32)).astype(x.dtype)


def short_conv(u, w):
    L = u.shape[1]
    up = jnp.pad(u, ((0, 0), (1, 1), (0, 0)))
    return up[:, 0:L] * w[0] + up[:, 1:L + 1] * w[1] + up[:, 2:L + 2] * w[2]


def pool_mixer(p, pool_w, pool_scale):
    B, L, _ = p.shape
    pg = p.astype(jnp.float32).reshape(B, L, N_POOL, POOL_CH)
    cs = jnp.concatenate([jnp.zeros((B, 1, N_POOL, POOL_CH), jnp.float32), jnp.cumsum(pg, axis=1)], axis=1)
    t = jnp.arange(L)[:, None]
    half = jnp.array(POOL_WINDOWS, dtype=jnp.int32)[None, :] // 2
    lo = jnp.clip(t - half, 0, L)
    hi = jnp.clip(t + half, 0, L)
    gidx = jnp.arange(N_POOL)[None, :]
    pooled = (cs[:, hi, gidx] - cs[:, lo, gidx]) / (hi - lo).astype(jnp.float32)[None, :, :, None]
    d = pooled - pg
    out = jnp.einsum('blgc,gcd->blgd', d, pool_w.astype(jnp.float32))
    return (out.reshape(B, L, D_GROUP) * pool_scale.astype(jnp.float32)).astype(p.dtype)


def rotate_half(x):
    x1, x2 = jnp.split(x, 2, axis=-1)
    return jnp.concatenate([-x2, x1], axis=-1)


def partial_rope(x):
    L = x.shape[1]
    inv_freq = ROPE_THETA ** (-jnp.arange(0, ROPE_DIM, 2, dtype=jnp.float32) / ROPE_DIM)
    ang = jnp.arange(L, dtype=jnp.float32)[:, None] * inv_freq[None, :]
    ang = jnp.concatenate([ang, ang], axis=-1)[None, :, None, :]
    cos = jnp.cos(ang).astype(x.dtype)
    sin = jnp.sin(ang).astype(x.dtype)
    rot = x[..., :ROPE_DIM]
    return jnp.concatenate([rot * cos + rotate_half(rot) * sin, x[..., ROPE_DIM:]], axis=-1)


def window_attention(q, k, v, sink):
    B, L = q.shape[0], q.shape[1]
    nb = L // BLOCK
    qb = q.reshape(B, nb, BLOCK, N_KV_HEADS, GQA_GROUP, HEAD_DIM)

    def band(t):
        tp = jnp.pad(t, ((0, 0), (BLOCK, BLOCK), (0, 0), (0, 0))).reshape(B, nb + 2, BLOCK, N_KV_HEADS, HEAD_DIM)
        return jnp.concatenate([tp[:, :-2], tp[:, 1:-1], tp[:, 2:]], axis=2)

    kb, vb = band(k), band(v)
    s = jnp.einsum('bnrhgd,bnchd->bnhgrc', qb, kb, preferred_element_type=jnp.float32) * (HEAD_DIM ** -0.5)
    r = jnp.arange(BLOCK)
    c = jnp.arange(3 * BLOCK)
    rel = c[None, :] - BLOCK - r[:, None]
    keypos = jnp.arange(nb)[:, None] * BLOCK - BLOCK + c[None, :]
    mask = (jnp.abs(rel) <= WINDOW)[None, :, :] & ((keypos >= 0) & (keypos < L))[:, None, :]
    s = jnp.where(mask[None, :, None, None, :, :], s, -jnp.inf)
    sink_l = sink.astype(jnp.float32).reshape(N_KV_HEADS, GQA_GROUP)[None, None, :, :, None, None]
    m = jnp.maximum(jnp.max(s, axis=-1, keepdims=True), sink_l)
    e = jnp.exp(s - m)
    p = e / (jnp.sum(e, axis=-1, keepdims=True) + jnp.exp(sink_l - m))
    o = jnp.einsum('bnhgrc,bnchd->bnrhgd', p.astype(v.dtype), vb)
    return o.reshape(B, L, N_Q_HEADS * HEAD_DIM)


def gated_short_conv(h, bg, cg, w):
    return bg * short_conv(cg * h, w)


def hyena_filters(L, w1, b1, w2, b2, w3, freq, decay):
    f32 = jnp.float32
    t = jnp.linspace(0.0, 1.0, L, dtype=f32)[:, None]
    bands = jnp.linspace(1e-4, HY_BANDS - 1, HY_BANDS, dtype=f32)[None, :]
    wpos = (2.0 * math.pi / L) * jnp.arange(L, dtype=f32)[:, None]
    z = jnp.concatenate([t, jnp.cos(bands * wpos), -jnp.sin(bands * wpos)], axis=-1)
    fr = freq.astype(f32)
    h = jnp.sin(fr * (z @ w1.astype(f32) + b1.astype(f32)))
    h = jnp.sin(fr * (h @ w2.astype(f32) + b2.astype(f32)))
    h = (h @ w3.astype(f32)) * jnp.exp(-t * jnp.abs(decay.astype(f32)))
    h = h.reshape(L, HY_ORDER, 2, D_GROUP)
    fwd, bwd = h[:, :, 0], h[:, :, 1]
    k = jnp.concatenate([fwd, jnp.zeros((1, HY_ORDER, D_GROUP), f32), bwd[1:][::-1]], axis=0)
    return k / jnp.sum(jnp.abs(k), axis=0, keepdims=True)


def long_conv(u, kf):
    L = u.shape[1]
    uf = jnp.fft.rfft(u, n=2 * L, axis=1)
    return jnp.fft.irfft(uf * kf[None], n=2 * L, axis=1)[:, :L]


def hyena_mixer(zd, short_w, w1, b1, w2, b2, w3, freq, decay, bias):
    L = zd.shape[1]
    z = short_conv(zd, short_w).astype(jnp.float32)
    v, g1, g2 = jnp.split(z, 3, axis=-1)
    kf = jnp.fft.rfft(hyena_filters(L, w1, b1, w2, b2, w3, freq, decay), axis=0)
    bias = bias.astype(jnp.float32)
    u = v
    for i, g in enumerate((g1, g2)):
        u = g * (long_conv(u, kf[:, i]) + bias[i] * u)
    return u.astype(zd.dtype)


def trunk(x, mix_norm_g, w_in, pool_w, pool_scale, attn_sink, sconv_w, hy_short_w,
          hy_w1, hy_b1, hy_w2, hy_b2, hy_w3, hy_freq, hy_decay, hy_bias,
          w_out, ffn_norm_g, w_gate_up, w_down, final_norm_g):
    B, L, _ = x.shape
    for l in range(DEPTH):
        hn = rms_norm(x, mix_norm_g[l])
        z = hn @ w_in[l]
        za = z[..., OFF_A:OFF_Q]
        q = partial_rope(z[..., OFF_Q:OFF_K].reshape(B, L, N_Q_HEADS, HEAD_DIM))
        k = partial_rope(z[..., OFF_K:OFF_V].reshape(B, L, N_KV_HEADS, HEAD_DIM))
        v = z[..., OFF_V:OFF_C].reshape(B, L, N_KV_HEADS, HEAD_DIM)
        ch, cb, cc = jnp.split(z[..., OFF_C:OFF_D], 3, axis=-1)
        zd = z[..., OFF_D:D_IN]
        out_a = pool_mixer(za, pool_w[l], pool_scale[l])
        out_b = window_attention(q, k, v, attn_sink[l])
        out_c = gated_short_conv(ch, cb, cc, sconv_w[l])
        out_d = hyena_mixer(zd, hy_short_w[l], hy_w1[l], hy_b1[l], hy_w2[l], hy_b2[l], hy_w3[l],
                            hy_freq[l], hy_decay[l], hy_bias[l])
        mix = jnp.concatenate([out_a.astype(x.dtype), out_b.astype(x.dtype),
                               out_c.astype(x.dtype), out_d.astype(x.dtype)], axis=-1)
        x = x + mix @ w_out[l]
        h2 = rms_norm(x, ffn_norm_g[l])
        gate, up = jnp.split(h2 @ w_gate_up[l], 2, axis=-1)
        x = x + (jax.nn.silu(gate) * up) @ w_down[l]
    return rms_norm(x, final_norm_g)


def setup_inputs(seed: int = 0) -> dict:
    key = jax.random.key(seed)
    ks = jax.random.split(key, 24)
    f32 = jnp.float32
    nrm = lambda k, shape, s: jax.random.normal(k, shape, f32) * s
    max_decay = math.log(HY_DECAY_TARGET) / HY_FAST_DECAY_PCT
    min_decay = math.log(HY_DECAY_TARGET) / HY_SLOW_DECAY_PCT
    decay0 = jnp.tile(jnp.abs(jnp.linspace(min_decay, max_decay, D_GROUP, dtype=f32)), HY_ORDER * 2)
    return {
        "x_prompt": nrm(ks[0], (BATCH, SEQ, D_MODEL), 1.0),
        "x_sample": nrm(ks[1], (DEC_BATCH, DEC_SEQ, D_MODEL), 1.0),
        "mix_norm_g": 1.0 + nrm(ks[2], (DEPTH, D_MODEL), 0.02),
        "w_in": nrm(ks[3], (DEPTH, D_MODEL, D_IN), D_MODEL ** -0.5),
        "pool_w": nrm(ks[4], (DEPTH, N_POOL, POOL_CH, POOL_CH), POOL_CH ** -0.5),
        "pool_scale": 1.0 + nrm(ks[5], (DEPTH, D_GROUP), 0.02),
        "attn_sink": nrm(ks[6], (DEPTH, N_Q_HEADS), 0.5),
        "sconv_w": nrm(ks[7], (DEPTH, CONV_W, D_GROUP), CONV_W ** -0.5),
        "hy_short_w": nrm(ks[8], (DEPTH, CONV_W, 3 * D_GROUP), CONV_W ** -0.5),
        "hy_w1": nrm(ks[9], (DEPTH, HY_EMB, HY_HIDDEN), HY_EMB ** -0.5),
        "hy_b1": nrm(ks[10], (DEPTH, HY_HIDDEN), 0.1),
        "hy_w2": nrm(ks[11], (DEPTH, HY_HIDDEN, HY_HIDDEN), HY_HIDDEN ** -0.5),
        "hy_b2": nrm(ks[12], (DEPTH, HY_HIDDEN), 0.1),
        "hy_w3": nrm(ks[13], (DEPTH, HY_HIDDEN, HY_FILTERS), HY_HIDDEN ** -0.5),
        "hy_freq": 1.0 + nrm(ks[14], (DEPTH, HY_HIDDEN), 0.1),
        "hy_decay": decay0[None, :] * (1.0 + nrm(ks[15], (DEPTH, HY_FILTERS), 0.05)),
        "hy_bias": nrm(ks[16], (DEPTH, HY_ORDER, D_GROUP), 0.5),
        "w_out": nrm(ks[17], (DEPTH, D_MIX, D_MODEL), D_MIX ** -0.5),
        "ffn_norm_g": 1.0 + nrm(ks[18], (DEPTH, D_MODEL), 0.02),
        "w_gate_up": nrm(ks[19], (DEPTH, D_MODEL, 2 * D_FF), D_MODEL ** -0.5),
        "w_down": nrm(ks[20], (DEPTH, D_FF, D_MODEL), D_FF ** -0.5),
        "final_norm_g": 1.0 + nrm(ks[21], (D_MODEL,), 0.02),
    }


def reference(x_prompt, x_sample, mix_norm_g, w_in, pool_w, pool_scale, attn_sink, sconv_w, hy_short_w,
              hy_w1, hy_b1, hy_w2, hy_b2, hy_w3, hy_freq, hy_decay, hy_bias,
              w_out, ffn_norm_g, w_gate_up, w_down, final_norm_g):
    y_prompt = trunk(x_prompt, mix_norm_g, w_in, pool_w, pool_scale, attn_sink, sconv_w, hy_short_w,
                     hy_w1, hy_b1, hy_w2, hy_b2, hy_w3, hy_freq, hy_decay, hy_bias,
                     w_out, ffn_norm_g, w_gate_up, w_down, final_norm_g)
    y_sample = trunk(x_sample, mix_norm_g, w_in, pool_w, pool_scale, attn_sink, sconv_w, hy_short_w,
                     hy_w1, hy_b1, hy_w2, hy_b2, hy_w3, hy_freq, hy_decay, hy_bias,
                     w_out, ffn_norm_g, w_gate_up, w_down, final_norm_g)
    return (y_prompt, y_sample)
```

```python
import math
import numpy as np
import ml_dtypes
import concourse.bass as bass
import concourse.mybir as mybir
from concourse.bass_utils import run_bass_kernel_spmd

F32 = mybir.dt.float32
BF16 = mybir.dt.bfloat16
AF = mybir.ActivationFunctionType
ALU = mybir.AluOpType
AX = mybir.AxisListType

D = 2048
DEPTH = 2
T = 2048
HALO = 128
TE = T + 2 * HALO
DIN = 4352
DFF = 5632
NFF = DFF // 128
EPS = 1e-6
NFFT = 8192
PAIRS = [[0, 1], [2, 3], [4, 5], [6, 7]]
ENGS = ['pe', 'act', 'dve', 'pool', 'sp']


class Sched:
    def __init__(self, nc, es):
        self.nc = nc
        self.ops = {e: [] for e in ENGS}
        self.cnt = {e: 0 for e in ENGS}
        self.sem = {e: es.enter_context(nc.semaphore('s_' + e)) for e in ['pe', 'act', 'dve', 'pool']}
        self.R = 8
        self.dsem = {q: [es.enter_context(nc.semaphore('d_%s%d' % (q, i))) for i in range(self.R)]
                     for q in ['sp', 'act', 'pool']}
        self.dcnt = {q: 0 for q in ['sp', 'act', 'pool']}
        self.ccsem = es.enter_context(nc.semaphore('ccsem'))
        self.cccnt = 0
        self.ccsem2 = es.enter_context(nc.semaphore('ccsem2'))
        self.sticky = set()
        self.waited = {e: {} for e in ENGS}
        self.lastw = {}
        self.reads = {}
        self.same_engine_sync = True

    def _wait(self, eng, tok):
        sk, sh, val = tok
        if self.waited[eng].get(sk, 0) >= val:
            return
        if sk == eng and (eng == 'pe' or not self.same_engine_sync):
            return
        self.waited[eng][sk] = val
        self.ops[eng].append(lambda e, sh=sh, val=val: e.wait_ge(sh, val))

    def _deps(self, eng, reads, writes):
        for k in reads:
            if k in self.lastw:
                self._wait(eng, self.lastw[k])
        for k in writes:
            if k in self.lastw:
                self._wait(eng, self.lastw[k])
            for tok in self.reads.get(k, {}).values():
                self._wait(eng, tok)

    def _commit(self, tok, reads, writes):
        for k in writes:
            self.lastw[k] = tok
            self.reads[k] = {}
        for k in reads:
            d = self.reads.setdefault(k, {})
            if tok[0] not in d or d[tok[0]][2] < tok[2]:
                d[tok[0]] = tok

    def op(self, eng, fn, reads=(), writes=()):
        self._deps(eng, reads, writes)
        self.cnt[eng] += 1
        val = self.cnt[eng]
        sh = self.sem[eng]
        self.ops[eng].append(lambda e, fn=fn, sh=sh: fn(e).then_inc(sh, 1))
        self._commit((eng, sh, val), reads, writes)

    def dma(self, q, out, in_, reads=(), writes=(), **kw):
        i = self.dcnt[q]
        self.dcnt[q] += 1
        slot = i % self.R
        val = 16 * (i // self.R + 1)
        sh = self.dsem[q][slot]
        sk = 'd_%s%d' % (q, slot)
        if val > 16:
            self._wait(q, (sk, sh, val - 16))
        self._deps(q, reads, writes)
        self.ops[q].append(lambda e, out=out, in_=in_, sh=sh, kw=kw:
                           e.dma_start(out=out, in_=in_, **kw).then_inc(sh, 16))
        self._commit((sk, sh, val), reads, writes)

    def allgather(self, in_ap, out_ap, reads=(), writes=(), groups=None, big=False):
        q = 'pool'
        self._deps(q, reads, writes)
        if big:
            self.cc2cnt = getattr(self, 'cc2cnt', 0) + 1
            sh, sk, val = self.ccsem2, 'cc2', self.cc2cnt
        else:
            self.cccnt += 1
            sh, sk, val = self.ccsem, 'cc', self.cccnt
        groups = groups or PAIRS
        self.ops[q].append(lambda e, sh=sh, groups=groups: e.collective_compute(
            "AllGather", ALU.bypass, replica_groups=groups,
            ins=[in_ap], outs=[out_ap]).then_inc(sh))
        self._commit((sk, sh, val), reads, writes)

    def barrier(self):
        toks = []
        for e in ['pe', 'act', 'dve', 'pool']:
            if self.cnt[e] > 0:
                toks.append((e, self.sem[e], self.cnt[e]))
        for q in ['sp', 'act', 'pool']:
            n = self.dcnt[q]
            for slot in range(self.R):
                uses = (n - slot + self.R - 1) // self.R if n > slot else 0
                if uses > 0:
                    toks.append(('d_%s%d' % (q, slot), self.dsem[q][slot], 16 * uses))
        if self.cccnt:
            toks.append(('cc', self.ccsem, self.cccnt))
        for e in ENGS:
            for tok in toks:
                if tok[0] == e and e == 'pe':
                    continue
                sk, sh, val = tok
                if self.waited[e].get(sk, 0) >= val:
                    continue
                self.waited[e][sk] = val
                self.ops[e].append(lambda en, sh=sh, val=val: en.wait_ge(sh, val))
        self.lastw = {k: v for k, v in self.lastw.items() if k in self.sticky}
        self.reads = {}

    def emit(self):
        self.barrier()
        nc = self.nc
        ops = self.ops
        with nc.Block() as block:
            @block.tensor
            def _(e):
                for f in ops['pe']:
                    f(e)

            @block.scalar
            def _(e):
                for f in ops['act']:
                    f(e)

            @block.vector
            def _(e):
                for f in ops['dve']:
                    f(e)

            @block.gpsimd
            def _(e):
                for f in ops['pool']:
                    f(e)

            @block.sync
            def _(e):
                for f in ops['sp']:
                    f(e)


class Arena:
    def __init__(self, nc, base, limit):
        self.nc = nc
        self.base = base
        self.limit = limit
        self.off = base
        self.n = 0
        self.marks = []

    def mark(self):
        self.marks.append(self.off)

    def release(self):
        self.off = self.marks.pop()

    def alloc(self, shape, dtype, name=None):
        nbytes = int(np.prod(shape[1:])) * (2 if dtype == BF16 else 4)
        nbytes = (nbytes + 63) // 64 * 64
        off = self.off
        assert off + nbytes <= self.limit, ("SBUF arena overflow", name, off, nbytes, self.limit)
        self.off += nbytes
        self.n += 1
        return self.nc.alloc_sbuf_tensor_at("%s_%d" % (name or 't', self.n), list(shape), dtype, offset=off)


def sap(t, part0, nparts, dims, off=0):
    full = t[:]
    pstep = full.ap[0][0]
    return bass.AP(t, full.offset + part0 * pstep + off, [[pstep, nparts]] + [[s, c] for (s, c) in dims])


def build(dbg=None):
    nc = bass.Bass("TRN2", target_bir_lowering=False)
    from contextlib import ExitStack
    es = ExitStack()
    S = Sched(nc, es)
    dbg = dbg or {}
    ins = {}

    def din(name, shape, dt=F32):
        ins[name] = nc.dram_tensor(name, list(shape), dt, kind="ExternalInput")
        return ins[name]

    x_ext = din("x_ext", [TE, D])
    mix_g = din("mix_norm_g", [DEPTH, D])
    w_in = din("w_in", [DEPTH, D, DIN])
    pool_w = din("pool_w", [DEPTH, 4, 128, 128])
    pool_scale = din("pool_scale", [DEPTH, 512])
    attn_sink = din("attn_sink", [DEPTH, 8])
    sconv_w = din("sconv_w", [DEPTH, 3, 512])
    hy_short_w = din("hy_short_w", [DEPTH, 3, 1536])
    hy_w1 = din("hy_w1", [DEPTH, 33, 64])
    hy_b1 = din("hy_b1", [DEPTH, 64])
    hy_w2 = din("hy_w2", [DEPTH, 64, 64])
    hy_b2 = din("hy_b2", [DEPTH, 64])
    hy_w3 = din("hy_w3_sh", [DEPTH, 64, 512])
    hy_freq = din("hy_freq", [DEPTH, 64])
    hy_decay = din("hy_decay_sh", [DEPTH, 512])
    hy_bias = din("hy_bias", [DEPTH, 2, 512])
    w_out = din("w_out", [DEPTH, D, D])
    ffn_g = din("ffn_norm_g", [DEPTH, D])
    w_gu = din("w_gate_up", [DEPTH, D, 2 * DFF])
    w_down = din("w_down", [DEPTH, DFF, D])
    fin_g = din("final_norm_g", [D])
    t_ident = din("t_ident", [128, 128], BF16)
    t_rope = din("t_rope", [2, 16, TE])
    t_perm = din("t_perm", [16, 16])
    t_mask = din("t_mask", [4, 128, 128], BF16)
    t_invcnt = din("t_invcnt", [4, T])
    t_flags = din("t_flags", [128, 2])
    t_w1d = din("t_w1d", [32, 66], BF16)
    t_w1f = din("t_w1f", [64, 66], BF16)
    t_m = din("t_m", [5, 128, 3 * 8 * 128], BF16)
    t_g = din("t_g", [128, 3 * 128], BF16)
    t_h = din("t_h", [66, 128 * 16], BF16)
    t_pos = din("t_pos", [33, NFFT])
    t_tdec = din("t_tdec", [1, NFFT])

    y_out = nc.dram_tensor("y_out", [T, D], F32, kind="ExternalOutput")
    dbg_out = {}
    for k, shp in dbg.items():
        dbg_out[k] = nc.dram_tensor("dbg_" + k, list(shp[0]), shp[1], kind="ExternalOutput")

    x1_ext = nc.dram_tensor("x1_ext", [TE, D], F32)
    xmid = nc.dram_tensor("xmid", [T, D], F32)
    mixT_d = nc.dram_tensor("mixT_d", [D, T], BF16)
    edge_in = nc.dram_tensor("edge_in", [256, D], F32)
    edge_all = nc.dram_tensor("edge_all", [512, D], F32)
    hv_in = nc.dram_tensor("hv_in", [64, 128 * 128], BF16)
    hv_all = nc.dram_tensor("hv_all", [128, 128 * 128], BF16)
    hu_in = nc.dram_tensor("hu_in", [64, 128 * 128], BF16)
    hu_all = nc.dram_tensor("hu_all", [128, 128 * 128], BF16)
    g12_d = nc.dram_tensor("g12_d", [2, 512, T], BF16)
    kk_d = nc.dram_tensor("kk_d", [2, 128, NFFT], BF16)
    kf_d = nc.dram_tensor("kf_d", [4 * 2560, 2 * 8 * 128], BF16)
    kf_loc = nc.dram_tensor("kf_loc", [4 * 640, 2 * 8 * 128], BF16)
    rn_loc = nc.dram_tensor("rn_loc", [4, 128], F32)
    rn_d = nc.dram_tensor("rn_d", [16, 128], F32)

    AR = Arena(nc, 16640, 229312)
    PS = [nc.alloc_psum_tensor("ps%d" % i, [128, 512], F32) for i in range(8)]

    def psb(i):
        return PS[i]

    ident = AR.alloc([128, 128], BF16, "ident")
    S.dma('sp', ident[:, :], t_ident[:, :], writes=['ident'])
    gcolM = AR.alloc([128, DEPTH, 16], F32, "gcolM")
    gcolF = AR.alloc([128, DEPTH, 16], F32, "gcolF")
    for l in range(DEPTH):
        S.dma('sp', gcolM[:, l, :], mix_g[l].rearrange("(k p) -> p k", p=128), writes=['gcol'],
              allow_slow_non_contiguous=True)
        S.dma('sp', gcolF[:, l, :], ffn_g[l].rearrange("(k p) -> p k", p=128), writes=['gcol'],
              allow_slow_non_contiguous=True)
    ctx = dict(nc=nc, S=S, AR=AR, PS=PS, ident=ident, gcolM=gcolM, gcolF=gcolF, ins=ins,
               dbg=dbg, dbg_out=dbg_out)
    ctx.update(x_ext=x_ext, x1_ext=x1_ext, xmid=xmid, mixT_d=mixT_d, y_out=y_out,
               edge_in=edge_in, edge_all=edge_all, hv_in=hv_in, hv_all=hv_all,
               hu_in=hu_in, hu_all=hu_all, g12_d=g12_d, kk_d=kk_d, kf_d=kf_d, kf_loc=kf_loc,
               rn_loc=rn_loc, rn_d=rn_d)
    return nc, S, ctx, es


def rms_to_T(ctx, xt, gcol_l, hT, col0, uid):
    S, AR, PS, ident = ctx['S'], ctx['AR'], ctx['PS'], ctx['ident']
    sc = ctx['sc_junk']
    ss = ctx['sc_ss']
    xn = ctx['sc_xn']
    S.op('act', lambda e: e.activation(out=sc[:, :], in_=xt[:, :], func=AF.Square, accum_out=ss[:, 0:1]),
         reads=[uid], writes=[ctx.get('junk_key', 'sc_junk'), 'sc_ss'])
    S.op('dve', lambda e: e.tensor_scalar(out=ss[:, 1:2], in0=ss[:, 0:1], scalar1=1.0 / D, scalar2=EPS,
                                          op0=ALU.mult, op1=ALU.add), reads=['sc_ss'], writes=['sc_ss1'])
    S.op('act', lambda e: e.activation(out=ss[:, 3:4], in_=ss[:, 1:2], func=AF.Sqrt),
         reads=['sc_ss1'], writes=['sc_ss3'])
    S.op('dve', lambda e: e.reciprocal(out=ss[:, 2:3], in_=ss[:, 3:4]), reads=['sc_ss3'], writes=['sc_ss2'])
    S.op('act', lambda e: e.activation(out=xn[:, :], in_=xt[:, :], func=AF.Copy, scale=ss[:, 2:3]),
         reads=[uid, 'sc_ss2'], writes=['sc_xn'])
    for q in range(4):
        bank = ctx['tp_bank'][ctx['tp_i'] % 2]
        ctx['tp_i'] += 1
        pst = PS[bank][:, :].bitcast(BF16)
        for j in range(4):
            kc = q * 4 + j
            S.op('pe', lambda e, kc=kc, j=j, pst=pst: e.transpose(out=pst[:, j * 128:(j + 1) * 128],
                                                                    in_=xn[:, kc * 128:(kc + 1) * 128],
                                                                    identity=ident[:, :]),
                 reads=['sc_xn', 'ident'], writes=['ps%d' % bank])
        S.op('dve', lambda e, q=q, pst=pst: e.tensor_tensor(
            out=hT[:, q * 4:(q + 1) * 4, col0:col0 + 128],
            in0=pst[:, 0:512].rearrange("p (a b) -> p a b", a=4),
            in1=gcol_l[:, q * 4:(q + 1) * 4].unsqueeze(2).to_broadcast([128, 4, 128]),
            op=ALU.mult), reads=['ps%d' % bank, 'gcol'], writes=['hT'])


def wload(ctx, dst, src, key):
    ctx['S'].dma('pool', dst, src, writes=[key])


def proj(ctx, wblk, wkey, c0, M, hT, tok0, ntok, bank):
    S, PS = ctx['S'], ctx['PS']
    for kc in range(16):
        S.op('pe', lambda e, kc=kc: e.matmul(PS[bank][0:M, 0:ntok], lhsT=wblk[:, kc, c0:c0 + M],
                                               rhs=hT[:, kc, tok0:tok0 + ntok],
                                               start=(kc == 0), stop=(kc == 15)),
             reads=[wkey, 'hT'], writes=['ps%d' % bank])


TOKG = [(0, 512), (512, 512), (1024, 512), (1536, 512), (2048, 256)]


def phase_norm(ctx, l, xsrc, tiles=None):
    S, AR = ctx['S'], ctx['AR']
    hT = ctx['hT']
    AR.mark()
    ctx['sc_junk'] = AR.alloc([128, D], F32, 'junk')
    ctx['junk_key'] = 'sc_junk'
    ctx['sc_ss'] = AR.alloc([128, 4], F32, 'ss')
    ctx['sc_xn'] = AR.alloc([128, D], BF16, 'xn')
    xt = [AR.alloc([128, D], F32, 'xt%d' % i) for i in range(2)]
    ctx['tp_bank'] = [0, 1]
    ctx['tp_i'] = 0
    for n, i in enumerate(tiles if tiles is not None else range(TE // 128)):
        b = n % 2
        S.dma('sp', xt[b][:, :], xsrc[i * 128:(i + 1) * 128, :], writes=['xt%d' % b])
        rms_to_T(ctx, xt[b], ctx['gcolM'][:, l, :], hT, i * 128, 'xt%d' % b)
    S.barrier()
    AR.release()


def conv3(ctx, eng, out, zin, wcol, c_lo, n, rkey, wkey_out):
    S = ctx['S']
    S.op(eng, lambda e: e.tensor_scalar(out=out[:, 0:n], in0=zin[:, c_lo:c_lo + n], scalar1=wcol[:, 1:2],
                                        scalar2=None, op0=ALU.mult), reads=[rkey, 'cw'], writes=[wkey_out])
    S.op(eng, lambda e: e.scalar_tensor_tensor(out=out[:, 0:n], in0=zin[:, c_lo - 1:c_lo - 1 + n],
                                               scalar=wcol[:, 0:1], in1=out[:, 0:n], op0=ALU.mult, op1=ALU.add),
         reads=[rkey, 'cw', wkey_out], writes=[wkey_out])
    S.op(eng, lambda e: e.scalar_tensor_tensor(out=out[:, 0:n], in0=zin[:, c_lo + 1:c_lo + 1 + n],
                                               scalar=wcol[:, 2:3], in1=out[:, 0:n], op0=ALU.mult, op1=ALU.add),
         reads=[rkey, 'cw', wkey_out], writes=[wkey_out])


def phase_local(ctx, l):
    S, AR, PS, ins = ctx['S'], ctx['AR'], ctx['PS'], ctx['ins']
    hT = ctx['hT']
    w_in = ins['w_in']
    AR.mark()
    wb = [AR.alloc([128, 16, 128], BF16, 'wb%d' % i) for i in range(3)]
    zc = [AR.alloc([128, TE], F32, 'zc%d' % i) for i in range(3)]
    sA = AR.alloc([128, TE], F32, 'sA')
    sB = AR.alloc([128, TE], F32, 'sB')
    invc = AR.alloc([128, T], F32, 'invc')
    dT = AR.alloc([128, T], BF16, 'dT')
    ob = [AR.alloc([128, T], BF16, 'ob%d' % i) for i in range(2)]
    pw = AR.alloc([128, 4, 128], BF16, 'pw')
    pscale = AR.alloc([128, 4], F32, 'pscale')
    cwC = AR.alloc([128, 4, 3], F32, 'cwC')
    cwD = AR.alloc([128, 12, 3], F32, 'cwD')
    wload(ctx, pw[:, :, :], ins['pool_w'][l].rearrange("g c d -> c g d"), 'pw')
    S.dma('sp', pscale[:, :], ins['pool_scale'][l].rearrange("(g p) -> p g", p=128), writes=['pscale'],
          allow_slow_non_contiguous=True)
    for k in range(3):
        S.dma('sp', cwC[:, :, k], ins['sconv_w'][l][k].rearrange("(c p) -> p c", p=128), writes=['cw'],
              allow_slow_non_contiguous=True)
        S.dma('sp', cwD[:, :, k], ins['hy_short_w'][l][k].rearrange("(c p) -> p c", p=128), writes=['cw'],
              allow_slow_non_contiguous=True)
    st = {'wi': 0, 'pb': 0, 'oi': 0}

    def zblock(blk, zi):
        w = st['wi'] % 3
        st['wi'] += 1
        wload(ctx, wb[w][:, :, :], w_in[l][:, blk * 128:(blk + 1) * 128].rearrange("(k p) c -> p k c", p=128),
              'wb%d' % w)
        for (t0, n) in TOKG:
            bank = 2 + st['pb'] % 4
            st['pb'] += 1
            proj(ctx, wb[w], 'wb%d' % w, 0, 128, hT, t0, n, bank)
            S.op('act', lambda e, bank=bank, t0=t0, n=n: e.copy(out=zc[zi][:, t0:t0 + n], in_=PS[bank][:, 0:n]),
                 reads=['ps%d' % bank], writes=['zc%d' % zi])

    def out_rows(row0, obuf, okey):
        S.dma('act', ctx['mixT_d'][row0:row0 + 128, :], obuf[:, :], reads=[okey], writes=['mixT_d'])

    for g in range(4):
        zblock(g, 0)
        z = zc[0]
        S.dma('sp', invc[:, :], ins['t_invcnt'][g:g + 1, :].to_broadcast([128, T]), writes=['invc'])
        S.op('dve', lambda e: e.tensor_tensor(out=sA[:, 1:TE], in0=z[:, 0:TE - 1], in1=z[:, 1:TE], op=ALU.add),
             reads=['zc0'], writes=['sA'])
        cur, ckey, oth, okey = sA, 'sA', sB, 'sB'
        vlo, vhi = 1, TE
        for lev in range(g):
            h = 1 << lev
            lo, hi = vlo + h, vhi - h
            S.op('dve', lambda e, cur=cur, oth=oth, h=h, lo=lo, hi=hi: e.tensor_tensor(
                out=oth[:, lo:hi], in0=cur[:, lo - h:hi - h], in1=cur[:, lo + h:hi + h], op=ALU.add),
                reads=[ckey], writes=[okey])
            vlo, vhi = lo, hi
            cur, ckey, oth, okey = oth, okey, cur, ckey
        S.op('dve', lambda e, cur=cur, oth=oth: e.tensor_tensor(out=oth[:, 0:T], in0=cur[:, HALO:HALO + T], in1=invc[:, :],
                                                       op=ALU.mult), reads=[ckey, 'invc'], writes=[okey])
        S.op('dve', lambda e, oth=oth: e.tensor_tensor(out=dT[:, :], in0=oth[:, 0:T], in1=z[:, HALO:HALO + T],
                                                       op=ALU.subtract), reads=[okey, 'zc0'], writes=['dT'])
        o = st['oi'] % 2
        st['oi'] += 1
        for q in range(4):
            bank = 6 + q % 2
            S.op('pe', lambda e, q=q, bank=bank, g=g: e.matmul(PS[bank][:, :], lhsT=pw[:, g, :],
                                                               rhs=dT[:, q * 512:(q + 1) * 512], start=True, stop=True),
                 reads=['pw', 'dT'], writes=['ps%d' % bank])
            S.op('act', lambda e, q=q, bank=bank, g=g, o=o: e.activation(
                out=ob[o][:, q * 512:(q + 1) * 512], in_=PS[bank][:, :], func=AF.Copy, scale=pscale[:, g:g + 1]),
                reads=['ps%d' % bank, 'pscale'], writes=['ob%d' % o])
        out_rows(g * 128, ob[o], 'ob%d' % o)

    for c in range(4):
        zblock(10 + c, 0)
        zblock(14 + c, 1)
        zblock(18 + c, 2)
        S.op('pool', lambda e: e.tensor_tensor(out=sA[:, :], in0=zc[2][:, :], in1=zc[0][:, :], op=ALU.mult),
             reads=['zc0', 'zc2'], writes=['sA'])
        conv3(ctx, 'dve', sB, sA, cwC[:, c, :], HALO, T, 'sA', 'sB')
        o = st['oi'] % 2
        st['oi'] += 1
        S.op('dve', lambda e, o=o: e.tensor_tensor(out=ob[o][:, :], in0=sB[:, 0:T], in1=zc[1][:, HALO:HALO + T],
                                                   op=ALU.mult), reads=['sB', 'zc1'], writes=['ob%d' % o])
        out_rows(1024 + c * 128, ob[o], 'ob%d' % o)

    for j in range(12):
        zi = j % 3
        zblock(22 + j, zi)
        o = st['oi'] % 2
        st['oi'] += 1
        eng = 'dve'
        conv3(ctx, eng, sA if j % 2 == 0 else sB, zc[zi], cwD[:, j, :], HALO, T, 'zc%d' % zi,
              'sA' if j % 2 == 0 else 'sB')
        src = sA if j % 2 == 0 else sB
        S.op('act', lambda e, o=o, src=src: e.copy(out=ob[o][:, :], in_=src[:, 0:T]),
             reads=['sA' if j % 2 == 0 else 'sB'], writes=['ob%d' % o])
        if j < 4:
            S.dma('act', ctx['hv_in'][j * 16:(j + 1) * 16, :].rearrange("t (c s) -> c t s", s=128),
                  ob[o][:, :].rearrange("c (t s) -> c t s", s=128), reads=['ob%d' % o], writes=['hv_in'])
        else:
            jj = j - 4
            S.dma('act', ctx['g12_d'][jj // 4, (jj % 4) * 128:(jj % 4 + 1) * 128, :], ob[o][:, :],
                  reads=['ob%d' % o], writes=['g12_d'])
    S.barrier()
    AR.release()
    S.allgather(ctx['hv_in'][:, :], ctx['hv_all'][:, :], reads=['hv_in'], writes=['hv_all'])
    for gath in ctx.pop('deferred_gathers', []):
        gath()


def phase_attn(ctx, l):
    S, AR, PS, ins, ident = ctx['S'], ctx['AR'], ctx['PS'], ctx['ins'], ctx['ident']
    hT = ctx['hT']
    w_in = ins['w_in']
    AR.mark()
    wb = [AR.alloc([128, 16, 128], BF16, 'wb%d' % i) for i in range(2)]
    qraw = AR.alloc([64, TE], F32, 'qraw')
    rope = AR.alloc([16, 2, TE], F32, 'rope')
    rt1 = AR.alloc([16, 512], F32, 'rt1')
    rt2 = AR.alloc([16, 512], F32, 'rt2')
    perm = AR.alloc([16, 16], F32, 'perm')
    qk = AR.alloc([64, 10, TE], BF16, 'qk')
    vp = AR.alloc([128, 18, 2, 66], BF16, 'vp')
    E = [AR.alloc([128, 512], BF16, 'E%d' % i) for i in range(6)]
    Ot = AR.alloc([128, 512], BF16, 'Ot')
    den = AR.alloc([128, 8], F32, 'den')
    mixB = AR.alloc([128, 4, T], BF16, 'mixB')
    masks = AR.alloc([128, 4, 128], BF16, 'masks')
    snk = AR.alloc([128, 8], F32, 'snk')
    S.dma('sp', rope[:, :, :], ins['t_rope'].rearrange("a d t -> d a t"), writes=['rope'])
    S.dma('sp', perm[:, :], ins['t_perm'][:, :], writes=['perm'])
    S.dma('sp', masks[:, :, :], ins['t_mask'].rearrange("m k q -> k m q"), writes=['masks'])
    S.dma('sp', snk[:, :], ins['attn_sink'][l:l + 1, :].to_broadcast([128, 8]), writes=['snk'])
    S.op('act', lambda e: e.activation(out=snk[:, :], in_=snk[:, :], func=AF.Exp), reads=['snk'], writes=['snk'])
    S.op('pool', lambda e: e.memset(vp[:, :, :, :], 1.0), writes=['vp'])
    for hh in range(10):
        blk = 4 + hh // 2
        w = (hh // 2) % 2
        if hh % 2 == 0:
            wload(ctx, wb[w][:, :, :], w_in[l][:, blk * 128:(blk + 1) * 128].rearrange("(k p) c -> p k c", p=128),
                  'wb%d' % w)
        for gi, (t0, n) in enumerate(TOKG):
            bank = 2 + gi % 3
            proj(ctx, wb[w], 'wb%d' % w, (hh % 2) * 64, 64, hT, t0, n, bank)
            S.op('act', lambda e, bank=bank, t0=t0, n=n: e.copy(out=qraw[:, t0:t0 + n], in_=PS[bank][0:64, 0:n]),
                 reads=['ps%d' % bank], writes=['qraw'])
            S.op('act', lambda e, hh=hh, t0=t0, n=n: e.copy(out=qk[:, hh, t0:t0 + n], in_=qraw[:, t0:t0 + n]),
                 reads=['qraw'], writes=['qk'])
            S.op('pe', lambda e, t0=t0, n=n: e.matmul(PS[5][0:16, 0:n], lhsT=perm[:, :], rhs=qraw[0:16, t0:t0 + n],
                                                      start=True, stop=True),
                 reads=['perm', 'qraw'], writes=['ps5'])
            S.op('dve', lambda e, t0=t0, n=n: e.tensor_tensor(out=rt1[:, 0:n], in0=qraw[0:16, t0:t0 + n],
                                                              in1=rope[:, 0, t0:t0 + n], op=ALU.mult),
                 reads=['qraw', 'rope'], writes=['rt1'])
            S.op('dve', lambda e, t0=t0, n=n: e.tensor_tensor(out=rt2[:, 0:n], in0=PS[5][0:16, 0:n],
                                                              in1=rope[:, 1, t0:t0 + n], op=ALU.mult),
                 reads=['ps5', 'rope'], writes=['rt2'])
            S.op('dve', lambda e, hh=hh, t0=t0, n=n: e.tensor_tensor(out=qk[0:16, hh, t0:t0 + n], in0=rt1[:, 0:n],
                                                                     in1=rt2[:, 0:n], op=ALU.add),
                 reads=['rt1', 'rt2', 'qk'], writes=['qk'])
    wload(ctx, wb[1][:, :, :], w_in[l][:, 9 * 128:10 * 128].rearrange("(k p) c -> p k c", p=128), 'wb1')
    for i in range(18):
        bank = 2 + i % 3
        for kc in range(16):
            S.op('pe', lambda e, kc=kc, i=i, bank=bank: e.matmul(PS[bank][:, 0:128], lhsT=hT[:, kc, i * 128:(i + 1) * 128],
                                                                   rhs=wb[1][:, kc, :], start=(kc == 0), stop=(kc == 15)),
                 reads=['wb1', 'hT'], writes=['ps%d' % bank])
        S.op('act', lambda e, i=i, bank=bank: e.copy(out=vp[:, i, :, 0:64],
                                                     in_=PS[bank][:, 0:128].rearrange("p (g d) -> p g d", g=2)),
             reads=['ps%d' % bank], writes=['vp'])
    for i in range(16):
        for g in range(2):
            eb = 3 * g
            for jj in range(3):
                bank = 2 + eb + jj
                S.op('pe', lambda e, i=i, g=g, jj=jj, bank=bank: e.matmul(
                    PS[bank][:, :].rearrange("p (h q) -> p h q", h=4),
                    lhsT=qk[:, 8 + g, (i + jj) * 128:(i + jj + 1) * 128],
                    rhs=qk[:, 4 * g:4 * g + 4, (i + 1) * 128:(i + 2) * 128], start=True, stop=True),
                    reads=['qk'], writes=['ps%d' % bank])
                S.op('act', lambda e, bank=bank, k=eb + jj: e.activation(out=E[k][:, :], in_=PS[bank][:, :],
                                                                       func=AF.Exp, scale=0.125),
                     reads=['ps%d' % bank], writes=['E%d' % (eb + jj)])
            mP = 0 if i == 0 else 1
            mN = 3 if i == 15 else 2
            S.op('pool', lambda e, k=eb, mP=mP: e.tensor_tensor(
                out=E[k][:, :].rearrange("p (h q) -> p h q", h=4), in0=E[k][:, :].rearrange("p (h q) -> p h q", h=4),
                in1=masks[:, mP, :].unsqueeze(1).to_broadcast([128, 4, 128]), op=ALU.mult),
                reads=['E%d' % eb, 'masks'], writes=['E%d' % eb])
            S.op('pool', lambda e, k=eb + 2, mN=mN: e.tensor_tensor(
                out=E[k][:, :].rearrange("p (h q) -> p h q", h=4), in0=E[k][:, :].rearrange("p (h q) -> p h q", h=4),
                in1=masks[:, mN, :].unsqueeze(1).to_broadcast([128, 4, 128]), op=ALU.mult),
                reads=['E%d' % (eb + 2), 'masks'], writes=['E%d' % (eb + 2)])
            for h4 in range(4):
                for jj in range(3):
                    S.op('pe', lambda e, i=i, g=g, jj=jj, h4=h4, k=eb + jj: e.matmul(
                        PS[0][:, h4 * 66:h4 * 66 + 65], lhsT=E[k][:, h4 * 128:(h4 + 1) * 128],
                        rhs=vp[:, i + jj, g, 0:65], start=(jj == 0), stop=(jj == 2)),
                        reads=['E%d' % (eb + jj), 'vp'], writes=['ps0'])
            pso = PS[0][:, 0:264].rearrange("p (h d) -> p h d", h=4)
            S.op('dve', lambda e, g=g, pso=pso: e.tensor_tensor(out=den[:, 0:4], in0=pso[:, :, 64],
                                                                in1=snk[:, 4 * g:4 * g + 4], op=ALU.add),
                 reads=['ps0', 'snk'], writes=['den'])
            S.op('dve', lambda e: e.reciprocal(out=den[:, 4:8], in_=den[:, 0:4]), reads=['den'], writes=['den'])
            S.op('dve', lambda e, g=g, pso=pso: e.tensor_tensor(
                out=Ot[:, g * 256:(g + 1) * 256].rearrange("p (h d) -> p h d", h=4), in0=pso[:, :, 0:64],
                in1=den[:, 4:8].unsqueeze(2).to_broadcast([128, 4, 64]), op=ALU.mult),
                reads=['ps0', 'den'], writes=['Ot'])
        pst = PS[1][:, :].bitcast(BF16)
        for cb in range(4):
            S.op('pe', lambda e, cb=cb, pst=pst: e.transpose(out=pst[:, cb * 128:(cb + 1) * 128],
                                                              in_=Ot[:, cb * 128:(cb + 1) * 128], identity=ident[:, :]),
                 reads=['Ot', 'ident'], writes=['ps1'])
        S.op('act', lambda e, i=i, pst=pst: e.copy(out=mixB[:, :, i * 128:(i + 1) * 128],
                                                   in_=pst[:, 0:512].rearrange("p (c q) -> p c q", c=4)),
             reads=['ps1'], writes=['mixB'])
    for cb in range(4):
        S.dma('sp', ctx['mixT_d'][512 + cb * 128:512 + (cb + 1) * 128, :], mixB[:, cb, :], reads=['mixB'],
              writes=['mixT_d'])
    S.barrier()
    AR.release()


def phase_ffn(ctx, l, xsrc, final):
    S, AR, PS, ins = ctx['S'], ctx['AR'], ctx['PS'], ctx['ins']
    G = 1024
    NT = G // 128
    AR.mark()
    ctx['sc_ss'] = AR.alloc([128, 4], F32, 'ss')
    ctx['sc_xn'] = AR.alloc([128, D], BF16, 'xn')
    ctx['sc_junk'] = ctx['sc_xn']
    ctx['junk_key'] = 'sc_xn'
    h2T = AR.alloc([128, 16, G], BF16, 'h2T')
    wg = [AR.alloc([128, 16, 128], BF16, 'wg%d' % i) for i in range(4)]
    sil = [AR.alloc([128, 512], F32, 'sil%d' % i) for i in range(2)]
    ctx['tp_bank'] = [0, 1]
    ctx['tp_i'] = 0
    w_out, w_gu, w_down = ins['w_out'], ins['w_gate_up'], ins['w_down']
    xdst = ctx['x1_ext']
    cnt = {'wo': 0, 'wg': 0, 'wd': 0, 'pb': 0, 'xb': 0}
    if final:
        fxt = [AR.alloc([128, D], F32, 'fxt%d' % i) for i in range(3)]
        gfin = AR.alloc([128, D], F32, 'gfin')
        fjunk = AR.alloc([128, D], BF16, 'fjunk')
        ssf = AR.alloc([128, 3, 4], F32, 'ssf')
        S.dma('sp', gfin[:, :], ins['final_norm_g'].rearrange("(o d) -> o d", o=1).to_broadcast([128, D]),
              writes=['gfin'])
        fcnt = {'n': 0}

        def final_tile(i):
            b = fcnt['n'] % 3
            fcnt['n'] += 1
            r0 = i * 128
            ss = ssf[:, b, :]
            S.dma('sp', fxt[b][:, :], xdst[HALO + r0:HALO + r0 + 128, :], reads=['x1_ext'], writes=['fxt%d' % b])
            S.op('act', lambda e, b=b, ss=ss: e.activation(out=fjunk[:, :], in_=fxt[b][:, :], func=AF.Square,
                                                           accum_out=ss[:, 0:1]),
                 reads=['fxt%d' % b], writes=['fjunk', 'ssf%d' % b])
            S.op('dve', lambda e, ss=ss: e.tensor_scalar(out=ss[:, 1:2], in0=ss[:, 0:1], scalar1=1.0 / D, scalar2=EPS,
                                                         op0=ALU.mult, op1=ALU.add), reads=['ssf%d' % b],
                 writes=['ssf%d' % b])
            S.op('act', lambda e, ss=ss: e.activation(out=ss[:, 3:4], in_=ss[:, 1:2], func=AF.Sqrt),
                 reads=['ssf%d' % b], writes=['ssf%d' % b])
            S.op('dve', lambda e, ss=ss: e.reciprocal(out=ss[:, 2:3], in_=ss[:, 3:4]), reads=['ssf%d' % b],
                 writes=['ssf%d' % b])
            S.op('dve', lambda e, b=b, ss=ss: e.scalar_tensor_tensor(out=fxt[b][:, :], in0=fxt[b][:, :],
                                                                     scalar=ss[:, 2:3], in1=gfin[:, :], op0=ALU.mult,
                                                                     op1=ALU.mult),
                 reads=['fxt%d' % b, 'ssf%d' % b, 'gfin'], writes=['fxt%d' % b])
            S.dma('act', ctx['y_out'][r0:r0 + 128, :], fxt[b][:, :], reads=['fxt%d' % b], writes=['y_out'])
        fpending = []
    AR.mark()
    mixg = AR.alloc([128, 16, 512], BF16, 'mixg')
    wo = [AR.alloc([128, 16, 512], BF16, 'wo%d' % i) for i in range(2)]
    xm = [AR.alloc([128, D], F32, 'xm%d' % i) for i in range(4)]
    AR.release()
    AR.mark()
    actT = AR.alloc([128, NFF, G], BF16, 'actT')
    wd = [AR.alloc([128, 4, 512], BF16, 'wd%d' % i) for i in range(3)]
    xb = [AR.alloc([128, 512], F32, 'xb%d' % i) for i in range(4)]
    for gi in range(T // G):
        for sg in range(G // 512):
            tok0 = gi * G + sg * 512
            S.dma('sp', mixg[:, :, :], ctx['mixT_d'][:, tok0:tok0 + 512].rearrange("(k p) t -> p k t", p=128),
                  reads=['mixT_d'], writes=['mixg'])
            for t in range(4):
                r0 = HALO + tok0 + t * 128
                S.dma('sp', xm[t][:, :], xsrc[r0:r0 + 128, :], writes=['xm%d' % t])
            for cg in range(4):
                w = cnt['wo'] % 2
                cnt['wo'] += 1
                wload(ctx, wo[w][:, :, :], w_out[l][:, cg * 512:(cg + 1) * 512].rearrange("(k p) c -> p k c", p=128),
                      'wo%d' % w)
                for t in range(4):
                    bank = 2 + cnt['pb'] % 4
                    cnt['pb'] += 1
                    for kc in range(16):
                        S.op('pe', lambda e, kc=kc, t=t, w=w, bank=bank: e.matmul(
                            PS[bank][:, :], lhsT=mixg[:, kc, t * 128:(t + 1) * 128], rhs=wo[w][:, kc, :],
                            start=(kc == 0), stop=(kc == 15)), reads=['mixg', 'wo%d' % w], writes=['ps%d' % bank])
                    S.op('dve', lambda e, t=t, cg=cg, bank=bank: e.tensor_tensor(
                        out=xm[t][:, cg * 512:(cg + 1) * 512], in0=PS[bank][:, :],
                        in1=xm[t][:, cg * 512:(cg + 1) * 512], op=ALU.add),
                        reads=['ps%d' % bank, 'xm%d' % t], writes=['xm%d' % t])
            for t in range(4):
                rms_to_T(ctx, xm[t], ctx['gcolF'][:, l, :], h2T, sg * 512 + t * 128, 'xm%d' % t)
                S.dma('act', ctx['xmid'][tok0 + t * 128:tok0 + (t + 1) * 128, :], xm[t][:, :], reads=['xm%d' % t],
                      writes=['xmid'])
        S.barrier()
        for j in range(NFF):
            if final and fpending and j % 5 == 2:
                final_tile(fpending.pop(0))
            w = (cnt['wg'] % 2) * 2
            cnt['wg'] += 1
            wload(ctx, wg[w][:, :, :], w_gu[l][:, j * 128:(j + 1) * 128].rearrange("(k p) c -> p k c", p=128),
                  'wg%d' % w)
            wload(ctx, wg[w + 1][:, :, :],
                  w_gu[l][:, DFF + j * 128:DFF + (j + 1) * 128].rearrange("(k p) c -> p k c", p=128), 'wg%d' % (w + 1))
            for hf in range(G // 512):
                pb = cnt['pb'] % 4
                cnt['pb'] += 1
                ba, bb = 2 * pb, 2 * pb + 1
                for kc in range(16):
                    S.op('pe', lambda e, kc=kc, w=w, ba=ba, hf=hf: e.matmul(
                        PS[ba][:, :], lhsT=wg[w][:, kc, :], rhs=h2T[:, kc, hf * 512:(hf + 1) * 512],
                        start=(kc == 0), stop=(kc == 15)), reads=['wg%d' % w, 'hT'], writes=['ps%d' % ba])
                for kc in range(16):
                    S.op('pe', lambda e, kc=kc, w=w, bb=bb, hf=hf: e.matmul(
                        PS[bb][:, :], lhsT=wg[w + 1][:, kc, :], rhs=h2T[:, kc, hf * 512:(hf + 1) * 512],
                        start=(kc == 0), stop=(kc == 15)), reads=['wg%d' % (w + 1), 'hT'], writes=['ps%d' % bb])
                si = cnt['pb'] % 2
                S.op('act', lambda e, ba=ba, si=si: e.activation(out=sil[si][:, :], in_=PS[ba][:, :], func=AF.Silu),
                     reads=['ps%d' % ba], writes=['sil%d' % si])
                S.op('dve', lambda e, bb=bb, si=si, j=j, hf=hf: e.tensor_tensor(
                    out=actT[:, j, hf * 512:(hf + 1) * 512], in0=sil[si][:, :], in1=PS[bb][:, :], op=ALU.mult),
                    reads=['ps%d' % bb, 'sil%d' % si], writes=['actT'])
        for cg in range(4):
            for k4 in range(NFF // 4):
                w = cnt['wd'] % 3
                cnt['wd'] += 1
                wload(ctx, wd[w][:, :, :],
                      w_down[l][k4 * 512:(k4 + 1) * 512, cg * 512:(cg + 1) * 512].rearrange("(k p) c -> p k c", p=128),
                      'wd%d' % w)
                for kk in range(4):
                    k = k4 * 4 + kk
                    for t in range(NT):
                        S.op('pe', lambda e, k=k, kk=kk, t=t, w=w: e.matmul(
                            PS[t][:, :], lhsT=actT[:, k, t * 128:(t + 1) * 128], rhs=wd[w][:, kk, :],
                            start=(k == 0), stop=(k == NFF - 1)),
                            reads=['actT', 'wd%d' % w], writes=['ps%d' % t])
            for t in range(NT):
                r0 = gi * G + t * 128
                b = cnt['xb'] % 4
                cnt['xb'] += 1
                S.dma('sp', xb[b][:, :], ctx['xmid'][r0:r0 + 128, cg * 512:(cg + 1) * 512], reads=['xmid'],
                      writes=['xb%d' % b])
                S.op('dve', lambda e, t=t, b=b: e.tensor_tensor(out=xb[b][:, :], in0=PS[t][:, :], in1=xb[b][:, :],
                                                                op=ALU.add),
                     reads=['ps%d' % t, 'xb%d' % b], writes=['xb%d' % b])
                S.dma('act', xdst[HALO + r0:HALO + r0 + 128, cg * 512:(cg + 1) * 512], xb[b][:, :],
                      reads=['xb%d' % b], writes=['x1_ext'])
                if not final and r0 == 0:
                    S.dma('act', ctx['edge_in'][0:128, cg * 512:(cg + 1) * 512], xb[b][:, :], reads=['xb%d' % b],
                          writes=['edge_in'])
                if not final and r0 == T - 128:
                    S.dma('act', ctx['edge_in'][128:256, cg * 512:(cg + 1) * 512], xb[b][:, :], reads=['xb%d' % b],
                          writes=['edge_in'])
        S.barrier()
        if final:
            fpending.extend(range(gi * NT, (gi + 1) * NT))
    if final:
        for i in fpending:
            final_tile(i)
        S.barrier()
    AR.release()
    AR.release()


def halo_start(ctx):
    S = ctx['S']
    S.allgather(ctx['edge_in'][:, :], ctx['edge_all'][:, :], reads=['edge_in'], writes=['edge_all'])
    S.sticky.add('edge_all')


def halo_finish(ctx):
    S, AR, ins = ctx['S'], ctx['AR'], ctx['ins']
    AR.mark()
    hb = [AR.alloc([128, D], F32, 'hb%d' % i) for i in range(2)]
    fl = AR.alloc([128, 2], F32, 'fl')
    S.dma('sp', fl[:, :], ins['t_flags'][:, :], writes=['fl'])
    S.dma('sp', hb[0][:, :], ctx['edge_all'][128:256, :], reads=['edge_all'], writes=['hb0'])
    S.dma('sp', hb[1][:, :], ctx['edge_all'][256:384, :], reads=['edge_all'], writes=['hb1'])
    for i in range(2):
        S.op('dve', lambda e, i=i: e.tensor_scalar(out=hb[i][:, :], in0=hb[i][:, :], scalar1=fl[:, i:i + 1],
                                                   scalar2=None, op0=ALU.mult), reads=['hb%d' % i, 'fl'],
             writes=['hb%d' % i])
    S.dma('sp', ctx['x1_ext'][0:HALO, :], hb[0][:, :], reads=['hb0'], writes=['x1_ext'])
    S.dma('sp', ctx['x1_ext'][HALO + T:TE, :], hb[1][:, :], reads=['hb1'], writes=['x1_ext'])
    S.sticky.discard('edge_all')
    S.barrier()
    AR.release()


def core_info(c):
    if c < 4:
        return dict(kind='p', seq=c // 2, pos0=2048 * (c % 2), L=4096, rank=c % 2)
    return dict(kind='s', seq=c - 4, pos0=0, L=2048, rank=c % 2)


def host_tables(c):
    ci = core_info(c)
    pos0, L, rank = ci['pos0'], ci['L'], ci['rank']
    bf = ml_dtypes.bfloat16
    tb = {}
    tb['t_ident'] = np.eye(128, dtype=np.float32).astype(bf)
    pos = (pos0 - HALO + np.arange(TE)).astype(np.float32)
    inv_freq = (np.float32(500000.0) ** (-np.arange(0, 16, 2, dtype=np.float32) / np.float32(16))).astype(np.float32)
    ang = (pos[:, None] * inv_freq[None, :]).astype(np.float32)
    ang = np.concatenate([ang, ang], axis=1).T
    cs = np.cos(ang).astype(np.float32)
    sn = np.sin(ang).astype(np.float32)
    sn[:8] *= -1.0
    tb['t_rope'] = np.stack([cs, sn]).astype(np.float32)
    pm = np.zeros((16, 16), np.float32)
    for m in range(16):
        pm[(m + 8) % 16, m] = 1.0
    tb['t_perm'] = pm
    kk = np.arange(128)[:, None]
    qq = np.arange(128)[None, :]
    lv = 1.0 if pos0 > 0 else 0.0
    rv = 1.0 if pos0 + T < L else 0.0
    mP = (kk >= qq).astype(np.float32)
    mN = (kk <= qq).astype(np.float32)
    tb['t_mask'] = np.stack([mP * lv, mP, mN, mN * rv]).astype(bf)
    t = pos0 + np.arange(T)
    ic = np.zeros((4, T), np.float32)
    for g, w in enumerate((2, 4, 8, 16)):
        lo = np.clip(t - w // 2, 0, L)
        hi = np.clip(t + w // 2, 0, L)
        ic[g] = (1.0 / (hi - lo).astype(np.float32)).astype(np.float32)
    tb['t_invcnt'] = ic
    fl = np.zeros((128, 2), np.float32)
    fl[:, 0] = lv
    fl[:, 1] = rv
    tb['t_flags'] = fl
    tb.update(fft_tables(c))
    return tb


def fft_tables(c):
    ci = core_info(c)
    pos0h = 2048 * ci['rank']
    L, kind, rank = ci['L'], ci['kind'], ci['rank']
    bf = ml_dtypes.bfloat16
    tb = {}
    N = NFFT
    NA = 33
    fa = np.arange(NA)
    sB = np.arange(64)
    ph = -2.0 * np.pi * np.outer(sB, fa) / 64.0
    w1 = np.concatenate([np.cos(ph), np.sin(ph)], axis=1)
    tb['t_w1f'] = w1.astype(np.float32).astype(bf)
    w1d = w1[:32].copy()
    if kind == 's':
        if rank == 0:
            w1d[16:32] = 0.0
        else:
            w1d[0:16] = 0.0
    tb['t_w1d'] = w1d.astype(np.float32).astype(bf)
    sA = np.arange(128)
    fb = np.arange(128)
    tm = np.zeros((5, 128, 3, 8, 128), np.float32)
    for g in range(5):
        for j in range(8):
            if g * 8 + j >= NA:
                continue
            f = (g * 8 + j) + 64 * fb
            ph = -2.0 * np.pi * np.outer(sA, f) / N
            tm[g, :, 0, j, :] = np.cos(ph)
            tm[g, :, 1, j, :] = np.sin(ph)
            tm[g, :, 2, j, :] = -np.sin(ph)
    tb['t_m'] = tm.reshape(5, 128, 3 * 8 * 128).astype(bf)
    ph = 2.0 * np.pi * np.outer(fb, np.arange(128)) / 128.0
    tb['t_g'] = np.concatenate([np.cos(ph), np.sin(ph), -np.sin(ph)], axis=1).astype(np.float32).astype(bf)
    tA = np.arange(128)[:, None]
    tB = np.arange(16)[None, :]
    tt = (pos0h + 128 * tB + tA).reshape(-1)
    ph = 2.0 * np.pi * np.outer(fa, tt) / N
    wgt = np.where((fa == 0) | (fa == 32), 1.0, 2.0)[:, None]
    tb['t_h'] = np.concatenate([wgt * np.cos(ph), -wgt * np.sin(ph)], axis=0).astype(np.float32).astype(bf)
    s = np.arange(N)
    idx = np.where(s < N // 2, s, N - s)
    valid = np.where(s < N // 2, s < L, (N - s) <= L - 1)
    tl = np.linspace(0.0, 1.0, L, dtype=np.float32)
    idc = np.clip(idx, 0, L - 1)
    tt = tl[idc]
    bands = np.linspace(1e-4, 15.0, 16, dtype=np.float32)[None, :]
    wpos = (np.float32(2.0 * math.pi / L) * idc.astype(np.float32))[:, None]
    z = np.concatenate([tt[:, None], np.cos(bands * wpos), -np.sin(bands * wpos)], axis=1).astype(np.float32)
    z = np.where(valid[:, None], z, 0.0).astype(np.float32)
    tb['t_pos'] = np.ascontiguousarray(z.T)
    tb['t_tdec'] = np.where(valid, tt, 1.0e4).astype(np.float32)[None, :]
    return tb


_CACHE = {}


def kernel(**inputs):
    xp = np.asarray(inputs['x_prompt'], np.float32)
    xs = np.asarray(inputs['x_sample'], np.float32)
    if 'nc' not in _CACHE:
        nc, S, ctx, es = build()
        full_program(ctx)
        S.emit()
        _CACHE['nc'] = nc
        _CACHE['tabs'] = [host_tables(c) for c in range(8)]
    nc = _CACHE['nc']
    wnames = ["mix_norm_g", "w_in", "pool_w", "pool_scale", "attn_sink", "sconv_w", "hy_short_w", "hy_w1", "hy_b1",
              "hy_w2", "hy_b2", "hy_freq", "hy_bias", "w_out", "ffn_norm_g", "w_gate_up",
              "w_down", "final_norm_g"]
    w3f = np.asarray(inputs["hy_w3"], np.float32).reshape(DEPTH, 64, 4, 4, 128)
    dcf = np.asarray(inputs["hy_decay"], np.float32).reshape(DEPTH, 4, 4, 128)
    shared = {k: np.ascontiguousarray(np.asarray(inputs[k], np.float32)) for k in wnames}
    in_maps = []
    for c in range(8):
        ci = core_info(c)
        seq = xp[ci['seq']] if ci['kind'] == 'p' else xs[ci['seq']]
        xe = np.zeros((TE, D), np.float32)
        lo = ci['pos0'] - HALO
        hi = ci['pos0'] + T + HALO
        a, b = max(lo, 0), min(hi, ci['L'])
        xe[a - lo:b - lo] = seq[a:b]
        m = dict(shared)
        m['x_ext'] = xe
        m['hy_w3_sh'] = np.ascontiguousarray(w3f[:, :, :, c % 4, :]).reshape(DEPTH, 64, 512)
        m['hy_decay_sh'] = np.ascontiguousarray(dcf[:, :, c % 4, :]).reshape(DEPTH, 512)
        m.update(_CACHE['tabs'][c])
        in_maps.append(m)
    res = run_bass_kernel_spmd(nc, in_maps, core_ids=list(range(8)))
    _CACHE['last'] = res
    yp = np.zeros((2, 4096, D), np.float32)
    ys = np.zeros((4, 2048, D), np.float32)
    for c in range(8):
        ci = core_info(c)
        y = np.asarray(res.results[c]['y_out'], np.float32)
        if ci['kind'] == 'p':
            yp[ci['seq'], ci['pos0']:ci['pos0'] + T] = y
        else:
            ys[ci['seq']] = y
    return (yp, ys)


SKIP_HYENA = False


def full_program(ctx):
    S, AR = ctx['S'], ctx['AR']
    if not SKIP_HYENA:
        phase_filters(ctx)
    for l in range(DEPTH):
        xsrc = ctx['x_ext'] if l == 0 else ctx['x1_ext']
        AR.mark()
        ctx['hT'] = AR.alloc([128, 16, TE], BF16, 'hT')
        if l == 0:
            phase_norm(ctx, l, xsrc)
        else:
            phase_norm(ctx, l, xsrc, tiles=list(range(1, 17)))
            halo_finish(ctx)
            phase_norm(ctx, l, xsrc, tiles=[0, 17])
        phase_local(ctx, l)
        phase_attn(ctx, l)
        AR.release()
        if SKIP_HYENA:
            AR.mark()
            zt = AR.alloc([128, T], BF16, 'zt')
            S.op('pool', lambda e: e.memset(zt[:, :], 0.0), writes=['zt'])
            for c in range(4):
                S.dma('sp', ctx['mixT_d'][1536 + c * 128:1536 + (c + 1) * 128, :], zt[:, :], reads=['zt'],
                      writes=['mixT_d'])
            S.barrier()
            AR.release()
        else:
            phase_hyena(ctx, l)
        phase_ffn(ctx, l, xsrc, final=(l == DEPTH - 1))
        if l == 0:
            halo_start(ctx)


NA = 33
FGROUPS = [(0, 8), (8, 8), (16, 8), (24, 8), (32, 1)]


def hy_alloc(ctx):
    AR = ctx['AR']
    h = {}
    h['InB'] = AR.alloc([64, 128 * 128], BF16, 'InB')
    h['A'] = AR.alloc([128, 2, NA, 128], BF16, 'A')
    h['Mt'] = [AR.alloc([128, 3, 8, 128], BF16, 'Mt%d' % i) for i in range(2)]
    h['w1d'] = AR.alloc([32, 2 * NA], BF16, 'w1d')
    h['w1f'] = AR.alloc([64, 2 * NA], BF16, 'w1f')
    S, ins = ctx['S'], ctx['ins']
    S.dma('sp', h['w1d'][:, :], ins['t_w1d'][:, :], writes=['w1d'])
    S.dma('sp', h['w1f'][:, :], ins['t_w1f'][:, :], writes=['w1f'])
    return h


def fwd_transform(ctx, h, K, w1, w1key, on_group, after_stage1=None, pre_stage2=None):
    S, PS, ins = ctx['S'], ctx['PS'], ctx['ins']
    In, A = h['InB'], h['A']

    def load_mt(g):
        m = g % 2
        S.dma('sp', h['Mt'][m][:, :, :, :], ins['t_m'][g].rearrange("p (a j f) -> p a j f", a=3, j=8),
              writes=['Mt%d' % m])
    load_mt(0)
    load_mt(1)
    if pre_stage2 is not None:
        pre_stage2()
    for c4 in range(32):
        bank = c4 % 2
        for cc in range(4):
            c = c4 * 4 + cc
            S.op('pe', lambda e, c=c, cc=cc, bank=bank: e.matmul(PS[bank][:, cc * 128:cc * 128 + 2 * NA],
                                                               lhsT=In[0:K, c * 128:(c + 1) * 128], rhs=w1[0:K, :],
                                                               start=True, stop=True),
                 reads=['InB', w1key], writes=['ps%d' % bank])
        src = sap(PS[bank], 0, 128, [(NA, 2), (1, NA), (128, 4)])
        S.op('act', lambda e, c4=c4, src=src: e.copy(out=A[:, :, :, c4 * 4:(c4 + 1) * 4], in_=src),
             reads=['ps%d' % bank], writes=['A'])
    if after_stage1 is not None:
        after_stage1()
    for g, (fa0, nj) in enumerate(FGROUPS):
        m = g % 2
        Mt = h['Mt'][m]
        b0 = 4 * (g % 2)
        for j in range(nj):
            fa = fa0 + j
            bre = b0 + j // 4
            bim = b0 + 2 + j // 4
            col = (j % 4) * 128
            for (bank, la, lb) in ((bre, 0, 2), (bim, 1, 0)):
                S.op('pe', lambda e, bank=bank, la=la, j=j, fa=fa, col=col, Mt=Mt: e.matmul(
                    PS[bank][:, col:col + 128], lhsT=Mt[:, la, j, :], rhs=A[:, 0, fa, :], start=True, stop=False),
                    reads=['Mt%d' % m, 'A'], writes=['ps%d' % bank])
                S.op('pe', lambda e, bank=bank, lb=lb, j=j, fa=fa, col=col, Mt=Mt: e.matmul(
                    PS[bank][:, col:col + 128], lhsT=Mt[:, lb, j, :], rhs=A[:, 1, fa, :], start=False, stop=True),
                    reads=['Mt%d' % m, 'A'], writes=['ps%d' % bank])
        if g + 2 < len(FGROUPS):
            load_mt(g + 2)
        on_group(g, b0, fa0, nj)


def kf_row(l, o, cc, g):
    base = (l * 2 + o) * 2560
    if g < 4:
        return base + (g // 2) * 1024 + cc * 256 + (g % 2) * 128
    return base + 2048 + cc * 128


def phase_filters(ctx):
    S, AR, PS, ins = ctx['S'], ctx['AR'], ctx['PS'], ctx['ins']
    AR.mark()
    h = hy_alloc(ctx)
    h2T = [AR.alloc([64, NFFT], BF16, 'h2T%d' % i) for i in range(2)]
    tdb = AR.alloc([128, NFFT], F32, 'tdb')
    pos = [AR.alloc([33, 512], F32, 'pos%d' % i) for i in range(2)]
    w1 = [AR.alloc([33, 64], F32, 'fw1%d' % i) for i in range(2)]
    w2 = [AR.alloc([64, 64], F32, 'fw2%d' % i) for i in range(2)]
    w3 = [AR.alloc([64, 512], BF16, 'fw3%d' % i) for i in range(2)]
    prm = AR.alloc([64, 2, 8], F32, 'prm')
    negd = AR.alloc([128, 2, 8], F32, 'negd')
    arg = [AR.alloc([64, 512], F32, 'arg%d' % i) for i in range(2)]
    argi = [AR.alloc([64, 512], mybir.dt.int32, 'argi%d' % i) for i in range(2)]
    argf = [AR.alloc([64, 512], F32, 'argf%d' % i) for i in range(2)]
    h1 = [AR.alloc([64, 512], F32, 'h1%d' % i) for i in range(2)]
    ex = [AR.alloc([128, 512], F32, 'ex%d' % i) for i in range(2)]
    kt = [AR.alloc([128, 512], F32, 'kt%d' % i) for i in range(2)]
    kk = AR.alloc([128, NFFT], BF16, 'kk')
    acc = AR.alloc([128, 4, 20], F32, 'acc')
    kfo = [AR.alloc([128, 2, 8, 128], BF16, 'kfo%d' % i) for i in range(2)]
    PI = math.pi
    G4 = [[0, 1, 2, 3], [4, 5, 6, 7]]
    for k in range(2):
        S.op('pool', lambda e, k=k: e.memset(kfo[k][:, :, :, :], 0.0), writes=['kfo%d' % k])
    for l in range(DEPTH):
        S.dma('sp', w1[l][:, :], ins['hy_w1'][l], writes=['fw%d' % l])
        S.dma('sp', w2[l][:, :], ins['hy_w2'][l], writes=['fw%d' % l])
        wload(ctx, w3[l][:, :], ins['hy_w3_sh'][l], 'fw3%d' % l)
        S.dma('sp', prm[:, l, 0:1], ins['hy_freq'][l].rearrange("(p o) -> p o", o=1), writes=['prm%d' % l])
        S.dma('sp', prm[:, l, 1:2], ins['hy_b1'][l].rearrange("(p o) -> p o", o=1), writes=['prm%d' % l])
        S.dma('sp', prm[:, l, 2:3], ins['hy_b2'][l].rearrange("(p o) -> p o", o=1), writes=['prm%d' % l])
        S.dma('sp', negd[:, l, 0:4], ins['hy_decay_sh'][l].rearrange("(q p) -> p q", p=128), writes=['negd%d' % l],
              allow_slow_non_contiguous=True)
    S.dma('sp', tdb[:, :], ins['t_tdec'][0:1, :].to_broadcast([128, NFFT]), writes=['tdb'])
    for l in range(DEPTH):
        S.op('dve', lambda e, l=l: e.tensor_scalar(out=prm[:, l, 3:4], in0=prm[:, l, 0:1], scalar1=1.0 / (2.0 * PI),
                                                   scalar2=None, op0=ALU.mult), reads=['prm%d' % l],
             writes=['prm%d' % l])
        S.op('dve', lambda e, l=l: e.tensor_tensor(out=prm[:, l, 4:6], in0=prm[:, l, 1:3],
                                                   in1=prm[:, l, 3:4].to_broadcast([64, 2]), op=ALU.mult),
             reads=['prm%d' % l], writes=['prm%d' % l])
        S.op('act', lambda e, l=l: e.activation(out=negd[:, l, 4:8], in_=negd[:, l, 0:4], func=AF.Abs),
             reads=['negd%d' % l], writes=['negd%d' % l])
        S.op('dve', lambda e, l=l: e.tensor_scalar(out=negd[:, l, 0:4], in0=negd[:, l, 4:8], scalar1=-1.0,
                                                   scalar2=None, op0=ALU.mult), reads=['negd%d' % l],
             writes=['negd%d' % l])
    for sb in range(16):
        pb = sb % 2
        S.dma('sp', pos[pb][:, :], ins['t_pos'][:, sb * 512:(sb + 1) * 512], writes=['pos%d' % pb])
        for stage in range(2):
            for l in range(DEPTH):
                b = l
                if stage == 0:
                    src, skey, wt, kq = pos[pb], 'pos%d' % pb, w1[l], 33
                else:
                    src, skey, wt, kq = h1[b], 'h1%d' % b, w2[l], 64
                bank = 6 + b
                a, ai, af = arg[b], argi[b], argf[b]
                S.op('pe', lambda e, src=src, wt=wt, bank=bank, kq=kq: e.matmul(
                    PS[bank][0:64, :], lhsT=wt[0:kq, :], rhs=src[0:kq, :], start=True, stop=True),
                    reads=['fw%d' % l, skey], writes=['ps%d' % bank])
                S.op('act', lambda e, stage=stage, a=a, bank=bank, l=l: e.activation(
                    out=a[:, :], in_=PS[bank][0:64, :], func=AF.Identity, scale=prm[:, l, 3:4],
                    bias=prm[:, l, 4 + stage:5 + stage]),
                    reads=['ps%d' % bank, 'prm%d' % l], writes=['arg%d' % b])
                S.op('dve', lambda e, a=a, ai=ai: e.tensor_copy(out=ai[:, :], in_=a[:, :]), reads=['arg%d' % b],
                     writes=['argi%d' % b])
                S.op('dve', lambda e, ai=ai, af=af: e.tensor_copy(out=af[:, :], in_=ai[:, :]), reads=['argi%d' % b],
                     writes=['argf%d' % b])
                S.op('dve', lambda e, a=a, af=af: e.tensor_tensor(out=a[:, :], in0=a[:, :], in1=af[:, :],
                                                                  op=ALU.subtract),
                     reads=['arg%d' % b, 'argf%d' % b], writes=['arg%d' % b])
                if stage == 0:
                    S.op('act', lambda e, a=a, b=b: e.activation(out=h1[b][:, :], in_=a[:, :], func=AF.Sin,
                                                                 scale=2.0 * PI * (1.0 - 1e-6)),
                         reads=['arg%d' % b], writes=['h1%d' % b])
                else:
                    S.op('act', lambda e, a=a, sb=sb, l=l: e.activation(
                        out=h2T[l][:, sb * 512:(sb + 1) * 512], in_=a[:, :], func=AF.Sin,
                        scale=2.0 * PI * (1.0 - 1e-6)), reads=['arg%d' % b], writes=['h2T%d' % l])
    kl = ctx['kf_loc']
    kd = ctx['kf_d']
    it = 0
    for l in range(DEPTH):
        for o in range(2):
            kb = it % 2
            it += 1
            for sb in range(16):
                dr = 0 if sb < 8 else 1
                q = o * 2 + dr
                t = sb % 2
                bank = 4 + t
                S.op('pe', lambda e, q=q, sb=sb, bank=bank, l=l: e.matmul(
                    PS[bank][:, :], lhsT=w3[l][:, q * 128:(q + 1) * 128], rhs=h2T[l][:, sb * 512:(sb + 1) * 512],
                    start=True, stop=True), reads=['fw3%d' % l, 'h2T%d' % l], writes=['ps%d' % bank])
                S.op('act', lambda e, t=t, sb=sb, q=q, l=l: e.activation(
                    out=ex[t][:, :], in_=tdb[:, sb * 512:(sb + 1) * 512], func=AF.Exp, scale=negd[:, l, q:q + 1]),
                    reads=['tdb', 'negd%d' % l], writes=['ex%d' % t])
                S.op('dve', lambda e, t=t, bank=bank: e.tensor_tensor(out=kt[t][:, :], in0=PS[bank][:, :],
                                                                      in1=ex[t][:, :], op=ALU.mult),
                     reads=['ps%d' % bank, 'ex%d' % t], writes=['kt%d' % t])
                S.op('dve', lambda e, sb=sb, t=t, aq=l * 2 + o: e.tensor_reduce(
                    out=acc[:, aq, sb:sb + 1], in_=kt[t][:, :], axis=AX.X, op=ALU.add,
                    apply_absolute_value=True), reads=['kt%d' % t], writes=['acc'])
                S.op('act', lambda e, sb=sb, t=t: e.copy(out=kk[:, sb * 512:(sb + 1) * 512], in_=kt[t][:, :]),
                     reads=['kt%d' % t], writes=['kk'])
            ai_ = l * 2 + o
            S.op('dve', lambda e, ai_=ai_: e.tensor_reduce(out=acc[:, ai_, 16:17], in_=acc[:, ai_, 0:16], axis=AX.X,
                                                           op=ALU.add), reads=['acc'], writes=['acc'])
            S.op('dve', lambda e, ai_=ai_: e.tensor_scalar(out=acc[:, ai_, 17:18], in0=acc[:, ai_, 16:17],
                                                           scalar1=float(NFFT), scalar2=None, op0=ALU.mult),
                 reads=['acc'], writes=['acc'])
            S.op('dve', lambda e, ai_=ai_: e.reciprocal(out=acc[:, ai_, 18:19], in_=acc[:, ai_, 17:18]),
                 reads=['acc'], writes=['acc'])
            S.dma('sp', ctx['rn_loc'][ai_].rearrange("(p o) -> p o", o=1), acc[:, ai_, 18:19], reads=['acc'],
                  writes=['rn_loc'])
            S.dma('act', ctx['kk_d'][kb], kk[:, :], reads=['kk'], writes=['kk_d%d' % kb])
            S.dma('sp', h['InB'][0:64, :].rearrange("p (c s) -> p c s", s=128),
                  ctx['kk_d'][kb].rearrange("c (t s) -> t c s", s=128), reads=['kk_d%d' % kb], writes=['InB'])

            def store(g, b0, fa0, nj, l=l, o=o):
                k = g % 2
                for bi in range(4):
                    r, jh = bi // 2, bi % 2
                    n = min(4, nj - jh * 4)
                    if n <= 0:
                        continue
                    S.op('act', lambda e, bi=bi, r=r, jh=jh, k=k, n=n: e.copy(
                        out=kfo[k][:, r, jh * 4:jh * 4 + n, :],
                        in_=PS[b0 + bi][:, 0:n * 128].rearrange("p (j c) -> p j c", j=n)),
                        reads=['ps%d' % (b0 + bi)], writes=['kfo%d' % k])
                r0 = ((l * 2 + o) * 5 + g) * 128
                S.dma('act', kl[r0:r0 + 128, :].rearrange("p (r j c) -> p r j c", r=2, j=8), kfo[k][:, :, :, :],
                      reads=['kfo%d' % k], writes=['kf_loc'])
            fwd_transform(ctx, h, 64, h['w1f'], 'w1f', store)
            lb = (l * 2 + o) * 640
            db = (l * 2 + o) * 2560
            for (lo_, n_, do_) in ((0, 256, 0), (256, 256, 1024), (512, 128, 2048)):
                def gath(a=lb + lo_, n_=n_, d=db + do_):
                    S.allgather(kl[a:a + n_, :], kd[d:d + 4 * n_, :], reads=['kf_loc'], writes=['kf_d'],
                                groups=G4, big=True)
                if l == 0:
                    gath()
                else:
                    ctx.setdefault('deferred_gathers', []).append(gath)
    S.allgather(ctx['rn_loc'][:, :], ctx['rn_d'][:, :], reads=['rn_loc'], writes=['rn_d'], groups=G4, big=True)
    S.sticky.update(['kf_d', 'rn_d'])
    S.barrier()
    AR.release()


def phase_hyena(ctx, l):
    S, AR, PS, ins = ctx['S'], ctx['AR'], ctx['PS'], ctx['ins']
    AR.mark()
    h = hy_alloc(ctx)
    InB = h['InB']
    NS = 2 * NA
    Y1 = AR.alloc([128, 128, NS], BF16, 'Y1')
    Y2 = AR.alloc([128, 128, NS], BF16, 'Y2')
    kfg = [AR.alloc([128, 2, 8, 128], BF16, 'kfg%d' % i) for i in range(2)]
    tmp = [AR.alloc([128, 512], F32, 'tmp%d' % i) for i in range(4)]
    G = AR.alloc([128, 3 * 128], BF16, 'G')
    H = AR.alloc([NS, 2048], BF16, 'H')
    rn = AR.alloc([128, 16], F32, 'rn')
    bia = AR.alloc([128, 2, 4], F32, 'bia')
    Bb = AR.alloc([NS, 128 * 128], BF16, 'Bb')
    gt = [AR.alloc([128, T], BF16, 'gt%d' % i) for i in range(2)]
    vt = [AR.alloc([128, T], BF16, 'vt%d' % i) for i in range(2)]
    bv = AR.alloc([128, T], F32, 'bv')
    yt = AR.alloc([128, T], F32, 'yt')
    ut = AR.alloc([128, T], BF16, 'ut')
    S.dma('sp', G[:, :], ins['t_g'][:, :], writes=['G'])
    S.dma('sp', H[:, :], ins['t_h'][:, :], writes=['H'])
    S.dma('sp', rn[:, :], ctx['rn_d'].rearrange("i p -> p i"), reads=['rn_d'], writes=['rn'],
          allow_slow_non_contiguous=True)
    for o in range(2):
        S.dma('sp', bia[:, o, :], ins['hy_bias'][l][o].rearrange("(c p) -> p c", p=128), writes=['bia'],
              allow_slow_non_contiguous=True)
    it = 0
    for o in range(2):
        src_all, src_own, skey_all, skey_own = ((ctx['hv_all'], ctx['hv_in'], 'hv_all', 'hv_in') if o == 0 else
                                                (ctx['hu_all'], ctx['hu_in'], 'hu_all', 'hu_in'))

        def load_in(cc, src_all=src_all, skey_all=skey_all):
            for r in range(2):
                S.dma('sp', InB[16 * r:16 * r + 16, :], src_all[r * 64 + cc * 16:r * 64 + (cc + 1) * 16, :],
                      reads=[skey_all], writes=['InB'])
        load_in(0)
        for cc in range(4):
            ib = it % 2
            it += 1
            S.dma('sp', gt[ib][:, :], ctx['g12_d'][o, cc * 128:(cc + 1) * 128, :], reads=['g12_d'],
                  writes=['gt%d' % ib])
            S.dma('sp', vt[ib][:, :].rearrange("c (t s) -> c t s", s=128),
                  src_own[cc * 16:(cc + 1) * 16, :].rearrange("t (c s) -> c t s", s=128), reads=[skey_own],
                  writes=['vt%d' % ib])

            def load_kf(g, l=l, o=o, cc=cc):
                k = g % 2
                nj_ = FGROUPS[g][1]
                r0 = kf_row(l, o, cc, g)
                S.dma('sp', kfg[k][:, :, 0:nj_, :],
                      ctx['kf_d'][r0:r0 + 128, :].rearrange("p (r j c) -> p r j c", r=2, j=8)[:, :, 0:nj_, :],
                      reads=['kf_d'], writes=['kfg%d' % k])

            def pre2(load_kf=load_kf):
                load_kf(0)
                load_kf(1)

            def prod(g, b0, fa0, nj, l=l, o=o, cc=cc, load_kf=load_kf):
                k = g % 2
                for hb in range((nj + 3) // 4):
                    n = min(4, nj - hb * 4)
                    xre, xim = PS[b0 + hb], PS[b0 + 2 + hb]
                    kre = kfg[k][:, 0, hb * 4:hb * 4 + n, :]
                    kim = kfg[k][:, 1, hb * 4:hb * 4 + n, :]
                    v3 = lambda t, n=n: t[:, 0:n * 128].rearrange("p (j c) -> p j c", j=n)
                    for ti, (xa, ka) in enumerate(((xre, kre), (xim, kim), (xre, kim), (xim, kre))):
                        S.op('dve', lambda e, ti=ti, xa=xa, ka=ka, v3=v3: e.tensor_tensor(
                            out=v3(tmp[ti]), in0=v3(xa), in1=ka, op=ALU.mult),
                            reads=['ps%d' % (b0 + hb), 'ps%d' % (b0 + 2 + hb), 'kfg%d' % k], writes=['tmp%d' % ti])
                    f0 = fa0 + hb * 4
                    v3t = lambda t, n=n: t[:, 0:n * 128].rearrange("p (j c) -> p c j", j=n)
                    S.op('pool', lambda e, f0=f0, n=n, v3t=v3t: e.tensor_tensor(
                        out=Y1[:, :, f0:f0 + n], in0=v3t(tmp[0]), in1=v3t(tmp[1]), op=ALU.subtract),
                        reads=['tmp0', 'tmp1'], writes=['Y1'])
                    S.op('dve', lambda e, f0=f0, n=n, v3t=v3t: e.tensor_tensor(
                        out=Y1[:, :, NA + f0:NA + f0 + n], in0=v3t(tmp[2]), in1=v3t(tmp[3]), op=ALU.add),
                        reads=['tmp2', 'tmp3'], writes=['Y1'])
                    S.op('act', lambda e, f0=f0, n=n: e.activation(out=Y2[:, :, f0:f0 + n],
                                                                   in_=Y1[:, :, NA + f0:NA + f0 + n], func=AF.Copy,
                                                                   scale=-1.0), reads=['Y1'], writes=['Y2'])
                    S.op('act', lambda e, f0=f0, n=n: e.copy(out=Y2[:, :, NA + f0:NA + f0 + n],
                                                             in_=Y1[:, :, f0:f0 + n]), reads=['Y1'], writes=['Y2'])
                if g + 2 < len(FGROUPS):
                    load_kf(g + 2)

            def after1(cc=cc, load_in=load_in):
                if cc < 3:
                    load_in(cc + 1)
            fwd_transform(ctx, h, 32, h['w1d'], 'w1d', prod, after_stage1=after1, pre_stage2=pre2)
            Bv = Bb[0:NS, :].rearrange("p (t c) -> p t c", t=128)
            S.op('dve', lambda e, o=o, cc=cc, ib=ib: e.tensor_scalar(
                out=bv[:, :], in0=vt[ib][:, :], scalar1=bia[:, o, cc:cc + 1], scalar2=None, op0=ALU.mult),
                reads=['vt%d' % ib, 'bia'], writes=['bv'])
            for c4 in range(32):
                bank = c4 % 4
                for ci in range(4):
                    c = c4 * 4 + ci
                    S.op('pe', lambda e, bank=bank, c=c, ci=ci: e.matmul(
                        PS[bank][0:NS, ci * 128:(ci + 1) * 128], lhsT=Y1[:, c, :], rhs=G[:, 0:128],
                        start=True, stop=False), reads=['Y1', 'G'], writes=['ps%d' % bank])
                    S.op('pe', lambda e, bank=bank, c=c, ci=ci: e.matmul(
                        PS[bank][0:NS, ci * 128:(ci + 1) * 128], lhsT=Y2[:, c, :], rhs=G[:, 128:256],
                        start=False, stop=True), reads=['Y2', 'G'], writes=['ps%d' % bank])
                srcv = PS[bank][0:NS, :].rearrange("p (c t) -> p t c", c=4)
                if c4 % 2 == 0:
                    S.op('act', lambda e, c4=c4, srcv=srcv: e.copy(out=Bv[:, :, c4 * 4:c4 * 4 + 4], in_=srcv),
                         reads=['ps%d' % bank], writes=['Bb'])
                else:
                    S.op('dve', lambda e, c4=c4, srcv=srcv: e.tensor_copy(out=Bv[:, :, c4 * 4:c4 * 4 + 4], in_=srcv),
                         reads=['ps%d' % bank], writes=['Bb'])
            for tA in range(128):
                bank = 4 + tA // 32
                col = (tA % 32) * 16
                S.op('pe', lambda e, tA=tA, bank=bank, col=col: e.matmul(
                    PS[bank][:, col:col + 16], lhsT=Bv[:, tA, :], rhs=H[:, tA * 16:(tA + 1) * 16],
                    start=True, stop=True), reads=['Bb', 'H'], writes=['ps%d' % bank])
            ridx = cc * 4 + l * 2 + o
            for b in range(4):
                dst = lambda t, b=b: t[:, :].rearrange("p (tb ta) -> p ta tb", ta=128)[:, 32 * b:32 * b + 32, :]
                S.op('dve', lambda e, b=b, dst=dst, ridx=ridx: e.scalar_tensor_tensor(
                    out=dst(yt), in0=PS[4 + b][:, :].rearrange("p (ta tb) -> p ta tb", tb=16),
                    scalar=rn[:, ridx:ridx + 1], in1=dst(bv), op0=ALU.mult, op1=ALU.add),
                    reads=['ps%d' % (4 + b), 'rn', 'bv'], writes=['yt'])
            S.op('pool', lambda e, ib=ib: e.tensor_tensor(out=ut[:, :], in0=yt[:, :], in1=gt[ib][:, :], op=ALU.mult),
                 reads=['yt', 'gt%d' % ib], writes=['ut'])
            if o == 0:
                S.dma('pool', ctx['hu_in'][cc * 16:(cc + 1) * 16, :].rearrange("t (c s) -> c t s", s=128),
                      ut[:, :].rearrange("c (t s) -> c t s", s=128), reads=['ut'], writes=['hu_in'])
            else:
                S.dma('pool', ctx['mixT_d'][1536 + cc * 128:1536 + (cc + 1) * 128, :], ut[:, :], reads=['ut'],
                      writes=['mixT_d'])
        if o == 0:
            S.allgather(ctx['hu_in'][:, :], ctx['hu_all'][:, :], reads=['hu_in'], writes=['hu_all'])
    S.barrier()
    AR.release()
```

```python
import math
import numpy as np
import ml_dtypes
import concourse.bass as bass
import concourse.mybir as mybir
from concourse.bass_utils import run_bass_kernel_spmd

F32 = mybir.dt.float32
BF16 = mybir.dt.bfloat16
AF = mybir.ActivationFunctionType
ALU = mybir.AluOpType
AX = mybir.AxisListType

D = 2048
DEPTH = 2
T = 2048
HALO = 128
TE = T + 2 * HALO
DIN = 4352
DFF = 5632
NFF = DFF // 128
EPS = 1e-6
NFFT = 8192
PAIRS = [[0, 1], [2, 3], [4, 5], [6, 7]]
ENGS = ['pe', 'act', 'dve', 'pool', 'sp']


class Sched:
    def __init__(self, nc, es):
        self.nc = nc
        self.ops = {e: [] for e in ENGS}
        self.cnt = {e: 0 for e in ENGS}
        self.sem = {e: es.enter_context(nc.semaphore('s_' + e)) for e in ['pe', 'act', 'dve', 'pool']}
        self.R = 8
        self.dsem = {q: [es.enter_context(nc.semaphore('d_%s%d' % (q, i))) for i in range(self.R)]
                     for q in ['sp', 'act', 'pool']}
        self.dcnt = {q: 0 for q in ['sp', 'act', 'pool']}
        self.ccsem = es.enter_context(nc.semaphore('ccsem'))
        self.cccnt = 0
        self.ccsem2 = es.enter_context(nc.semaphore('ccsem2'))
        self.sticky = set()
        self.waited = {e: {} for e in ENGS}
        self.lastw = {}
        self.reads = {}
        self.same_engine_sync = True

    def _wait(self, eng, tok):
        sk, sh, val = tok
        if self.waited[eng].get(sk, 0) >= val:
            return
        if sk == eng and (eng == 'pe' or not self.same_engine_sync):
            return
        self.waited[eng][sk] = val
        self.ops[eng].append(lambda e, sh=sh, val=val: e.wait_ge(sh, val))

    def _deps(self, eng, reads, writes):
        for k in reads:
            if k in self.lastw:
                self._wait(eng, self.lastw[k])
        for k in writes:
            if k in self.lastw:
                self._wait(eng, self.lastw[k])
            for tok in self.reads.get(k, {}).values():
                self._wait(eng, tok)

    def _commit(self, tok, reads, writes):
        for k in writes:
            self.lastw[k] = tok
            self.reads[k] = {}
        for k in reads:
            d = self.reads.setdefault(k, {})
            if tok[0] not in d or d[tok[0]][2] < tok[2]:
                d[tok[0]] = tok

    def op(self, eng, fn, reads=(), writes=()):
        self._deps(eng, reads, writes)
        self.cnt[eng] += 1
        val = self.cnt[eng]
        sh = self.sem[eng]
        self.ops[eng].append(lambda e, fn=fn, sh=sh: fn(e).then_inc(sh, 1))
        self._commit((eng, sh, val), reads, writes)

    def dma(self, q, out, in_, reads=(), writes=(), **kw):
        i = self.dcnt[q]
        self.dcnt[q] += 1
        slot = i % self.R
        val = 16 * (i // self.R + 1)
        sh = self.dsem[q][slot]
        sk = 'd_%s%d' % (q, slot)
        if val > 16:
            self._wait(q, (sk, sh, val - 16))
        self._deps(q, reads, writes)
        self.ops[q].append(lambda e, out=out, in_=in_, sh=sh, kw=kw:
                           e.dma_start(out=out, in_=in_, **kw).then_inc(sh, 16))
        self._commit((sk, sh, val), reads, writes)

    def allgather(self, in_ap, out_ap, reads=(), writes=(), groups=None, big=False):
        q = 'pool'
        self._deps(q, reads, writes)
        if big:
            self.cc2cnt = getattr(self, 'cc2cnt', 0) + 1
            sh, sk, val = self.ccsem2, 'cc2', self.cc2cnt
        else:
            self.cccnt += 1
            sh, sk, val = self.ccsem, 'cc', self.cccnt
        groups = groups or PAIRS
        self.ops[q].append(lambda e, sh=sh, groups=groups: e.collective_compute(
            "AllGather", ALU.bypass, replica_groups=groups,
            ins=[in_ap], outs=[out_ap]).then_inc(sh))
        self._commit((sk, sh, val), reads, writes)

    def barrier(self):
        toks = []
        for e in ['pe', 'act', 'dve', 'pool']:
            if self.cnt[e] > 0:
                toks.append((e, self.sem[e], self.cnt[e]))
        for q in ['sp', 'act', 'pool']:
            n = self.dcnt[q]
            for slot in range(self.R):
                uses = (n - slot + self.R - 1) // self.R if n > slot else 0
                if uses > 0:
                    toks.append(('d_%s%d' % (q, slot), self.dsem[q][slot], 16 * uses))
        if self.cccnt:
            toks.append(('cc', self.ccsem, self.cccnt))
        for e in ENGS:
            for tok in toks:
                if tok[0] == e and e == 'pe':
                    continue
                sk, sh, val = tok
                if self.waited[e].get(sk, 0) >= val:
                    continue
                self.waited[e][sk] = val
                self.ops[e].append(lambda en, sh=sh, val=val: en.wait_ge(sh, val))
        self.lastw = {k: v for k, v in self.lastw.items() if k in self.sticky}
        self.reads = {}

    def emit(self):
        self.barrier()
        nc = self.nc
        ops = self.ops
        with nc.Block() as block:
            @block.tensor
            def _(e):
                for f in ops['pe']:
                    f(e)

            @block.scalar
            def _(e):
                for f in ops['act']:
                    f(e)

            @block.vector
            def _(e):
                for f in ops['dve']:
                    f(e)

            @block.gpsimd
            def _(e):
                for f in ops['pool']:
                    f(e)

            @block.sync
            def _(e):
                for f in ops['sp']:
                    f(e)


class Arena:
    def __init__(self, nc, base, limit):
        self.nc = nc
        self.base = base
        self.limit = limit
        self.off = base
        self.n = 0
        self.marks = []

    def mark(self):
        self.marks.append(self.off)

    def release(self):
        self.off = self.marks.pop()

    def alloc(self, shape, dtype, name=None):
        nbytes = int(np.prod(shape[1:])) * (2 if dtype == BF16 else 4)
        nbytes = (nbytes + 63) // 64 * 64
        off = self.off
        assert off + nbytes <= self.limit, ("SBUF arena overflow", name, off, nbytes, self.limit)
        self.off += nbytes
        self.n += 1
        return self.nc.alloc_sbuf_tensor_at("%s_%d" % (name or 't', self.n), list(shape), dtype, offset=off)


def sap(t, part0, nparts, dims, off=0):
    full = t[:]
    pstep = full.ap[0][0]
    return bass.AP(t, full.offset + part0 * pstep + off, [[pstep, nparts]] + [[s, c] for (s, c) in dims])


def build(dbg=None):
    nc = bass.Bass("TRN2", target_bir_lowering=False)
    from contextlib import ExitStack
    es = ExitStack()
    S = Sched(nc, es)
    dbg = dbg or {}
    ins = {}

    def din(name, shape, dt=F32):
        ins[name] = nc.dram_tensor(name, list(shape), dt, kind="ExternalInput")
        return ins[name]

    x_ext = din("x_ext", [TE, D])
    mix_g = din("mix_norm_g", [DEPTH, D])
    w_in = din("w_in", [DEPTH, D, DIN])
    pool_w = din("pool_w", [DEPTH, 4, 128, 128])
    pool_scale = din("pool_scale", [DEPTH, 512])
    attn_sink = din("attn_sink", [DEPTH, 8])
    sconv_w = din("sconv_w", [DEPTH, 3, 512])
    hy_short_w = din("hy_short_w", [DEPTH, 3, 1536])
    hy_w1 = din("hy_w1", [DEPTH, 33, 64])
    hy_b1 = din("hy_b1", [DEPTH, 64])
    hy_w2 = din("hy_w2", [DEPTH, 64, 64])
    hy_b2 = din("hy_b2", [DEPTH, 64])
    hy_w3 = din("hy_w3_sh", [DEPTH, 64, 512])
    hy_freq = din("hy_freq", [DEPTH, 64])
    hy_decay = din("hy_decay_sh", [DEPTH, 512])
    hy_bias = din("hy_bias", [DEPTH, 2, 512])
    w_out = din("w_out", [DEPTH, D, D])
    ffn_g = din("ffn_norm_g", [DEPTH, D])
    w_gu = din("w_gate_up", [DEPTH, D, 2 * DFF])
    w_down = din("w_down", [DEPTH, DFF, D])
    fin_g = din("final_norm_g", [D])
    t_ident = din("t_ident", [128, 128], BF16)
    t_rope = din("t_rope", [2, 16, TE])
    t_perm = din("t_perm", [16, 16])
    t_mask = din("t_mask", [4, 128, 128], BF16)
    t_invcnt = din("t_invcnt", [4, T])
    t_flags = din("t_flags", [128, 2])
    t_w1d = din("t_w1d", [32, 66], BF16)
    t_w1f = din("t_w1f", [64, 66], BF16)
    t_m = din("t_m", [5, 128, 3 * 8 * 128], BF16)
    t_g = din("t_g", [128, 3 * 128], BF16)
    t_h = din("t_h", [66, 128 * 16], BF16)
    t_pos = din("t_pos", [33, NFFT])
    t_tdec = din("t_tdec", [1, NFFT])

    y_out = nc.dram_tensor("y_out", [T, D], F32, kind="ExternalOutput")
    dbg_out = {}
    for k, shp in dbg.items():
        dbg_out[k] = nc.dram_tensor("dbg_" + k, list(shp[0]), shp[1], kind="ExternalOutput")

    x1_ext = nc.dram_tensor("x1_ext", [TE, D], F32)
    xmid = nc.dram_tensor("xmid", [T, D], F32)
    mixT_d = nc.dram_tensor("mixT_d", [D, T], BF16)
    edge_in = nc.dram_tensor("edge_in", [256, D], F32)
    edge_all = nc.dram_tensor("edge_all", [512, D], F32)
    hv_in = nc.dram_tensor("hv_in", [64, 128 * 128], BF16)
    hv_all = nc.dram_tensor("hv_all", [128, 128 * 128], BF16)
    hu_in = nc.dram_tensor("hu_in", [64, 128 * 128], BF16)
    hu_all = nc.dram_tensor("hu_all", [128, 128 * 128], BF16)
    g12_d = nc.dram_tensor("g12_d", [2, 512, T], BF16)
    kk_d = nc.dram_tensor("kk_d", [2, 128, NFFT], BF16)
    kf_d = nc.dram_tensor("kf_d", [4 * 2560, 2 * 8 * 128], BF16)
    kf_loc = nc.dram_tensor("kf_loc", [4 * 640, 2 * 8 * 128], BF16)
    rn_loc = nc.dram_tensor("rn_loc", [4, 128], F32)
    rn_d = nc.dram_tensor("rn_d", [16, 128], F32)

    AR = Arena(nc, 16640, 229312)
    PS = [nc.alloc_psum_tensor("ps%d" % i, [128, 512], F32) for i in range(8)]

    def psb(i):
        return PS[i]

    ident = AR.alloc([128, 128], BF16, "ident")
    S.dma('sp', ident[:, :], t_ident[:, :], writes=['ident'])
    gcolM = AR.alloc([128, DEPTH, 16], F32, "gcolM")
    gcolF = AR.alloc([128, DEPTH, 16], F32, "gcolF")
    for l in range(DEPTH):
        S.dma('sp', gcolM[:, l, :], mix_g[l].rearrange("(k p) -> p k", p=128), writes=['gcol'],
              allow_slow_non_contiguous=True)
        S.dma('sp', gcolF[:, l, :], ffn_g[l].rearrange("(k p) -> p k", p=128), writes=['gcol'],
              allow_slow_non_contiguous=True)
    ctx = dict(nc=nc, S=S, AR=AR, PS=PS, ident=ident, gcolM=gcolM, gcolF=gcolF, ins=ins,
               dbg=dbg, dbg_out=dbg_out)
    ctx.update(x_ext=x_ext, x1_ext=x1_ext, xmid=xmid, mixT_d=mixT_d, y_out=y_out,
               edge_in=edge_in, edge_all=edge_all, hv_in=hv_in, hv_all=hv_all,
               hu_in=hu_in, hu_all=hu_all, g12_d=g12_d, kk_d=kk_d, kf_d=kf_d, kf_loc=kf_loc,
               rn_loc=rn_loc, rn_d=rn_d)
    return nc, S, ctx, es


def rms_to_T(ctx, xt, gcol_l, hT, col0, uid):
    S, AR, PS, ident = ctx['S'], ctx['AR'], ctx['PS'], ctx['ident']
    sc = ctx['sc_junk']
    ss = ctx['sc_ss']
    xn = ctx['sc_xn']
    S.op('act', lambda e: e.activation(out=sc[:, :], in_=xt[:, :], func=AF.Square, accum_out=ss[:, 0:1]),
         reads=[uid], writes=[ctx.get('junk_key', 'sc_junk'), 'sc_ss'])
    S.op('dve', lambda e: e.tensor_scalar(out=ss[:, 1:2], in0=ss[:, 0:1], scalar1=1.0 / D, scalar2=EPS,
                                          op0=ALU.mult, op1=ALU.add), reads=['sc_ss'], writes=['sc_ss1'])
    S.op('act', lambda e: e.activation(out=ss[:, 3:4], in_=ss[:, 1:2], func=AF.Sqrt),
         reads=['sc_ss1'], writes=['sc_ss3'])
    S.op('dve', lambda e: e.reciprocal(out=ss[:, 2:3], in_=ss[:, 3:4]), reads=['sc_ss3'], writes=['sc_ss2'])
    S.op('act', lambda e: e.activation(out=xn[:, :], in_=xt[:, :], func=AF.Copy, scale=ss[:, 2:3]),
         reads=[uid, 'sc_ss2'], writes=['sc_xn'])
    for q in range(4):
        bank = ctx['tp_bank'][ctx['tp_i'] % 2]
        ctx['tp_i'] += 1
        pst = PS[bank][:, :].bitcast(BF16)
        for j in range(4):
            kc = q * 4 + j
            S.op('pe', lambda e, kc=kc, j=j, pst=pst: e.transpose(out=pst[:, j * 128:(j + 1) * 128],
                                                                    in_=xn[:, kc * 128:(kc + 1) * 128],
                                                                    identity=ident[:, :]),
                 reads=['sc_xn', 'ident'], writes=['ps%d' % bank])
        S.op('dve', lambda e, q=q, pst=pst: e.tensor_tensor(
            out=hT[:, q * 4:(q + 1) * 4, col0:col0 + 128],
            in0=pst[:, 0:512].rearrange("p (a b) -> p a b", a=4),
            in1=gcol_l[:, q * 4:(q + 1) * 4].unsqueeze(2).to_broadcast([128, 4, 128]),
            op=ALU.mult), reads=['ps%d' % bank, 'gcol'], writes=['hT'])


def wload(ctx, dst, src, key):
    ctx['S'].dma('pool', dst, src, writes=[key])


def proj(ctx, wblk, wkey, c0, M, hT, tok0, ntok, bank):
    S, PS = ctx['S'], ctx['PS']
    for kc in range(16):
        S.op('pe', lambda e, kc=kc: e.matmul(PS[bank][0:M, 0:ntok], lhsT=wblk[:, kc, c0:c0 + M],
                                               rhs=hT[:, kc, tok0:tok0 + ntok],
                                               start=(kc == 0), stop=(kc == 15)),
             reads=[wkey, 'hT'], writes=['ps%d' % bank])


TOKG = [(0, 512), (512, 512), (1024, 512), (1536, 512), (2048, 256)]


def phase_norm(ctx, l, xsrc, tiles=None):
    S, AR = ctx['S'], ctx['AR']
    hT = ctx['hT']
    AR.mark()
    ctx['sc_junk'] = AR.alloc([128, D], F32, 'junk')
    ctx['junk_key'] = 'sc_junk'
    ctx['sc_ss'] = AR.alloc([128, 4], F32, 'ss')
    ctx['sc_xn'] = AR.alloc([128, D], BF16, 'xn')
    xt = [AR.alloc([128, D], F32, 'xt%d' % i) for i in range(2)]
    ctx['tp_bank'] = [0, 1]
    ctx['tp_i'] = 0
    for n, i in enumerate(tiles if tiles is not None else range(TE // 128)):
        b = n % 2
        S.dma('sp', xt[b][:, :], xsrc[i * 128:(i + 1) * 128, :], writes=['xt%d' % b])
        rms_to_T(ctx, xt[b], ctx['gcolM'][:, l, :], hT, i * 128, 'xt%d' % b)
    S.barrier()
    AR.release()


def conv3(ctx, eng, out, zin, wcol, c_lo, n, rkey, wkey_out):
    S = ctx['S']
    S.op(eng, lambda e: e.tensor_scalar(out=out[:, 0:n], in0=zin[:, c_lo:c_lo + n], scalar1=wcol[:, 1:2],
                                        scalar2=None, op0=ALU.mult), reads=[rkey, 'cw'], writes=[wkey_out])
    S.op(eng, lambda e: e.scalar_tensor_tensor(out=out[:, 0:n], in0=zin[:, c_lo - 1:c_lo - 1 + n],
                                               scalar=wcol[:, 0:1], in1=out[:, 0:n], op0=ALU.mult, op1=ALU.add),
         reads=[rkey, 'cw', wkey_out], writes=[wkey_out])
    S.op(eng, lambda e: e.scalar_tensor_tensor(out=out[:, 0:n], in0=zin[:, c_lo + 1:c_lo + 1 + n],
                                               scalar=wcol[:, 2:3], in1=out[:, 0:n], op0=ALU.mult, op1=ALU.add),
         reads=[rkey, 'cw', wkey_out], writes=[wkey_out])


def phase_local(ctx, l):
    S, AR, PS, ins = ctx['S'], ctx['AR'], ctx['PS'], ctx['ins']
    hT = ctx['hT']
    w_in = ins['w_in']
    AR.mark()
    wb = [AR.alloc([128, 16, 128], BF16, 'wb%d' % i) for i in range(3)]
    zc = [AR.alloc([128, TE], F32, 'zc%d' % i) for i in range(3)]
    sA = AR.alloc([128, TE], F32, 'sA')
    sB = AR.alloc([128, TE], F32, 'sB')
    invc = AR.alloc([128, T], F32, 'invc')
    dT = AR.alloc([128, T], BF16, 'dT')
    ob = [AR.alloc([128, T], BF16, 'ob%d' % i) for i in range(2)]
    pw = AR.alloc([128, 4, 128], BF16, 'pw')
    pscale = AR.alloc([128, 4], F32, 'pscale')
    cwC = AR.alloc([128, 4, 3], F32, 'cwC')
    cwD = AR.alloc([128, 12, 3], F32, 'cwD')
    wload(ctx, pw[:, :, :], ins['pool_w'][l].rearrange("g c d -> c g d"), 'pw')
    S.dma('sp', pscale[:, :], ins['pool_scale'][l].rearrange("(g p) -> p g", p=128), writes=['pscale'],
          allow_slow_non_contiguous=True)
    for k in range(3):
        S.dma('sp', cwC[:, :, k], ins['sconv_w'][l][k].rearrange("(c p) -> p c", p=128), writes=['cw'],
              allow_slow_non_contiguous=True)
        S.dma('sp', cwD[:, :, k], ins['hy_short_w'][l][k].rearrange("(c p) -> p c", p=128), writes=['cw'],
              allow_slow_non_contiguous=True)
    st = {'wi': 0, 'pb': 0, 'oi': 0}

    def zblock(blk, zi):
        w = st['wi'] % 3
        st['wi'] += 1
        wload(ctx, wb[w][:, :, :], w_in[l][:, blk * 128:(blk + 1) * 128].rearrange("(k p) c -> p k c", p=128),
              'wb%d' % w)
        for (t0, n) in TOKG:
            bank = 2 + st['pb'] % 4
            st['pb'] += 1
            proj(ctx, wb[w], 'wb%d' % w, 0, 128, hT, t0, n, bank)
            S.op('act', lambda e, bank=bank, t0=t0, n=n: e.copy(out=zc[zi][:, t0:t0 + n], in_=PS[bank][:, 0:n]),
                 reads=['ps%d' % bank], writes=['zc%d' % zi])

    def out_rows(row0, obuf, okey):
        S.dma('act', ctx['mixT_d'][row0:row0 + 128, :], obuf[:, :], reads=[okey], writes=['mixT_d'])

    for g in range(4):
        zblock(g, 0)
        z = zc[0]
        S.dma('sp', invc[:, :], ins['t_invcnt'][g:g + 1, :].to_broadcast([128, T]), writes=['invc'])
        S.op('dve', lambda e: e.tensor_tensor(out=sA[:, 1:TE], in0=z[:, 0:TE - 1], in1=z[:, 1:TE], op=ALU.add),
             reads=['zc0'], writes=['sA'])
        cur, ckey, oth, okey = sA, 'sA', sB, 'sB'
        vlo, vhi = 1, TE
        for lev in range(g):
            h = 1 << lev
            lo, hi = vlo + h, vhi - h
            S.op('dve', lambda e, cur=cur, oth=oth, h=h, lo=lo, hi=hi: e.tensor_tensor(
                out=oth[:, lo:hi], in0=cur[:, lo - h:hi - h], in1=cur[:, lo + h:hi + h], op=ALU.add),
                reads=[ckey], writes=[okey])
            vlo, vhi = lo, hi
            cur, ckey, oth, okey = oth, okey, cur, ckey
        S.op('dve', lambda e, cur=cur, oth=oth: e.tensor_tensor(out=oth[:, 0:T], in0=cur[:, HALO:HALO + T], in1=invc[:, :],
                                                       op=ALU.mult), reads=[ckey, 'invc'], writes=[okey])
        S.op('dve', lambda e, oth=oth: e.tensor_tensor(out=dT[:, :], in0=oth[:, 0:T], in1=z[:, HALO:HALO + T],
                                                       op=ALU.subtract), reads=[okey, 'zc0'], writes=['dT'])
        o = st['oi'] % 2
        st['oi'] += 1
        for q in range(4):
            bank = 6 + q % 2
            S.op('pe', lambda e, q=q, bank=bank, g=g: e.matmul(PS[bank][:, :], lhsT=pw[:, g, :],
                                                               rhs=dT[:, q * 512:(q + 1) * 512], start=True, stop=True),
                 reads=['pw', 'dT'], writes=['ps%d' % bank])
            S.op('act', lambda e, q=q, bank=bank, g=g, o=o: e.activation(
                out=ob[o][:, q * 512:(q + 1) * 512], in_=PS[bank][:, :], func=AF.Copy, scale=pscale[:, g:g + 1]),
                reads=['ps%d' % bank, 'pscale'], writes=['ob%d' % o])
        out_rows(g * 128, ob[o], 'ob%d' % o)

    for c in range(4):
        zblock(10 + c, 0)
        zblock(14 + c, 1)
        zblock(18 + c, 2)
        S.op('pool', lambda e: e.tensor_tensor(out=sA[:, :], in0=zc[2][:, :], in1=zc[0][:, :], op=ALU.mult),
             reads=['zc0', 'zc2'], writes=['sA'])
        conv3(ctx, 'dve', sB, sA, cwC[:, c, :], HALO, T, 'sA', 'sB')
        o = st['oi'] % 2
        st['oi'] += 1
        S.op('dve', lambda e, o=o: e.tensor_tensor(out=ob[o][:, :], in0=sB[:, 0:T], in1=zc[1][:, HALO:HALO + T],
                                                   op=ALU.mult), reads=['sB', 'zc1'], writes=['ob%d' % o])
        out_rows(1024 + c * 128, ob[o], 'ob%d' % o)

    for j in range(12):
        zi = j % 3
        zblock(22 + j, zi)
        o = st['oi'] % 2
        st['oi'] += 1
        eng = 'dve'
        conv3(ctx, eng, sA if j % 2 == 0 else sB, zc[zi], cwD[:, j, :], HALO, T, 'zc%d' % zi,
              'sA' if j % 2 == 0 else 'sB')
        src = sA if j % 2 == 0 else sB
        S.op('act', lambda e, o=o, src=src: e.copy(out=ob[o][:, :], in_=src[:, 0:T]),
             reads=['sA' if j % 2 == 0 else 'sB'], writes=['ob%d' % o])
        if j < 4:
            S.dma('act', ctx['hv_in'][j * 16:(j + 1) * 16, :].rearrange("t (c s) -> c t s", s=128),
                  ob[o][:, :].rearrange("c (t s) -> c t s", s=128), reads=['ob%d' % o], writes=['hv_in'])
        else:
            jj = j - 4
            S.dma('act', ctx['g12_d'][jj // 4, (jj % 4) * 128:(jj % 4 + 1) * 128, :], ob[o][:, :],
                  reads=['ob%d' % o], writes=['g12_d'])
    S.barrier()
    AR.release()
    S.allgather(ctx['hv_in'][:, :], ctx['hv_all'][:, :], reads=['hv_in'], writes=['hv_all'])


def phase_attn(ctx, l):
    S, AR, PS, ins, ident = ctx['S'], ctx['AR'], ctx['PS'], ctx['ins'], ctx['ident']
    hT = ctx['hT']
    w_in = ins['w_in']
    AR.mark()
    wb = [AR.alloc([128, 16, 128], BF16, 'wb%d' % i) for i in range(2)]
    qraw = AR.alloc([64, TE], F32, 'qraw')
    rope = AR.alloc([16, 2, TE], F32, 'rope')
    rt1 = AR.alloc([16, 512], F32, 'rt1')
    rt2 = AR.alloc([16, 512], F32, 'rt2')
    perm = AR.alloc([16, 16], F32, 'perm')
    qk = AR.alloc([64, 10, TE], BF16, 'qk')
    vp = AR.alloc([128, 18, 2, 66], BF16, 'vp')
    E = [AR.alloc([128, 512], BF16, 'E%d' % i) for i in range(6)]
    Ot = AR.alloc([128, 512], BF16, 'Ot')
    den = AR.alloc([128, 8], F32, 'den')
    mixB = AR.alloc([128, 4, T], BF16, 'mixB')
    masks = AR.alloc([128, 4, 128], BF16, 'masks')
    snk = AR.alloc([128, 8], F32, 'snk')
    S.dma('sp', rope[:, :, :], ins['t_rope'].rearrange("a d t -> d a t"), writes=['rope'])
    S.dma('sp', perm[:, :], ins['t_perm'][:, :], writes=['perm'])
    S.dma('sp', masks[:, :, :], ins['t_mask'].rearrange("m k q -> k m q"), writes=['masks'])
    S.dma('sp', snk[:, :], ins['attn_sink'][l:l + 1, :].to_broadcast([128, 8]), writes=['snk'])
    S.op('act', lambda e: e.activation(out=snk[:, :], in_=snk[:, :], func=AF.Exp), reads=['snk'], writes=['snk'])
    S.op('pool', lambda e: e.memset(vp[:, :, :, :], 1.0), writes=['vp'])
    for hh in range(10):
        blk = 4 + hh // 2
        w = (hh // 2) % 2
        if hh % 2 == 0:
            wload(ctx, wb[w][:, :, :], w_in[l][:, blk * 128:(blk + 1) * 128].rearrange("(k p) c -> p k c", p=128),
                  'wb%d' % w)
        for gi, (t0, n) in enumerate(TOKG):
            bank = 2 + gi % 3
            proj(ctx, wb[w], 'wb%d' % w, (hh % 2) * 64, 64, hT, t0, n, bank)
            S.op('act', lambda e, bank=bank, t0=t0, n=n: e.copy(out=qraw[:, t0:t0 + n], in_=PS[bank][0:64, 0:n]),
                 reads=['ps%d' % bank], writes=['qraw'])
            S.op('act', lambda e, hh=hh, t0=t0, n=n: e.copy(out=qk[:, hh, t0:t0 + n], in_=qraw[:, t0:t0 + n]),
                 reads=['qraw'], writes=['qk'])
            S.op('pe', lambda e, t0=t0, n=n: e.matmul(PS[5][0:16, 0:n], lhsT=perm[:, :], rhs=qraw[0:16, t0:t0 + n],
                                                      start=True, stop=True),
                 reads=['perm', 'qraw'], writes=['ps5'])
            S.op('dve', lambda e, t0=t0, n=n: e.tensor_tensor(out=rt1[:, 0:n], in0=qraw[0:16, t0:t0 + n],
                                                              in1=rope[:, 0, t0:t0 + n], op=ALU.mult),
                 reads=['qraw', 'rope'], writes=['rt1'])
            S.op('dve', lambda e, t0=t0, n=n: e.tensor_tensor(out=rt2[:, 0:n], in0=PS[5][0:16, 0:n],
                                                              in1=rope[:, 1, t0:t0 + n], op=ALU.mult),
                 reads=['ps5', 'rope'], writes=['rt2'])
            S.op('dve', lambda e, hh=hh, t0=t0, n=n: e.tensor_tensor(out=qk[0:16, hh, t0:t0 + n], in0=rt1[:, 0:n],
                                                                     in1=rt2[:, 0:n], op=ALU.add),
                 reads=['rt1', 'rt2', 'qk'], writes=['qk'])
    wload(ctx, wb[1][:, :, :], w_in[l][:, 9 * 128:10 * 128].rearrange("(k p) c -> p k c", p=128), 'wb1')
    for i in range(18):
        bank = 2 + i % 3
        for kc in range(16):
            S.op('pe', lambda e, kc=kc, i=i, bank=bank: e.matmul(PS[bank][:, 0:128], lhsT=hT[:, kc, i * 128:(i + 1) * 128],
                                                                   rhs=wb[1][:, kc, :], start=(kc == 0), stop=(kc == 15)),
                 reads=['wb1', 'hT'], writes=['ps%d' % bank])
        S.op('act', lambda e, i=i, bank=bank: e.copy(out=vp[:, i, :, 0:64],
                                                     in_=PS[bank][:, 0:128].rearrange("p (g d) -> p g d", g=2)),
             reads=['ps%d' % bank], writes=['vp'])
    for i in range(16):
        for g in range(2):
            eb = 3 * g
            for jj in range(3):
                bank = 2 + eb + jj
                S.op('pe', lambda e, i=i, g=g, jj=jj, bank=bank: e.matmul(
                    PS[bank][:, :].rearrange("p (h q) -> p h q", h=4),
                    lhsT=qk[:, 8 + g, (i + jj) * 128:(i + jj + 1) * 128],
                    rhs=qk[:, 4 * g:4 * g + 4, (i + 1) * 128:(i + 2) * 128], start=True, stop=True),
                    reads=['qk'], writes=['ps%d' % bank])
                S.op('act', lambda e, bank=bank, k=eb + jj: e.activation(out=E[k][:, :], in_=PS[bank][:, :],
                                                                       func=AF.Exp, scale=0.125),
                     reads=['ps%d' % bank], writes=['E%d' % (eb + jj)])
            mP = 0 if i == 0 else 1
            mN = 3 if i == 15 else 2
            S.op('pool', lambda e, k=eb, mP=mP: e.tensor_tensor(
                out=E[k][:, :].rearrange("p (h q) -> p h q", h=4), in0=E[k][:, :].rearrange("p (h q) -> p h q", h=4),
                in1=masks[:, mP, :].unsqueeze(1).to_broadcast([128, 4, 128]), op=ALU.mult),
                reads=['E%d' % eb, 'masks'], writes=['E%d' % eb])
            S.op('pool', lambda e, k=eb + 2, mN=mN: e.tensor_tensor(
                out=E[k][:, :].rearrange("p (h q) -> p h q", h=4), in0=E[k][:, :].rearrange("p (h q) -> p h q", h=4),
                in1=masks[:, mN, :].unsqueeze(1).to_broadcast([128, 4, 128]), op=ALU.mult),
                reads=['E%d' % (eb + 2), 'masks'], writes=['E%d' % (eb + 2)])
            for h4 in range(4):
                for jj in range(3):
                    S.op('pe', lambda e, i=i, g=g, jj=jj, h4=h4, k=eb + jj: e.matmul(
                        PS[0][:, h4 * 66:h4 * 66 + 65], lhsT=E[k][:, h4 * 128:(h4 + 1) * 128],
                        rhs=vp[:, i + jj, g, 0:65], start=(jj == 0), stop=(jj == 2)),
                        reads=['E%d' % (eb + jj), 'vp'], writes=['ps0'])
            pso = PS[0][:, 0:264].rearrange("p (h d) -> p h d", h=4)
            S.op('dve', lambda e, g=g, pso=pso: e.tensor_tensor(out=den[:, 0:4], in0=pso[:, :, 64],
                                                                in1=snk[:, 4 * g:4 * g + 4], op=ALU.add),
                 reads=['ps0', 'snk'], writes=['den'])
            S.op('dve', lambda e: e.reciprocal(out=den[:, 4:8], in_=den[:, 0:4]), reads=['den'], writes=['den'])
            S.op('dve', lambda e, g=g, pso=pso: e.tensor_tensor(
                out=Ot[:, g * 256:(g + 1) * 256].rearrange("p (h d) -> p h d", h=4), in0=pso[:, :, 0:64],
                in1=den[:, 4:8].unsqueeze(2).to_broadcast([128, 4, 64]), op=ALU.mult),
                reads=['ps0', 'den'], writes=['Ot'])
        pst = PS[1][:, :].bitcast(BF16)
        for cb in range(4):
            S.op('pe', lambda e, cb=cb, pst=pst: e.transpose(out=pst[:, cb * 128:(cb + 1) * 128],
                                                              in_=Ot[:, cb * 128:(cb + 1) * 128], identity=ident[:, :]),
                 reads=['Ot', 'ident'], writes=['ps1'])
        S.op('act', lambda e, i=i, pst=pst: e.copy(out=mixB[:, :, i * 128:(i + 1) * 128],
                                                   in_=pst[:, 0:512].rearrange("p (c q) -> p c q", c=4)),
             reads=['ps1'], writes=['mixB'])
    for cb in range(4):
        S.dma('sp', ctx['mixT_d'][512 + cb * 128:512 + (cb + 1) * 128, :], mixB[:, cb, :], reads=['mixB'],
              writes=['mixT_d'])
    S.barrier()
    AR.release()


def phase_ffn(ctx, l, xsrc, final):
    S, AR, PS, ins = ctx['S'], ctx['AR'], ctx['PS'], ctx['ins']
    G = 1024
    NT = G // 128
    AR.mark()
    ctx['sc_ss'] = AR.alloc([128, 4], F32, 'ss')
    ctx['sc_xn'] = AR.alloc([128, D], BF16, 'xn')
    ctx['sc_junk'] = ctx['sc_xn']
    ctx['junk_key'] = 'sc_xn'
    h2T = AR.alloc([128, 16, G], BF16, 'h2T')
    wg = [AR.alloc([128, 16, 128], BF16, 'wg%d' % i) for i in range(4)]
    sil = [AR.alloc([128, 512], F32, 'sil%d' % i) for i in range(2)]
    ctx['tp_bank'] = [0, 1]
    ctx['tp_i'] = 0
    w_out, w_gu, w_down = ins['w_out'], ins['w_gate_up'], ins['w_down']
    xdst = ctx['x1_ext']
    cnt = {'wo': 0, 'wg': 0, 'wd': 0, 'pb': 0, 'xb': 0}
    if final:
        fxt = [AR.alloc([128, D], F32, 'fxt%d' % i) for i in range(3)]
        gfin = AR.alloc([128, D], F32, 'gfin')
        fjunk = AR.alloc([128, D], BF16, 'fjunk')
        ssf = AR.alloc([128, 3, 4], F32, 'ssf')
        S.dma('sp', gfin[:, :], ins['final_norm_g'].rearrange("(o d) -> o d", o=1).to_broadcast([128, D]),
              writes=['gfin'])
        fcnt = {'n': 0}

        def final_tile(i):
            b = fcnt['n'] % 3
            fcnt['n'] += 1
            r0 = i * 128
            ss = ssf[:, b, :]
            S.dma('sp', fxt[b][:, :], xdst[HALO + r0:HALO + r0 + 128, :], reads=['x1_ext'], writes=['fxt%d' % b])
            S.op('act', lambda e, b=b, ss=ss: e.activation(out=fjunk[:, :], in_=fxt[b][:, :], func=AF.Square,
                                                           accum_out=ss[:, 0:1]),
                 reads=['fxt%d' % b], writes=['fjunk', 'ssf%d' % b])
            S.op('dve', lambda e, ss=ss: e.tensor_scalar(out=ss[:, 1:2], in0=ss[:, 0:1], scalar1=1.0 / D, scalar2=EPS,
                                                         op0=ALU.mult, op1=ALU.add), reads=['ssf%d' % b],
                 writes=['ssf%d' % b])
            S.op('act', lambda e, ss=ss: e.activation(out=ss[:, 3:4], in_=ss[:, 1:2], func=AF.Sqrt),
                 reads=['ssf%d' % b], writes=['ssf%d' % b])
            S.op('dve', lambda e, ss=ss: e.reciprocal(out=ss[:, 2:3], in_=ss[:, 3:4]), reads=['ssf%d' % b],
                 writes=['ssf%d' % b])
            S.op('dve', lambda e, b=b, ss=ss: e.scalar_tensor_tensor(out=fxt[b][:, :], in0=fxt[b][:, :],
                                                                     scalar=ss[:, 2:3], in1=gfin[:, :], op0=ALU.mult,
                                                                     op1=ALU.mult),
                 reads=['fxt%d' % b, 'ssf%d' % b, 'gfin'], writes=['fxt%d' % b])
            S.dma('act', ctx['y_out'][r0:r0 + 128, :], fxt[b][:, :], reads=['fxt%d' % b], writes=['y_out'])
        fpending = []
    AR.mark()
    mixg = AR.alloc([128, 16, 512], BF16, 'mixg')
    wo = [AR.alloc([128, 16, 512], BF16, 'wo%d' % i) for i in range(2)]
    xm = [AR.alloc([128, D], F32, 'xm%d' % i) for i in range(4)]
    AR.release()
    AR.mark()
    actT = AR.alloc([128, NFF, G], BF16, 'actT')
    wd = [AR.alloc([128, 4, 512], BF16, 'wd%d' % i) for i in range(3)]
    xb = [AR.alloc([128, 512], F32, 'xb%d' % i) for i in range(4)]
    for gi in range(T // G):
        for sg in range(G // 512):
            tok0 = gi * G + sg * 512
            S.dma('sp', mixg[:, :, :], ctx['mixT_d'][:, tok0:tok0 + 512].rearrange("(k p) t -> p k t", p=128),
                  reads=['mixT_d'], writes=['mixg'])
            for t in range(4):
                r0 = HALO + tok0 + t * 128
                S.dma('sp', xm[t][:, :], xsrc[r0:r0 + 128, :], writes=['xm%d' % t])
            for cg in range(4):
                w = cnt['wo'] % 2
                cnt['wo'] += 1
                wload(ctx, wo[w][:, :, :], w_out[l][:, cg * 512:(cg + 1) * 512].rearrange("(k p) c -> p k c", p=128),
                      'wo%d' % w)
                for t in range(4):
                    bank = 2 + cnt['pb'] % 4
                    cnt['pb'] += 1
                    for kc in range(16):
                        S.op('pe', lambda e, kc=kc, t=t, w=w, bank=bank: e.matmul(
                            PS[bank][:, :], lhsT=mixg[:, kc, t * 128:(t + 1) * 128], rhs=wo[w][:, kc, :],
                            start=(kc == 0), stop=(kc == 15)), reads=['mixg', 'wo%d' % w], writes=['ps%d' % bank])
                    S.op('dve', lambda e, t=t, cg=cg, bank=bank: e.tensor_tensor(
                        out=xm[t][:, cg * 512:(cg + 1) * 512], in0=PS[bank][:, :],
                        in1=xm[t][:, cg * 512:(cg + 1) * 512], op=ALU.add),
                        reads=['ps%d' % bank, 'xm%d' % t], writes=['xm%d' % t])
            for t in range(4):
                rms_to_T(ctx, xm[t], ctx['gcolF'][:, l, :], h2T, sg * 512 + t * 128, 'xm%d' % t)
                S.dma('act', ctx['xmid'][tok0 + t * 128:tok0 + (t + 1) * 128, :], xm[t][:, :], reads=['xm%d' % t],
                      writes=['xmid'])
        S.barrier()
        for j in range(NFF):
            if final and fpending and j % 5 == 2:
                final_tile(fpending.pop(0))
            w = (cnt['wg'] % 2) * 2
            cnt['wg'] += 1
            wload(ctx, wg[w][:, :, :], w_gu[l][:, j * 128:(j + 1) * 128].rearrange("(k p) c -> p k c", p=128),
                  'wg%d' % w)
            wload(ctx, wg[w + 1][:, :, :],
                  w_gu[l][:, DFF + j * 128:DFF + (j + 1) * 128].rearrange("(k p) c -> p k c", p=128), 'wg%d' % (w + 1))
            for hf in range(G // 512):
                pb = cnt['pb'] % 4
                cnt['pb'] += 1
                ba, bb = 2 * pb, 2 * pb + 1
                for kc in range(16):
                    S.op('pe', lambda e, kc=kc, w=w, ba=ba, hf=hf: e.matmul(
                        PS[ba][:, :], lhsT=wg[w][:, kc, :], rhs=h2T[:, kc, hf * 512:(hf + 1) * 512],
                        start=(kc == 0), stop=(kc == 15)), reads=['wg%d' % w, 'hT'], writes=['ps%d' % ba])
                for kc in range(16):
                    S.op('pe', lambda e, kc=kc, w=w, bb=bb, hf=hf: e.matmul(
                        PS[bb][:, :], lhsT=wg[w + 1][:, kc, :], rhs=h2T[:, kc, hf * 512:(hf + 1) * 512],
                        start=(kc == 0), stop=(kc == 15)), reads=['wg%d' % (w + 1), 'hT'], writes=['ps%d' % bb])
                si = cnt['pb'] % 2
                S.op('act', lambda e, ba=ba, si=si: e.activation(out=sil[si][:, :], in_=PS[ba][:, :], func=AF.Silu),
                     reads=['ps%d' % ba], writes=['sil%d' % si])
                S.op('dve', lambda e, bb=bb, si=si, j=j, hf=hf: e.tensor_tensor(
                    out=actT[:, j, hf * 512:(hf + 1) * 512], in0=sil[si][:, :], in1=PS[bb][:, :], op=ALU.mult),
                    reads=['ps%d' % bb, 'sil%d' % si], writes=['actT'])
        for cg in range(4):
            for k4 in range(NFF // 4):
                w = cnt['wd'] % 3
                cnt['wd'] += 1
                wload(ctx, wd[w][:, :, :],
                      w_down[l][k4 * 512:(k4 + 1) * 512, cg * 512:(cg + 1) * 512].rearrange("(k p) c -> p k c", p=128),
                      'wd%d' % w)
                for kk in range(4):
                    k = k4 * 4 + kk
                    for t in range(NT):
                        S.op('pe', lambda e, k=k, kk=kk, t=t, w=w: e.matmul(
                            PS[t][:, :], lhsT=actT[:, k, t * 128:(t + 1) * 128], rhs=wd[w][:, kk, :],
                            start=(k == 0), stop=(k == NFF - 1)),
                            reads=['actT', 'wd%d' % w], writes=['ps%d' % t])
            for t in range(NT):
                r0 = gi * G + t * 128
                b = cnt['xb'] % 4
                cnt['xb'] += 1
                S.dma('sp', xb[b][:, :], ctx['xmid'][r0:r0 + 128, cg * 512:(cg + 1) * 512], reads=['xmid'],
                      writes=['xb%d' % b])
                S.op('dve', lambda e, t=t, b=b: e.tensor_tensor(out=xb[b][:, :], in0=PS[t][:, :], in1=xb[b][:, :],
                                                                op=ALU.add),
                     reads=['ps%d' % t, 'xb%d' % b], writes=['xb%d' % b])
                S.dma('act', xdst[HALO + r0:HALO + r0 + 128, cg * 512:(cg + 1) * 512], xb[b][:, :],
                      reads=['xb%d' % b], writes=['x1_ext'])
                if not final and r0 == 0:
                    S.dma('act', ctx['edge_in'][0:128, cg * 512:(cg + 1) * 512], xb[b][:, :], reads=['xb%d' % b],
                          writes=['edge_in'])
                if not final and r0 == T - 128:
                    S.dma('act', ctx['edge_in'][128:256, cg * 512:(cg + 1) * 512], xb[b][:, :], reads=['xb%d' % b],
                          writes=['edge_in'])
        S.barrier()
        if final:
            fpending.extend(range(gi * NT, (gi + 1) * NT))
    if final:
        for i in fpending:
            final_tile(i)
        S.barrier()
    AR.release()
    AR.release()


def halo_start(ctx):
    S = ctx['S']
    S.allgather(ctx['edge_in'][:, :], ctx['edge_all'][:, :], reads=['edge_in'], writes=['edge_all'])
    S.sticky.add('edge_all')


def halo_finish(ctx):
    S, AR, ins = ctx['S'], ctx['AR'], ctx['ins']
    AR.mark()
    hb = [AR.alloc([128, D], F32, 'hb%d' % i) for i in range(2)]
    fl = AR.alloc([128, 2], F32, 'fl')
    S.dma('sp', fl[:, :], ins['t_flags'][:, :], writes=['fl'])
    S.dma('sp', hb[0][:, :], ctx['edge_all'][128:256, :], reads=['edge_all'], writes=['hb0'])
    S.dma('sp', hb[1][:, :], ctx['edge_all'][256:384, :], reads=['edge_all'], writes=['hb1'])
    for i in range(2):
        S.op('dve', lambda e, i=i: e.tensor_scalar(out=hb[i][:, :], in0=hb[i][:, :], scalar1=fl[:, i:i + 1],
                                                   scalar2=None, op0=ALU.mult), reads=['hb%d' % i, 'fl'],
             writes=['hb%d' % i])
    S.dma('sp', ctx['x1_ext'][0:HALO, :], hb[0][:, :], reads=['hb0'], writes=['x1_ext'])
    S.dma('sp', ctx['x1_ext'][HALO + T:TE, :], hb[1][:, :], reads=['hb1'], writes=['x1_ext'])
    S.sticky.discard('edge_all')
    S.barrier()
    AR.release()


def core_info(c):
    if c < 4:
        return dict(kind='p', seq=c // 2, pos0=2048 * (c % 2), L=4096, rank=c % 2)
    return dict(kind='s', seq=c - 4, pos0=0, L=2048, rank=c % 2)


def host_tables(c):
    ci = core_info(c)
    pos0, L, rank = ci['pos0'], ci['L'], ci['rank']
    bf = ml_dtypes.bfloat16
    tb = {}
    tb['t_ident'] = np.eye(128, dtype=np.float32).astype(bf)
    pos = (pos0 - HALO + np.arange(TE)).astype(np.float32)
    inv_freq = (np.float32(500000.0) ** (-np.arange(0, 16, 2, dtype=np.float32) / np.float32(16))).astype(np.float32)
    ang = (pos[:, None] * inv_freq[None, :]).astype(np.float32)
    ang = np.concatenate([ang, ang], axis=1).T
    cs = np.cos(ang).astype(np.float32)
    sn = np.sin(ang).astype(np.float32)
    sn[:8] *= -1.0
    tb['t_rope'] = np.stack([cs, sn]).astype(np.float32)
    pm = np.zeros((16, 16), np.float32)
    for m in range(16):
        pm[(m + 8) % 16, m] = 1.0
    tb['t_perm'] = pm
    kk = np.arange(128)[:, None]
    qq = np.arange(128)[None, :]
    lv = 1.0 if pos0 > 0 else 0.0
    rv = 1.0 if pos0 + T < L else 0.0
    mP = (kk >= qq).astype(np.float32)
    mN = (kk <= qq).astype(np.float32)
    tb['t_mask'] = np.stack([mP * lv, mP, mN, mN * rv]).astype(bf)
    t = pos0 + np.arange(T)
    ic = np.zeros((4, T), np.float32)
    for g, w in enumerate((2, 4, 8, 16)):
        lo = np.clip(t - w // 2, 0, L)
        hi = np.clip(t + w // 2, 0, L)
        ic[g] = (1.0 / (hi - lo).astype(np.float32)).astype(np.float32)
    tb['t_invcnt'] = ic
    fl = np.zeros((128, 2), np.float32)
    fl[:, 0] = lv
    fl[:, 1] = rv
    tb['t_flags'] = fl
    tb.update(fft_tables(c))
    return tb


def fft_tables(c):
    ci = core_info(c)
    pos0h = 2048 * ci['rank']
    L, kind, rank = ci['L'], ci['kind'], ci['rank']
    bf = ml_dtypes.bfloat16
    tb = {}
    N = NFFT
    NA = 33
    fa = np.arange(NA)
    sB = np.arange(64)
    ph = -2.0 * np.pi * np.outer(sB, fa) / 64.0
    w1 = np.concatenate([np.cos(ph), np.sin(ph)], axis=1)
    tb['t_w1f'] = w1.astype(np.float32).astype(bf)
    w1d = w1[:32].copy()
    if kind == 's':
        if rank == 0:
            w1d[16:32] = 0.0
        else:
            w1d[0:16] = 0.0
    tb['t_w1d'] = w1d.astype(np.float32).astype(bf)
    sA = np.arange(128)
    fb = np.arange(128)
    tm = np.zeros((5, 128, 3, 8, 128), np.float32)
    for g in range(5):
        for j in range(8):
            if g * 8 + j >= NA:
                continue
            f = (g * 8 + j) + 64 * fb
            ph = -2.0 * np.pi * np.outer(sA, f) / N
            tm[g, :, 0, j, :] = np.cos(ph)
            tm[g, :, 1, j, :] = np.sin(ph)
            tm[g, :, 2, j, :] = -np.sin(ph)
    tb['t_m'] = tm.reshape(5, 128, 3 * 8 * 128).astype(bf)
    ph = 2.0 * np.pi * np.outer(fb, np.arange(128)) / 128.0
    tb['t_g'] = np.concatenate([np.cos(ph), np.sin(ph), -np.sin(ph)], axis=1).astype(np.float32).astype(bf)
    tA = np.arange(128)[:, None]
    tB = np.arange(16)[None, :]
    tt = (pos0h + 128 * tB + tA).reshape(-1)
    ph = 2.0 * np.pi * np.outer(fa, tt) / N
    wgt = np.where((fa == 0) | (fa == 32), 1.0, 2.0)[:, None]
    tb['t_h'] = np.concatenate([wgt * np.cos(ph), -wgt * np.sin(ph)], axis=0).astype(np.float32).astype(bf)
    s = np.arange(N)
    idx = np.where(s < N // 2, s, N - s)
    valid = np.where(s < N // 2, s < L, (N - s) <= L - 1)
    tl = np.linspace(0.0, 1.0, L, dtype=np.float32)
    idc = np.clip(idx, 0, L - 1)
    tt = tl[idc]
    bands = np.linspace(1e-4, 15.0, 16, dtype=np.float32)[None, :]
    wpos = (np.float32(2.0 * math.pi / L) * idc.astype(np.float32))[:, None]
    z = np.concatenate([tt[:, None], np.cos(bands * wpos), -np.sin(bands * wpos)], axis=1).astype(np.float32)
    z = np.where(valid[:, None], z, 0.0).astype(np.float32)
    tb['t_pos'] = np.ascontiguousarray(z.T)
    tb['t_tdec'] = np.where(valid, tt, 1.0e4).astype(np.float32)[None, :]
    return tb


_CACHE = {}


def kernel(**inputs):
    xp = np.asarray(inputs['x_prompt'], np.float32)
    xs = np.asarray(inputs['x_sample'], np.float32)
    if 'nc' not in _CACHE:
        nc, S, ctx, es = build()
        full_program(ctx)
        S.emit()
        _CACHE['nc'] = nc
        _CACHE['tabs'] = [host_tables(c) for c in range(8)]
    nc = _CACHE['nc']
    wnames = ["mix_norm_g", "w_in", "pool_w", "pool_scale", "attn_sink", "sconv_w", "hy_short_w", "hy_w1", "hy_b1",
              "hy_w2", "hy_b2", "hy_freq", "hy_bias", "w_out", "ffn_norm_g", "w_gate_up",
              "w_down", "final_norm_g"]
    w3f = np.asarray(inputs["hy_w3"], np.float32).reshape(DEPTH, 64, 4, 4, 128)
    dcf = np.asarray(inputs["hy_decay"], np.float32).reshape(DEPTH, 4, 4, 128)
    shared = {k: np.ascontiguousarray(np.asarray(inputs[k], np.float32)) for k in wnames}
    in_maps = []
    for c in range(8):
        ci = core_info(c)
        seq = xp[ci['seq']] if ci['kind'] == 'p' else xs[ci['seq']]
        xe = np.zeros((TE, D), np.float32)
        lo = ci['pos0'] - HALO
        hi = ci['pos0'] + T + HALO
        a, b = max(lo, 0), min(hi, ci['L'])
        xe[a - lo:b - lo] = seq[a:b]
        m = dict(shared)
        m['x_ext'] = xe
        m['hy_w3_sh'] = np.ascontiguousarray(w3f[:, :, :, c % 4, :]).reshape(DEPTH, 64, 512)
        m['hy_decay_sh'] = np.ascontiguousarray(dcf[:, :, c % 4, :]).reshape(DEPTH, 512)
        m.update(_CACHE['tabs'][c])
        in_maps.append(m)
    res = run_bass_kernel_spmd(nc, in_maps, core_ids=list(range(8)))
    _CACHE['last'] = res
    yp = np.zeros((2, 4096, D), np.float32)
    ys = np.zeros((4, 2048, D), np.float32)
    for c in range(8):
        ci = core_info(c)
        y = np.asarray(res.results[c]['y_out'], np.float32)
        if ci['kind'] == 'p':
            yp[ci['seq'], ci['pos0']:ci['pos0'] + T] = y
        else:
            ys[ci['seq']] = y
    return (yp, ys)


SKIP_HYENA = False


def full_program(ctx):
    S, AR = ctx['S'], ctx['AR']
    if not SKIP_HYENA:
        phase_filters(ctx)
    for l in range(DEPTH):
        xsrc = ctx['x_ext'] if l == 0 else ctx['x1_ext']
        AR.mark()
        ctx['hT'] = AR.alloc([128, 16, TE], BF16, 'hT')
        if l == 0:
            phase_norm(ctx, l, xsrc)
        else:
            phase_norm(ctx, l, xsrc, tiles=list(range(1, 17)))
            halo_finish(ctx)
            phase_norm(ctx, l, xsrc, tiles=[0, 17])
        phase_local(ctx, l)
        phase_attn(ctx, l)
        AR.release()
        if SKIP_HYENA:
            AR.mark()
            zt = AR.alloc([128, T], BF16, 'zt')
            S.op('pool', lambda e: e.memset(zt[:, :], 0.0), writes=['zt'])
            for c in range(4):
                S.dma('sp', ctx['mixT_d'][1536 + c * 128:1536 + (c + 1) * 128, :], zt[:, :], reads=['zt'],
                      writes=['mixT_d'])
            S.barrier()
            AR.release()
        else:
            phase_hyena(ctx, l)
        phase_ffn(ctx, l, xsrc, final=(l == DEPTH - 1))
        if l == 0:
            halo_start(ctx)


NA = 33
FGROUPS = [(0, 8), (8, 8), (16, 8), (24, 8), (32, 1)]


def hy_alloc(ctx):
    AR = ctx['AR']
    h = {}
    h['InB'] = AR.alloc([64, 128 * 128], BF16, 'InB')
    h['A'] = AR.alloc([128, 2, NA, 128], BF16, 'A')
    h['Mt'] = [AR.alloc([128, 3, 8, 128], BF16, 'Mt%d' % i) for i in range(len(FGROUPS))]
    h['w1d'] = AR.alloc([32, 2 * NA], BF16, 'w1d')
    h['w1f'] = AR.alloc([64, 2 * NA], BF16, 'w1f')
    S, ins = ctx['S'], ctx['ins']
    S.dma('sp', h['w1d'][:, :], ins['t_w1d'][:, :], writes=['w1d'])
    S.dma('sp', h['w1f'][:, :], ins['t_w1f'][:, :], writes=['w1f'])
    for g in range(len(FGROUPS)):
        S.dma('sp', h['Mt'][g][:, :, :, :], ins['t_m'][g].rearrange("p (a j f) -> p a j f", a=3, j=8),
              writes=['Mt%d' % g])
    return h


def fwd_transform(ctx, h, K, w1, w1key, on_group, after_stage1=None, pre_stage2=None):
    S, PS, ins = ctx['S'], ctx['PS'], ctx['ins']
    In, A = h['InB'], h['A']

    if pre_stage2 is not None:
        pre_stage2()
    for c4 in range(32):
        bank = c4 % 2
        for cc in range(4):
            c = c4 * 4 + cc
            S.op('pe', lambda e, c=c, cc=cc, bank=bank: e.matmul(PS[bank][:, cc * 128:cc * 128 + 2 * NA],
                                                               lhsT=In[0:K, c * 128:(c + 1) * 128], rhs=w1[0:K, :],
                                                               start=True, stop=True),
                 reads=['InB', w1key], writes=['ps%d' % bank])
        src = sap(PS[bank], 0, 128, [(NA, 2), (1, NA), (128, 4)])
        S.op('act', lambda e, c4=c4, src=src: e.copy(out=A[:, :, :, c4 * 4:(c4 + 1) * 4], in_=src),
             reads=['ps%d' % bank], writes=['A'])
    if after_stage1 is not None:
        after_stage1()
    for g, (fa0, nj) in enumerate(FGROUPS):
        m = g
        Mt = h['Mt'][m]
        b0 = 4 * (g % 2)
        for j in range(nj):
            fa = fa0 + j
            bre = b0 + j // 4
            bim = b0 + 2 + j // 4
            col = (j % 4) * 128
            for (bank, la, lb) in ((bre, 0, 2), (bim, 1, 0)):
                S.op('pe', lambda e, bank=bank, la=la, j=j, fa=fa, col=col, Mt=Mt: e.matmul(
                    PS[bank][:, col:col + 128], lhsT=Mt[:, la, j, :], rhs=A[:, 0, fa, :], start=True, stop=False),
                    reads=['Mt%d' % m, 'A'], writes=['ps%d' % bank])
                S.op('pe', lambda e, bank=bank, lb=lb, j=j, fa=fa, col=col, Mt=Mt: e.matmul(
                    PS[bank][:, col:col + 128], lhsT=Mt[:, lb, j, :], rhs=A[:, 1, fa, :], start=False, stop=True),
                    reads=['Mt%d' % m, 'A'], writes=['ps%d' % bank])
        on_group(g, b0, fa0, nj)


def kf_row(l, o, cc, g):
    base = (l * 2 + o) * 2560
    if g < 4:
        return base + (g // 2) * 1024 + cc * 256 + (g % 2) * 128
    return base + 2048 + cc * 128


def phase_filters(ctx):
    S, AR, PS, ins = ctx['S'], ctx['AR'], ctx['PS'], ctx['ins']
    AR.mark()
    h = hy_alloc(ctx)
    h2T = [AR.alloc([64, NFFT], BF16, 'h2T%d' % i) for i in range(2)]
    tdb = AR.alloc([128, NFFT], F32, 'tdb')
    pos = [AR.alloc([33, 512], F32, 'pos%d' % i) for i in range(2)]
    w1 = [AR.alloc([33, 64], F32, 'fw1%d' % i) for i in range(2)]
    w2 = [AR.alloc([64, 64], F32, 'fw2%d' % i) for i in range(2)]
    w3 = [AR.alloc([64, 512], BF16, 'fw3%d' % i) for i in range(2)]
    prm = AR.alloc([64, 2, 8], F32, 'prm')
    negd = AR.alloc([128, 2, 8], F32, 'negd')
    arg = [AR.alloc([64, 512], F32, 'arg%d' % i) for i in range(2)]
    argi = [AR.alloc([64, 512], mybir.dt.int32, 'argi%d' % i) for i in range(2)]
    argf = [AR.alloc([64, 512], F32, 'argf%d' % i) for i in range(2)]
    h1 = [AR.alloc([64, 512], F32, 'h1%d' % i) for i in range(2)]
    ex = [AR.alloc([128, 512], F32, 'ex%d' % i) for i in range(2)]
    kt = [AR.alloc([128, 512], F32, 'kt%d' % i) for i in range(2)]
    kk = AR.alloc([128, NFFT], BF16, 'kk')
    acc = AR.alloc([128, 4, 20], F32, 'acc')
    kfo = [AR.alloc([128, 2, 8, 128], BF16, 'kfo%d' % i) for i in range(2)]
    PI = math.pi
    G4 = [[0, 1, 2, 3], [4, 5, 6, 7]]
    for k in range(2):
        S.op('pool', lambda e, k=k: e.memset(kfo[k][:, :, :, :], 0.0), writes=['kfo%d' % k])
    for l in range(DEPTH):
        S.dma('sp', w1[l][:, :], ins['hy_w1'][l], writes=['fw%d' % l])
        S.dma('sp', w2[l][:, :], ins['hy_w2'][l], writes=['fw%d' % l])
        wload(ctx, w3[l][:, :], ins['hy_w3_sh'][l], 'fw3%d' % l)
        S.dma('sp', prm[:, l, 0:1], ins['hy_freq'][l].rearrange("(p o) -> p o", o=1), writes=['prm%d' % l])
        S.dma('sp', prm[:, l, 1:2], ins['hy_b1'][l].rearrange("(p o) -> p o", o=1), writes=['prm%d' % l])
        S.dma('sp', prm[:, l, 2:3], ins['hy_b2'][l].rearrange("(p o) -> p o", o=1), writes=['prm%d' % l])
        S.dma('sp', negd[:, l, 0:4], ins['hy_decay_sh'][l].rearrange("(q p) -> p q", p=128), writes=['negd%d' % l],
              allow_slow_non_contiguous=True)
    S.dma('sp', tdb[:, :], ins['t_tdec'][0:1, :].to_broadcast([128, NFFT]), writes=['tdb'])
    for l in range(DEPTH):
        S.op('dve', lambda e, l=l: e.tensor_scalar(out=prm[:, l, 3:4], in0=prm[:, l, 0:1], scalar1=1.0 / (2.0 * PI),
                                                   scalar2=None, op0=ALU.mult), reads=['prm%d' % l],
             writes=['prm%d' % l])
        S.op('dve', lambda e, l=l: e.tensor_tensor(out=prm[:, l, 4:6], in0=prm[:, l, 1:3],
                                                   in1=prm[:, l, 3:4].to_broadcast([64, 2]), op=ALU.mult),
             reads=['prm%d' % l], writes=['prm%d' % l])
        S.op('act', lambda e, l=l: e.activation(out=negd[:, l, 4:8], in_=negd[:, l, 0:4], func=AF.Abs),
             reads=['negd%d' % l], writes=['negd%d' % l])
        S.op('dve', lambda e, l=l: e.tensor_scalar(out=negd[:, l, 0:4], in0=negd[:, l, 4:8], scalar1=-1.0,
                                                   scalar2=None, op0=ALU.mult), reads=['negd%d' % l],
             writes=['negd%d' % l])
    for sb in range(16):
        pb = sb % 2
        S.dma('sp', pos[pb][:, :], ins['t_pos'][:, sb * 512:(sb + 1) * 512], writes=['pos%d' % pb])
        for stage in range(2):
            for l in range(DEPTH):
                b = l
                if stage == 0:
                    src, skey, wt, kq = pos[pb], 'pos%d' % pb, w1[l], 33
                else:
                    src, skey, wt, kq = h1[b], 'h1%d' % b, w2[l], 64
                bank = 6 + b
                a, ai, af = arg[b], argi[b], argf[b]
                S.op('pe', lambda e, src=src, wt=wt, bank=bank, kq=kq: e.matmul(
                    PS[bank][0:64, :], lhsT=wt[0:kq, :], rhs=src[0:kq, :], start=True, stop=True),
                    reads=['fw%d' % l, skey], writes=['ps%d' % bank])
                S.op('act', lambda e, stage=stage, a=a, bank=bank, l=l: e.activation(
                    out=a[:, :], in_=PS[bank][0:64, :], func=AF.Identity, scale=prm[:, l, 3:4],
                    bias=prm[:, l, 4 + stage:5 + stage]),
                    reads=['ps%d' % bank, 'prm%d' % l], writes=['arg%d' % b])
                S.op('dve', lambda e, a=a, ai=ai: e.tensor_copy(out=ai[:, :], in_=a[:, :]), reads=['arg%d' % b],
                     writes=['argi%d' % b])
                S.op('dve', lambda e, ai=ai, af=af: e.tensor_copy(out=af[:, :], in_=ai[:, :]), reads=['argi%d' % b],
                     writes=['argf%d' % b])
                S.op('dve', lambda e, a=a, af=af: e.tensor_tensor(out=a[:, :], in0=a[:, :], in1=af[:, :],
                                                                  op=ALU.subtract),
                     reads=['arg%d' % b, 'argf%d' % b], writes=['arg%d' % b])
                if stage == 0:
                    S.op('act', lambda e, a=a, b=b: e.activation(out=h1[b][:, :], in_=a[:, :], func=AF.Sin,
                                                                 scale=2.0 * PI * (1.0 - 1e-6)),
                         reads=['arg%d' % b], writes=['h1%d' % b])
                else:
                    S.op('act', lambda e, a=a, sb=sb, l=l: e.activation(
                        out=h2T[l][:, sb * 512:(sb + 1) * 512], in_=a[:, :], func=AF.Sin,
                        scale=2.0 * PI * (1.0 - 1e-6)), reads=['arg%d' % b], writes=['h2T%d' % l])
    kl = ctx['kf_loc']
    kd = ctx['kf_d']
    it = 0
    for l in range(DEPTH):
        for o in range(2):
            kb = it % 2
            it += 1
            for sb in range(16):
                dr = 0 if sb < 8 else 1
                q = o * 2 + dr
                t = sb % 2
                bank = 4 + t
                S.op('pe', lambda e, q=q, sb=sb, bank=bank, l=l: e.matmul(
                    PS[bank][:, :], lhsT=w3[l][:, q * 128:(q + 1) * 128], rhs=h2T[l][:, sb * 512:(sb + 1) * 512],
                    start=True, stop=True), reads=['fw3%d' % l, 'h2T%d' % l], writes=['ps%d' % bank])
                S.op('act', lambda e, t=t, sb=sb, q=q, l=l: e.activation(
                    out=ex[t][:, :], in_=tdb[:, sb * 512:(sb + 1) * 512], func=AF.Exp, scale=negd[:, l, q:q + 1]),
                    reads=['tdb', 'negd%d' % l], writes=['ex%d' % t])
                S.op('dve', lambda e, t=t, bank=bank: e.tensor_tensor(out=kt[t][:, :], in0=PS[bank][:, :],
                                                                      in1=ex[t][:, :], op=ALU.mult),
                     reads=['ps%d' % bank, 'ex%d' % t], writes=['kt%d' % t])
                S.op('dve', lambda e, sb=sb, t=t, aq=l * 2 + o: e.tensor_reduce(
                    out=acc[:, aq, sb:sb + 1], in_=kt[t][:, :], axis=AX.X, op=ALU.add,
                    apply_absolute_value=True), reads=['kt%d' % t], writes=['acc'])
                S.op('act', lambda e, sb=sb, t=t: e.copy(out=kk[:, sb * 512:(sb + 1) * 512], in_=kt[t][:, :]),
                     reads=['kt%d' % t], writes=['kk'])
            ai_ = l * 2 + o
            S.op('dve', lambda e, ai_=ai_: e.tensor_reduce(out=acc[:, ai_, 16:17], in_=acc[:, ai_, 0:16], axis=AX.X,
                                                           op=ALU.add), reads=['acc'], writes=['acc'])
            S.op('dve', lambda e, ai_=ai_: e.tensor_scalar(out=acc[:, ai_, 17:18], in0=acc[:, ai_, 16:17],
                                                           scalar1=float(NFFT), scalar2=None, op0=ALU.mult),
                 reads=['acc'], writes=['acc'])
            S.op('dve', lambda e, ai_=ai_: e.reciprocal(out=acc[:, ai_, 18:19], in_=acc[:, ai_, 17:18]),
                 reads=['acc'], writes=['acc'])
            S.dma('sp', ctx['rn_loc'][ai_].rearrange("(p o) -> p o", o=1), acc[:, ai_, 18:19], reads=['acc'],
                  writes=['rn_loc'])
            S.dma('act', ctx['kk_d'][kb], kk[:, :], reads=['kk'], writes=['kk_d%d' % kb])
            S.dma('sp', h['InB'][0:64, :].rearrange("p (c s) -> p c s", s=128),
                  ctx['kk_d'][kb].rearrange("c (t s) -> t c s", s=128), reads=['kk_d%d' % kb], writes=['InB'])

            def store(g, b0, fa0, nj, l=l, o=o):
                k = g % 2
                for bi in range(4):
                    r, jh = bi // 2, bi % 2
                    n = min(4, nj - jh * 4)
                    if n <= 0:
                        continue
                    S.op('act', lambda e, bi=bi, r=r, jh=jh, k=k, n=n: e.copy(
                        out=kfo[k][:, r, jh * 4:jh * 4 + n, :],
                        in_=PS[b0 + bi][:, 0:n * 128].rearrange("p (j c) -> p j c", j=n)),
                        reads=['ps%d' % (b0 + bi)], writes=['kfo%d' % k])
                r0 = ((l * 2 + o) * 5 + g) * 128
                S.dma('act', kl[r0:r0 + 128, :].rearrange("p (r j c) -> p r j c", r=2, j=8), kfo[k][:, :, :, :],
                      reads=['kfo%d' % k], writes=['kf_loc'])
            fwd_transform(ctx, h, 64, h['w1f'], 'w1f', store)
            lb = (l * 2 + o) * 640
            db = (l * 2 + o) * 2560
            for (lo_, n_, do_) in ((0, 256, 0), (256, 256, 1024), (512, 128, 2048)):
                S.allgather(kl[lb + lo_:lb + lo_ + n_, :], kd[db + do_:db + do_ + 4 * n_, :],
                            reads=['kf_loc'], writes=['kf_d'], groups=G4, big=True)
    S.allgather(ctx['rn_loc'][:, :], ctx['rn_d'][:, :], reads=['rn_loc'], writes=['rn_d'], groups=G4, big=True)
    S.sticky.update(['kf_d', 'rn_d'])
    S.barrier()
    AR.release()


def phase_hyena(ctx, l):
    S, AR, PS, ins = ctx['S'], ctx['AR'], ctx['PS'], ctx['ins']
    AR.mark()
    h = hy_alloc(ctx)
    InB = h['InB']
    NS = 2 * NA
    Y1 = AR.alloc([128, 128, NS], BF16, 'Y1')
    Y2 = AR.alloc([128, 128, NS], BF16, 'Y2')
    kfg = [AR.alloc([128, 2, 8, 128], BF16, 'kfg%d' % i) for i in range(2)]
    tmp = [AR.alloc([128, 512], F32, 'tmp%d' % i) for i in range(4)]
    G = AR.alloc([128, 3 * 128], BF16, 'G')
    H = AR.alloc([NS, 2048], BF16, 'H')
    rn = AR.alloc([128, 16], F32, 'rn')
    bia = AR.alloc([128, 2, 4], F32, 'bia')
    Bb = AR.alloc([NS, 128 * 128], BF16, 'Bb')
    gt = [AR.alloc([128, T], BF16, 'gt%d' % i) for i in range(2)]
    vt = [AR.alloc([128, T], BF16, 'vt%d' % i) for i in range(2)]
    bv = AR.alloc([128, T], F32, 'bv')
    yt = AR.alloc([128, T], F32, 'yt')
    ut = AR.alloc([128, T], BF16, 'ut')
    S.dma('sp', G[:, :], ins['t_g'][:, :], writes=['G'])
    S.dma('sp', H[:, :], ins['t_h'][:, :], writes=['H'])
    S.dma('sp', rn[:, :], ctx['rn_d'].rearrange("i p -> p i"), reads=['rn_d'], writes=['rn'],
          allow_slow_non_contiguous=True)
    for o in range(2):
        S.dma('sp', bia[:, o, :], ins['hy_bias'][l][o].rearrange("(c p) -> p c", p=128), writes=['bia'],
              allow_slow_non_contiguous=True)
    it = 0
    for o in range(2):
        src_all, src_own, skey_all, skey_own = ((ctx['hv_all'], ctx['hv_in'], 'hv_all', 'hv_in') if o == 0 else
                                                (ctx['hu_all'], ctx['hu_in'], 'hu_all', 'hu_in'))

        def load_in(cc, src_all=src_all, skey_all=skey_all):
            for r in range(2):
                S.dma('sp', InB[16 * r:16 * r + 16, :], src_all[r * 64 + cc * 16:r * 64 + (cc + 1) * 16, :],
                      reads=[skey_all], writes=['InB'])
        load_in(0)
        for cc in range(4):
            ib = it % 2
            it += 1
            S.dma('sp', gt[ib][:, :], ctx['g12_d'][o, cc * 128:(cc + 1) * 128, :], reads=['g12_d'],
                  writes=['gt%d' % ib])
            S.dma('sp', vt[ib][:, :].rearrange("c (t s) -> c t s", s=128),
                  src_own[cc * 16:(cc + 1) * 16, :].rearrange("t (c s) -> c t s", s=128), reads=[skey_own],
                  writes=['vt%d' % ib])

            def load_kf(g, l=l, o=o, cc=cc):
                k = g % 2
                nj_ = FGROUPS[g][1]
                r0 = kf_row(l, o, cc, g)
                S.dma('sp', kfg[k][:, :, 0:nj_, :],
                      ctx['kf_d'][r0:r0 + 128, :].rearrange("p (r j c) -> p r j c", r=2, j=8)[:, :, 0:nj_, :],
                      reads=['kf_d'], writes=['kfg%d' % k])

            def pre2(load_kf=load_kf):
                load_kf(0)
                load_kf(1)

            def prod(g, b0, fa0, nj, l=l, o=o, cc=cc, load_kf=load_kf):
                k = g % 2
                for hb in range((nj + 3) // 4):
                    n = min(4, nj - hb * 4)
                    xre, xim = PS[b0 + hb], PS[b0 + 2 + hb]
                    kre = kfg[k][:, 0, hb * 4:hb * 4 + n, :]
                    kim = kfg[k][:, 1, hb * 4:hb * 4 + n, :]
                    v3 = lambda t, n=n: t[:, 0:n * 128].rearrange("p (j c) -> p j c", j=n)
                    for ti, (xa, ka) in enumerate(((xre, kre), (xim, kim), (xre, kim), (xim, kre))):
                        S.op('dve', lambda e, ti=ti, xa=xa, ka=ka, v3=v3: e.tensor_tensor(
                            out=v3(tmp[ti]), in0=v3(xa), in1=ka, op=ALU.mult),
                            reads=['ps%d' % (b0 + hb), 'ps%d' % (b0 + 2 + hb), 'kfg%d' % k], writes=['tmp%d' % ti])
                    f0 = fa0 + hb * 4
                    v3t = lambda t, n=n: t[:, 0:n * 128].rearrange("p (j c) -> p c j", j=n)
                    S.op('pool', lambda e, f0=f0, n=n, v3t=v3t: e.tensor_tensor(
                        out=Y1[:, :, f0:f0 + n], in0=v3t(tmp[0]), in1=v3t(tmp[1]), op=ALU.subtract),
                        reads=['tmp0', 'tmp1'], writes=['Y1'])
                    S.op('dve', lambda e, f0=f0, n=n, v3t=v3t: e.tensor_tensor(
                        out=Y1[:, :, NA + f0:NA + f0 + n], in0=v3t(tmp[2]), in1=v3t(tmp[3]), op=ALU.add),
                        reads=['tmp2', 'tmp3'], writes=['Y1'])
                    S.op('act', lambda e, f0=f0, n=n: e.activation(out=Y2[:, :, f0:f0 + n],
                                                                   in_=Y1[:, :, NA + f0:NA + f0 + n], func=AF.Copy,
                                                                   scale=-1.0), reads=['Y1'], writes=['Y2'])
                    S.op('act', lambda e, f0=f0, n=n: e.copy(out=Y2[:, :, NA + f0:NA + f0 + n],
                                                             in_=Y1[:, :, f0:f0 + n]), reads=['Y1'], writes=['Y2'])
                if g + 2 < len(FGROUPS):
                    load_kf(g + 2)

            def after1(cc=cc, load_in=load_in):
                if cc < 3:
                    load_in(cc + 1)
            fwd_transform(ctx, h, 32, h['w1d'], 'w1d', prod, after_stage1=after1, pre_stage2=pre2)
            Bv = Bb[0:NS, :].rearrange("p (t c) -> p t c", t=128)
            S.op('dve', lambda e, o=o, cc=cc, ib=ib: e.tensor_scalar(
                out=bv[:, :], in0=vt[ib][:, :], scalar1=bia[:, o, cc:cc + 1], scalar2=None, op0=ALU.mult),
                reads=['vt%d' % ib, 'bia'], writes=['bv'])
            for c4 in range(32):
                bank = c4 % 4
                for ci in range(4):
                    c = c4 * 4 + ci
                    S.op('pe', lambda e, bank=bank, c=c, ci=ci: e.matmul(
                        PS[bank][0:NS, ci * 128:(ci + 1) * 128], lhsT=Y1[:, c, :], rhs=G[:, 0:128],
                        start=True, stop=False), reads=['Y1', 'G'], writes=['ps%d' % bank])
                    S.op('pe', lambda e, bank=bank, c=c, ci=ci: e.matmul(
                        PS[bank][0:NS, ci * 128:(ci + 1) * 128], lhsT=Y2[:, c, :], rhs=G[:, 128:256],
                        start=False, stop=True), reads=['Y2', 'G'], writes=['ps%d' % bank])
                srcv = PS[bank][0:NS, :].rearrange("p (c t) -> p t c", c=4)
                if c4 % 2 == 0:
                    S.op('act', lambda e, c4=c4, srcv=srcv: e.copy(out=Bv[:, :, c4 * 4:c4 * 4 + 4], in_=srcv),
                         reads=['ps%d' % bank], writes=['Bb'])
                else:
                    S.op('dve', lambda e, c4=c4, srcv=srcv: e.tensor_copy(out=Bv[:, :, c4 * 4:c4 * 4 + 4], in_=srcv),
                         reads=['ps%d' % bank], writes=['Bb'])
            for tA in range(128):
                bank = 4 + tA // 32
                col = (tA % 32) * 16
                S.op('pe', lambda e, tA=tA, bank=bank, col=col: e.matmul(
                    PS[bank][:, col:col + 16], lhsT=Bv[:, tA, :], rhs=H[:, tA * 16:(tA + 1) * 16],
                    start=True, stop=True), reads=['Bb', 'H'], writes=['ps%d' % bank])
            ridx = cc * 4 + l * 2 + o
            for b in range(4):
                dst = lambda t, b=b: t[:, :].rearrange("p (tb ta) -> p ta tb", ta=128)[:, 32 * b:32 * b + 32, :]
                S.op('dve', lambda e, b=b, dst=dst, ridx=ridx: e.scalar_tensor_tensor(
                    out=dst(yt), in0=PS[4 + b][:, :].rearrange("p (ta tb) -> p ta tb", tb=16),
                    scalar=rn[:, ridx:ridx + 1], in1=dst(bv), op0=ALU.mult, op1=ALU.add),
                    reads=['ps%d' % (4 + b), 'rn', 'bv'], writes=['yt'])
            S.op('pool', lambda e, ib=ib: e.tensor_tensor(out=ut[:, :], in0=yt[:, :], in1=gt[ib][:, :], op=ALU.mult),
                 reads=['yt', 'gt%d' % ib], writes=['ut'])
            if o == 0:
                S.dma('pool', ctx['hu_in'][cc * 16:(cc + 1) * 16, :].rearrange("t (c s) -> c t s", s=128),
                      ut[:, :].rearrange("c (t s) -> c t s", s=128), reads=['ut'], writes=['hu_in'])
            else:
                S.dma('pool', ctx['mixT_d'][1536 + cc * 128:1536 + (cc + 1) * 128, :], ut[:, :], reads=['ut'],
                      writes=['mixT_d'])
        if o == 0:
            S.allgather(ctx['hu_in'][:, :], ctx['hu_all'][:, :], reads=['hu_in'], writes=['hu_all'])
    S.barrier()
    AR.release()
```

```python
import math
import numpy as np
import ml_dtypes
import concourse.bass as bass
import concourse.mybir as mybir
from concourse.bass_utils import run_bass_kernel_spmd

F32 = mybir.dt.float32
BF16 = mybir.dt.bfloat16
AF = mybir.ActivationFunctionType
ALU = mybir.AluOpType
AX = mybir.AxisListType

D = 2048
DEPTH = 2
T = 2048
HALO = 128
TE = T + 2 * HALO
DIN = 4352
DFF = 5632
NFF = DFF // 128
EPS = 1e-6
NFFT = 8192
PAIRS = [[0, 1], [2, 3], [4, 5], [6, 7]]
ENGS = ['pe', 'act', 'dve', 'pool', 'sp']


class Sched:
    def __init__(self, nc, es):
        self.nc = nc
        self.ops = {e: [] for e in ENGS}
        self.cnt = {e: 0 for e in ENGS}
        self.sem = {e: es.enter_context(nc.semaphore('s_' + e)) for e in ['pe', 'act', 'dve', 'pool']}
        self.R = 8
        self.dsem = {q: [es.enter_context(nc.semaphore('d_%s%d' % (q, i))) for i in range(self.R)]
                     for q in ['sp', 'act', 'pool']}
        self.dcnt = {q: 0 for q in ['sp', 'act', 'pool']}
        self.ccsem = es.enter_context(nc.semaphore('ccsem'))
        self.cccnt = 0
        self.ccsem2 = es.enter_context(nc.semaphore('ccsem2'))
        self.sticky = set()
        self.waited = {e: {} for e in ENGS}
        self.lastw = {}
        self.reads = {}
        self.same_engine_sync = True

    def _wait(self, eng, tok):
        sk, sh, val = tok
        if self.waited[eng].get(sk, 0) >= val:
            return
        if sk == eng and (eng == 'pe' or not self.same_engine_sync):
            return
        self.waited[eng][sk] = val
        self.ops[eng].append(lambda e, sh=sh, val=val: e.wait_ge(sh, val))

    def _deps(self, eng, reads, writes):
        for k in reads:
            if k in self.lastw:
                self._wait(eng, self.lastw[k])
        for k in writes:
            if k in self.lastw:
                self._wait(eng, self.lastw[k])
            for tok in self.reads.get(k, {}).values():
                self._wait(eng, tok)

    def _commit(self, tok, reads, writes):
        for k in writes:
            self.lastw[k] = tok
            self.reads[k] = {}
        for k in reads:
            d = self.reads.setdefault(k, {})
            if tok[0] not in d or d[tok[0]][2] < tok[2]:
                d[tok[0]] = tok

    def op(self, eng, fn, reads=(), writes=()):
        self._deps(eng, reads, writes)
        self.cnt[eng] += 1
        val = self.cnt[eng]
        sh = self.sem[eng]
        self.ops[eng].append(lambda e, fn=fn, sh=sh: fn(e).then_inc(sh, 1))
        self._commit((eng, sh, val), reads, writes)

    def dma(self, q, out, in_, reads=(), writes=(), **kw):
        i = self.dcnt[q]
        self.dcnt[q] += 1
        slot = i % self.R
        val = 16 * (i // self.R + 1)
        sh = self.dsem[q][slot]
        sk = 'd_%s%d' % (q, slot)
        if val > 16:
            self._wait(q, (sk, sh, val - 16))
        self._deps(q, reads, writes)
        self.ops[q].append(lambda e, out=out, in_=in_, sh=sh, kw=kw:
                           e.dma_start(out=out, in_=in_, **kw).then_inc(sh, 16))
        self._commit((sk, sh, val), reads, writes)

    def allgather(self, in_ap, out_ap, reads=(), writes=(), groups=None, big=False):
        q = 'pool'
        self._deps(q, reads, writes)
        if big:
            self.cc2cnt = getattr(self, 'cc2cnt', 0) + 1
            sh, sk, val = self.ccsem2, 'cc2', self.cc2cnt
        else:
            self.cccnt += 1
            sh, sk, val = self.ccsem, 'cc', self.cccnt
        groups = groups or PAIRS
        self.ops[q].append(lambda e, sh=sh, groups=groups: e.collective_compute(
            "AllGather", ALU.bypass, replica_groups=groups,
            ins=[in_ap], outs=[out_ap]).then_inc(sh))
        self._commit((sk, sh, val), reads, writes)

    def barrier(self):
        toks = []
        for e in ['pe', 'act', 'dve', 'pool']:
            if self.cnt[e] > 0:
                toks.append((e, self.sem[e], self.cnt[e]))
        for q in ['sp', 'act', 'pool']:
            n = self.dcnt[q]
            for slot in range(self.R):
                uses = (n - slot + self.R - 1) // self.R if n > slot else 0
                if uses > 0:
                    toks.append(('d_%s%d' % (q, slot), self.dsem[q][slot], 16 * uses))
        if self.cccnt:
            toks.append(('cc', self.ccsem, self.cccnt))
        for e in ENGS:
            for tok in toks:
                if tok[0] == e and e == 'pe':
                    continue
                sk, sh, val = tok
                if self.waited[e].get(sk, 0) >= val:
                    continue
                self.waited[e][sk] = val
                self.ops[e].append(lambda en, sh=sh, val=val: en.wait_ge(sh, val))
        self.lastw = {k: v for k, v in self.lastw.items() if k in self.sticky}
        self.reads = {}

    def emit(self):
        self.barrier()
        nc = self.nc
        ops = self.ops
        with nc.Block() as block:
            @block.tensor
            def _(e):
                for f in ops['pe']:
                    f(e)

            @block.scalar
            def _(e):
                for f in ops['act']:
                    f(e)

            @block.vector
            def _(e):
                for f in ops['dve']:
                    f(e)

            @block.gpsimd
            def _(e):
                for f in ops['pool']:
                    f(e)

            @block.sync
            def _(e):
                for f in ops['sp']:
                    f(e)


class Arena:
    def __init__(self, nc, base, limit):
        self.nc = nc
        self.base = base
        self.limit = limit
        self.off = base
        self.n = 0
        self.marks = []

    def mark(self):
        self.marks.append(self.off)

    def release(self):
        self.off = self.marks.pop()

    def alloc(self, shape, dtype, name=None):
        nbytes = int(np.prod(shape[1:])) * (2 if dtype == BF16 else 4)
        nbytes = (nbytes + 63) // 64 * 64
        off = self.off
        assert off + nbytes <= self.limit, ("SBUF arena overflow", name, off, nbytes, self.limit)
        self.off += nbytes
        self.n += 1
        return self.nc.alloc_sbuf_tensor_at("%s_%d" % (name or 't', self.n), list(shape), dtype, offset=off)


def sap(t, part0, nparts, dims, off=0):
    full = t[:]
    pstep = full.ap[0][0]
    return bass.AP(t, full.offset + part0 * pstep + off, [[pstep, nparts]] + [[s, c] for (s, c) in dims])


def build(dbg=None):
    nc = bass.Bass("TRN2", target_bir_lowering=False)
    from contextlib import ExitStack
    es = ExitStack()
    S = Sched(nc, es)
    dbg = dbg or {}
    ins = {}

    def din(name, shape, dt=F32):
        ins[name] = nc.dram_tensor(name, list(shape), dt, kind="ExternalInput")
        return ins[name]

    x_ext = din("x_ext", [TE, D])
    mix_g = din("mix_norm_g", [DEPTH, D])
    w_in = din("w_in", [DEPTH, D, DIN])
    pool_w = din("pool_w", [DEPTH, 4, 128, 128])
    pool_scale = din("pool_scale", [DEPTH, 512])
    attn_sink = din("attn_sink", [DEPTH, 8])
    sconv_w = din("sconv_w", [DEPTH, 3, 512])
    hy_short_w = din("hy_short_w", [DEPTH, 3, 1536])
    hy_w1 = din("hy_w1", [DEPTH, 33, 64])
    hy_b1 = din("hy_b1", [DEPTH, 64])
    hy_w2 = din("hy_w2", [DEPTH, 64, 64])
    hy_b2 = din("hy_b2", [DEPTH, 64])
    hy_w3 = din("hy_w3_sh", [DEPTH, 64, 512])
    hy_freq = din("hy_freq", [DEPTH, 64])
    hy_decay = din("hy_decay_sh", [DEPTH, 512])
    hy_bias = din("hy_bias", [DEPTH, 2, 512])
    w_out = din("w_out", [DEPTH, D, D])
    ffn_g = din("ffn_norm_g", [DEPTH, D])
    w_gu = din("w_gate_up", [DEPTH, D, 2 * DFF])
    w_down = din("w_down", [DEPTH, DFF, D])
    fin_g = din("final_norm_g", [D])
    t_ident = din("t_ident", [128, 128], BF16)
    t_rope = din("t_rope", [2, 16, TE])
    t_perm = din("t_perm", [16, 16])
    t_mask = din("t_mask", [4, 128, 128], BF16)
    t_invcnt = din("t_invcnt", [4, T])
    t_flags = din("t_flags", [128, 2])
    t_w1d = din("t_w1d", [32, 66], BF16)
    t_w1f = din("t_w1f", [64, 66], BF16)
    t_m = din("t_m", [5, 128, 3 * 8 * 128], BF16)
    t_g = din("t_g", [128, 3 * 128], BF16)
    t_h = din("t_h", [66, 128 * 16], BF16)
    t_pos = din("t_pos", [33, NFFT])
    t_tdec = din("t_tdec", [1, NFFT])

    y_out = nc.dram_tensor("y_out", [T, D], F32, kind="ExternalOutput")
    dbg_out = {}
    for k, shp in dbg.items():
        dbg_out[k] = nc.dram_tensor("dbg_" + k, list(shp[0]), shp[1], kind="ExternalOutput")

    x1_ext = nc.dram_tensor("x1_ext", [TE, D], F32)
    xmid = nc.dram_tensor("xmid", [T, D], F32)
    mixT_d = nc.dram_tensor("mixT_d", [D, T], BF16)
    edge_in = nc.dram_tensor("edge_in", [256, D], F32)
    edge_all = nc.dram_tensor("edge_all", [512, D], F32)
    hv_in = nc.dram_tensor("hv_in", [64, 128 * 128], BF16)
    hv_all = nc.dram_tensor("hv_all", [128, 128 * 128], BF16)
    hu_in = nc.dram_tensor("hu_in", [64, 128 * 128], BF16)
    hu_all = nc.dram_tensor("hu_all", [128, 128 * 128], BF16)
    g12_d = nc.dram_tensor("g12_d", [2, 512, T], BF16)
    kk_d = nc.dram_tensor("kk_d", [2, 128, NFFT], BF16)
    kf_d = nc.dram_tensor("kf_d", [4 * 2560, 2 * 8 * 128], BF16)
    kf_loc = nc.dram_tensor("kf_loc", [4 * 640, 2 * 8 * 128], BF16)
    rn_loc = nc.dram_tensor("rn_loc", [4, 128], F32)
    rn_d = nc.dram_tensor("rn_d", [16, 128], F32)

    AR = Arena(nc, 16640, 229312)
    PS = [nc.alloc_psum_tensor("ps%d" % i, [128, 512], F32) for i in range(8)]

    def psb(i):
        return PS[i]

    ident = AR.alloc([128, 128], BF16, "ident")
    S.dma('sp', ident[:, :], t_ident[:, :], writes=['ident'])
    gcolM = AR.alloc([128, DEPTH, 16], F32, "gcolM")
    gcolF = AR.alloc([128, DEPTH, 16], F32, "gcolF")
    for l in range(DEPTH):
        S.dma('sp', gcolM[:, l, :], mix_g[l].rearrange("(k p) -> p k", p=128), writes=['gcol'],
              allow_slow_non_contiguous=True)
        S.dma('sp', gcolF[:, l, :], ffn_g[l].rearrange("(k p) -> p k", p=128), writes=['gcol'],
              allow_slow_non_contiguous=True)
    ctx = dict(nc=nc, S=S, AR=AR, PS=PS, ident=ident, gcolM=gcolM, gcolF=gcolF, ins=ins,
               dbg=dbg, dbg_out=dbg_out)
    ctx.update(x_ext=x_ext, x1_ext=x1_ext, xmid=xmid, mixT_d=mixT_d, y_out=y_out,
               edge_in=edge_in, edge_all=edge_all, hv_in=hv_in, hv_all=hv_all,
               hu_in=hu_in, hu_all=hu_all, g12_d=g12_d, kk_d=kk_d, kf_d=kf_d, kf_loc=kf_loc,
               rn_loc=rn_loc, rn_d=rn_d)
    return nc, S, ctx, es


def rms_to_T(ctx, xt, gcol_l, hT, col0, uid):
    S, AR, PS, ident = ctx['S'], ctx['AR'], ctx['PS'], ctx['ident']
    sc = ctx['sc_junk']
    ss = ctx['sc_ss']
    xn = ctx['sc_xn']
    S.op('act', lambda e: e.activation(out=sc[:, :], in_=xt[:, :], func=AF.Square, accum_out=ss[:, 0:1]),
         reads=[uid], writes=[ctx.get('junk_key', 'sc_junk'), 'sc_ss'])
    S.op('dve', lambda e: e.tensor_scalar(out=ss[:, 1:2], in0=ss[:, 0:1], scalar1=1.0 / D, scalar2=EPS,
                                          op0=ALU.mult, op1=ALU.add), reads=['sc_ss'], writes=['sc_ss1'])
    S.op('act', lambda e: e.activation(out=ss[:, 3:4], in_=ss[:, 1:2], func=AF.Sqrt),
         reads=['sc_ss1'], writes=['sc_ss3'])
    S.op('dve', lambda e: e.reciprocal(out=ss[:, 2:3], in_=ss[:, 3:4]), reads=['sc_ss3'], writes=['sc_ss2'])
    S.op('act', lambda e: e.activation(out=xn[:, :], in_=xt[:, :], func=AF.Copy, scale=ss[:, 2:3]),
         reads=[uid, 'sc_ss2'], writes=['sc_xn'])
    for q in range(4):
        bank = ctx['tp_bank'][ctx['tp_i'] % 2]
        ctx['tp_i'] += 1
        pst = PS[bank][:, :].bitcast(BF16)
        for j in range(4):
            kc = q * 4 + j
            S.op('pe', lambda e, kc=kc, j=j, pst=pst: e.transpose(out=pst[:, j * 128:(j + 1) * 128],
                                                                    in_=xn[:, kc * 128:(kc + 1) * 128],
                                                                    identity=ident[:, :]),
                 reads=['sc_xn', 'ident'], writes=['ps%d' % bank])
        S.op('dve', lambda e, q=q, pst=pst: e.tensor_tensor(
            out=hT[:, q * 4:(q + 1) * 4, col0:col0 + 128],
            in0=pst[:, 0:512].rearrange("p (a b) -> p a b", a=4),
            in1=gcol_l[:, q * 4:(q + 1) * 4].unsqueeze(2).to_broadcast([128, 4, 128]),
            op=ALU.mult), reads=['ps%d' % bank, 'gcol'], writes=['hT'])


def wload(ctx, dst, src, key):
    ctx['S'].dma('pool', dst, src, writes=[key])


def proj(ctx, wblk, wkey, c0, M, hT, tok0, ntok, bank):
    S, PS = ctx['S'], ctx['PS']
    for kc in range(16):
        S.op('pe', lambda e, kc=kc: e.matmul(PS[bank][0:M, 0:ntok], lhsT=wblk[:, kc, c0:c0 + M],
                                               rhs=hT[:, kc, tok0:tok0 + ntok],
                                               start=(kc == 0), stop=(kc == 15)),
             reads=[wkey, 'hT'], writes=['ps%d' % bank])


TOKG = [(0, 512), (512, 512), (1024, 512), (1536, 512), (2048, 256)]


def phase_norm(ctx, l, xsrc, tiles=None):
    S, AR = ctx['S'], ctx['AR']
    hT = ctx['hT']
    AR.mark()
    ctx['sc_junk'] = AR.alloc([128, D], F32, 'junk')
    ctx['junk_key'] = 'sc_junk'
    ctx['sc_ss'] = AR.alloc([128, 4], F32, 'ss')
    ctx['sc_xn'] = AR.alloc([128, D], BF16, 'xn')
    xt = [AR.alloc([128, D], F32, 'xt%d' % i) for i in range(2)]
    ctx['tp_bank'] = [0, 1]
    ctx['tp_i'] = 0
    for n, i in enumerate(tiles if tiles is not None else range(TE // 128)):
        b = n % 2
        S.dma('sp', xt[b][:, :], xsrc[i * 128:(i + 1) * 128, :], writes=['xt%d' % b])
        rms_to_T(ctx, xt[b], ctx['gcolM'][:, l, :], hT, i * 128, 'xt%d' % b)
    S.barrier()
    AR.release()


def conv3(ctx, eng, out, zin, wcol, c_lo, n, rkey, wkey_out):
    S = ctx['S']
    S.op(eng, lambda e: e.tensor_scalar(out=out[:, 0:n], in0=zin[:, c_lo:c_lo + n], scalar1=wcol[:, 1:2],
                                        scalar2=None, op0=ALU.mult), reads=[rkey, 'cw'], writes=[wkey_out])
    S.op(eng, lambda e: e.scalar_tensor_tensor(out=out[:, 0:n], in0=zin[:, c_lo - 1:c_lo - 1 + n],
                                               scalar=wcol[:, 0:1], in1=out[:, 0:n], op0=ALU.mult, op1=ALU.add),
         reads=[rkey, 'cw', wkey_out], writes=[wkey_out])
    S.op(eng, lambda e: e.scalar_tensor_tensor(out=out[:, 0:n], in0=zin[:, c_lo + 1:c_lo + 1 + n],
                                               scalar=wcol[:, 2:3], in1=out[:, 0:n], op0=ALU.mult, op1=ALU.add),
         reads=[rkey, 'cw', wkey_out], writes=[wkey_out])


def phase_local(ctx, l):
    S, AR, PS, ins = ctx['S'], ctx['AR'], ctx['PS'], ctx['ins']
    hT = ctx['hT']
    w_in = ins['w_in']
    AR.mark()
    wb = [AR.alloc([128, 16, 128], BF16, 'wb%d' % i) for i in range(3)]
    zc = [AR.alloc([128, TE], F32, 'zc%d' % i) for i in range(3)]
    sA = AR.alloc([128, TE], F32, 'sA')
    sB = AR.alloc([128, TE], F32, 'sB')
    invc = AR.alloc([128, T], F32, 'invc')
    dT = AR.alloc([128, T], BF16, 'dT')
    ob = [AR.alloc([128, T], BF16, 'ob%d' % i) for i in range(2)]
    pw = AR.alloc([128, 4, 128], BF16, 'pw')
    pscale = AR.alloc([128, 4], F32, 'pscale')
    cwC = AR.alloc([128, 4, 3], F32, 'cwC')
    cwD = AR.alloc([128, 12, 3], F32, 'cwD')
    wload(ctx, pw[:, :, :], ins['pool_w'][l].rearrange("g c d -> c g d"), 'pw')
    S.dma('sp', pscale[:, :], ins['pool_scale'][l].rearrange("(g p) -> p g", p=128), writes=['pscale'],
          allow_slow_non_contiguous=True)
    for k in range(3):
        S.dma('sp', cwC[:, :, k], ins['sconv_w'][l][k].rearrange("(c p) -> p c", p=128), writes=['cw'],
              allow_slow_non_contiguous=True)
        S.dma('sp', cwD[:, :, k], ins['hy_short_w'][l][k].rearrange("(c p) -> p c", p=128), writes=['cw'],
              allow_slow_non_contiguous=True)
    st = {'wi': 0, 'pb': 0, 'oi': 0}

    def zblock(blk, zi):
        w = st['wi'] % 3
        st['wi'] += 1
        wload(ctx, wb[w][:, :, :], w_in[l][:, blk * 128:(blk + 1) * 128].rearrange("(k p) c -> p k c", p=128),
              'wb%d' % w)
        for (t0, n) in TOKG:
            bank = 2 + st['pb'] % 4
            st['pb'] += 1
            proj(ctx, wb[w], 'wb%d' % w, 0, 128, hT, t0, n, bank)
            S.op('act', lambda e, bank=bank, t0=t0, n=n: e.copy(out=zc[zi][:, t0:t0 + n], in_=PS[bank][:, 0:n]),
                 reads=['ps%d' % bank], writes=['zc%d' % zi])

    def out_rows(row0, obuf, okey):
        S.dma('act', ctx['mixT_d'][row0:row0 + 128, :], obuf[:, :], reads=[okey], writes=['mixT_d'])

    for g in range(4):
        zblock(g, 0)
        z = zc[0]
        S.dma('sp', invc[:, :], ins['t_invcnt'][g:g + 1, :].to_broadcast([128, T]), writes=['invc'])
        S.op('dve', lambda e: e.tensor_tensor(out=sA[:, 1:TE], in0=z[:, 0:TE - 1], in1=z[:, 1:TE], op=ALU.add),
             reads=['zc0'], writes=['sA'])
        cur, ckey, oth, okey = sA, 'sA', sB, 'sB'
        vlo, vhi = 1, TE
        for lev in range(g):
            h = 1 << lev
            lo, hi = vlo + h, vhi - h
            S.op('dve', lambda e, cur=cur, oth=oth, h=h, lo=lo, hi=hi: e.tensor_tensor(
                out=oth[:, lo:hi], in0=cur[:, lo - h:hi - h], in1=cur[:, lo + h:hi + h], op=ALU.add),
                reads=[ckey], writes=[okey])
            vlo, vhi = lo, hi
            cur, ckey, oth, okey = oth, okey, cur, ckey
        S.op('dve', lambda e, cur=cur, oth=oth: e.tensor_tensor(out=oth[:, 0:T], in0=cur[:, HALO:HALO + T], in1=invc[:, :],
                                                       op=ALU.mult), reads=[ckey, 'invc'], writes=[okey])
        S.op('dve', lambda e, oth=oth: e.tensor_tensor(out=dT[:, :], in0=oth[:, 0:T], in1=z[:, HALO:HALO + T],
                                                       op=ALU.subtract), reads=[okey, 'zc0'], writes=['dT'])
        o = st['oi'] % 2
        st['oi'] += 1
        for q in range(4):
            bank = 6 + q % 2
            S.op('pe', lambda e, q=q, bank=bank, g=g: e.matmul(PS[bank][:, :], lhsT=pw[:, g, :],
                                                               rhs=dT[:, q * 512:(q + 1) * 512], start=True, stop=True),
                 reads=['pw', 'dT'], writes=['ps%d' % bank])
            S.op('act', lambda e, q=q, bank=bank, g=g, o=o: e.activation(
                out=ob[o][:, q * 512:(q + 1) * 512], in_=PS[bank][:, :], func=AF.Copy, scale=pscale[:, g:g + 1]),
                reads=['ps%d' % bank, 'pscale'], writes=['ob%d' % o])
        out_rows(g * 128, ob[o], 'ob%d' % o)

    for c in range(4):
        zblock(10 + c, 0)
        zblock(14 + c, 1)
        zblock(18 + c, 2)
        S.op('pool', lambda e: e.tensor_tensor(out=sA[:, :], in0=zc[2][:, :], in1=zc[0][:, :], op=ALU.mult),
             reads=['zc0', 'zc2'], writes=['sA'])
        conv3(ctx, 'dve', sB, sA, cwC[:, c, :], HALO, T, 'sA', 'sB')
        o = st['oi'] % 2
        st['oi'] += 1
        S.op('dve', lambda e, o=o: e.tensor_tensor(out=ob[o][:, :], in0=sB[:, 0:T], in1=zc[1][:, HALO:HALO + T],
                                                   op=ALU.mult), reads=['sB', 'zc1'], writes=['ob%d' % o])
        out_rows(1024 + c * 128, ob[o], 'ob%d' % o)

    for j in range(12):
        zi = j % 3
        zblock(22 + j, zi)
        o = st['oi'] % 2
        st['oi'] += 1
        eng = 'dve'
        conv3(ctx, eng, sA if j % 2 == 0 else sB, zc[zi], cwD[:, j, :], HALO, T, 'zc%d' % zi,
              'sA' if j % 2 == 0 else 'sB')
        src = sA if j % 2 == 0 else sB
        S.op('act', lambda e, o=o, src=src: e.copy(out=ob[o][:, :], in_=src[:, 0:T]),
             reads=['sA' if j % 2 == 0 else 'sB'], writes=['ob%d' % o])
        if j < 4:
            S.dma('act', ctx['hv_in'][j * 16:(j + 1) * 16, :].rearrange("t (c s) -> c t s", s=128),
                  ob[o][:, :].rearrange("c (t s) -> c t s", s=128), reads=['ob%d' % o], writes=['hv_in'])
        else:
            jj = j - 4
            S.dma('act', ctx['g12_d'][jj // 4, (jj % 4) * 128:(jj % 4 + 1) * 128, :], ob[o][:, :],
                  reads=['ob%d' % o], writes=['g12_d'])
    S.barrier()
    AR.release()
    S.allgather(ctx['hv_in'][:, :], ctx['hv_all'][:, :], reads=['hv_in'], writes=['hv_all'])


def phase_attn(ctx, l):
    S, AR, PS, ins, ident = ctx['S'], ctx['AR'], ctx['PS'], ctx['ins'], ctx['ident']
    hT = ctx['hT']
    w_in = ins['w_in']
    AR.mark()
    wb = [AR.alloc([128, 16, 128], BF16, 'wb%d' % i) for i in range(2)]
    qraw = AR.alloc([64, TE], F32, 'qraw')
    rope = AR.alloc([16, 2, TE], F32, 'rope')
    rt1 = AR.alloc([16, 512], F32, 'rt1')
    rt2 = AR.alloc([16, 512], F32, 'rt2')
    perm = AR.alloc([16, 16], F32, 'perm')
    qk = AR.alloc([64, 10, TE], BF16, 'qk')
    vp = AR.alloc([128, 18, 2, 66], BF16, 'vp')
    E = [AR.alloc([128, 512], BF16, 'E%d' % i) for i in range(6)]
    Ot = AR.alloc([128, 512], BF16, 'Ot')
    den = AR.alloc([128, 8], F32, 'den')
    mixB = AR.alloc([128, 4, T], BF16, 'mixB')
    masks = AR.alloc([128, 4, 128], BF16, 'masks')
    snk = AR.alloc([128, 8], F32, 'snk')
    S.dma('sp', rope[:, :, :], ins['t_rope'].rearrange("a d t -> d a t"), writes=['rope'])
    S.dma('sp', perm[:, :], ins['t_perm'][:, :], writes=['perm'])
    S.dma('sp', masks[:, :, :], ins['t_mask'].rearrange("m k q -> k m q"), writes=['masks'])
    S.dma('sp', snk[:, :], ins['attn_sink'][l:l + 1, :].to_broadcast([128, 8]), writes=['snk'])
    S.op('act', lambda e: e.activation(out=snk[:, :], in_=snk[:, :], func=AF.Exp), reads=['snk'], writes=['snk'])
    S.op('pool', lambda e: e.memset(vp[:, :, :, :], 1.0), writes=['vp'])
    for hh in range(10):
        blk = 4 + hh // 2
        w = (hh // 2) % 2
        if hh % 2 == 0:
            wload(ctx, wb[w][:, :, :], w_in[l][:, blk * 128:(blk + 1) * 128].rearrange("(k p) c -> p k c", p=128),
                  'wb%d' % w)
        for gi, (t0, n) in enumerate(TOKG):
            bank = 2 + gi % 3
            proj(ctx, wb[w], 'wb%d' % w, (hh % 2) * 64, 64, hT, t0, n, bank)
            S.op('act', lambda e, bank=bank, t0=t0, n=n: e.copy(out=qraw[:, t0:t0 + n], in_=PS[bank][0:64, 0:n]),
                 reads=['ps%d' % bank], writes=['qraw'])
            S.op('act', lambda e, hh=hh, t0=t0, n=n: e.copy(out=qk[:, hh, t0:t0 + n], in_=qraw[:, t0:t0 + n]),
                 reads=['qraw'], writes=['qk'])
            S.op('pe', lambda e, t0=t0, n=n: e.matmul(PS[5][0:16, 0:n], lhsT=perm[:, :], rhs=qraw[0:16, t0:t0 + n],
                                                      start=True, stop=True),
                 reads=['perm', 'qraw'], writes=['ps5'])
            S.op('dve', lambda e, t0=t0, n=n: e.tensor_tensor(out=rt1[:, 0:n], in0=qraw[0:16, t0:t0 + n],
                                                              in1=rope[:, 0, t0:t0 + n], op=ALU.mult),
                 reads=['qraw', 'rope'], writes=['rt1'])
            S.op('dve', lambda e, t0=t0, n=n: e.tensor_tensor(out=rt2[:, 0:n], in0=PS[5][0:16, 0:n],
                                                              in1=rope[:, 1, t0:t0 + n], op=ALU.mult),
                 reads=['ps5', 'rope'], writes=['rt2'])
            S.op('dve', lambda e, hh=hh, t0=t0, n=n: e.tensor_tensor(out=qk[0:16, hh, t0:t0 + n], in0=rt1[:, 0:n],
                                                                     in1=rt2[:, 0:n], op=ALU.add),
                 reads=['rt1', 'rt2', 'qk'], writes=['qk'])
    wload(ctx, wb[1][:, :, :], w_in[l][:, 9 * 128:10 * 128].rearrange("(k p) c -> p k c", p=128), 'wb1')
    for i in range(18):
        bank = 2 + i % 3
        for kc in range(16):
            S.op('pe', lambda e, kc=kc, i=i, bank=bank: e.matmul(PS[bank][:, 0:128], lhsT=hT[:, kc, i * 128:(i + 1) * 128],
                                                                   rhs=wb[1][:, kc, :], start=(kc == 0), stop=(kc == 15)),
                 reads=['wb1', 'hT'], writes=['ps%d' % bank])
        S.op('act', lambda e, i=i, bank=bank: e.copy(out=vp[:, i, :, 0:64],
                                                     in_=PS[bank][:, 0:128].rearrange("p (g d) -> p g d", g=2)),
             reads=['ps%d' % bank], writes=['vp'])
    for i in range(16):
        for g in range(2):
            eb = 3 * g
            for jj in range(3):
                bank = 2 + eb + jj
                S.op('pe', lambda e, i=i, g=g, jj=jj, bank=bank: e.matmul(
                    PS[bank][:, :].rearrange("p (h q) -> p h q", h=4),
                    lhsT=qk[:, 8 + g, (i + jj) * 128:(i + jj + 1) * 128],
                    rhs=qk[:, 4 * g:4 * g + 4, (i + 1) * 128:(i + 2) * 128], start=True, stop=True),
                    reads=['qk'], writes=['ps%d' % bank])
                S.op('act', lambda e, bank=bank, k=eb + jj: e.activation(out=E[k][:, :], in_=PS[bank][:, :],
                                                                       func=AF.Exp, scale=0.125),
                     reads=['ps%d' % bank], writes=['E%d' % (eb + jj)])
            mP = 0 if i == 0 else 1
            mN = 3 if i == 15 else 2
            S.op('dve', lambda e, k=eb, mP=mP: e.tensor_tensor(
                out=E[k][:, :].rearrange("p (h q) -> p h q", h=4), in0=E[k][:, :].rearrange("p (h q) -> p h q", h=4),
                in1=masks[:, mP, :].unsqueeze(1).to_broadcast([128, 4, 128]), op=ALU.mult),
                reads=['E%d' % eb, 'masks'], writes=['E%d' % eb])
            S.op('dve', lambda e, k=eb + 2, mN=mN: e.tensor_tensor(
                out=E[k][:, :].rearrange("p (h q) -> p h q", h=4), in0=E[k][:, :].rearrange("p (h q) -> p h q", h=4),
                in1=masks[:, mN, :].unsqueeze(1).to_broadcast([128, 4, 128]), op=ALU.mult),
                reads=['E%d' % (eb + 2), 'masks'], writes=['E%d' % (eb + 2)])
            for h4 in range(4):
                for jj in range(3):
                    S.op('pe', lambda e, i=i, g=g, jj=jj, h4=h4, k=eb + jj: e.matmul(
                        PS[0][:, h4 * 66:h4 * 66 + 65], lhsT=E[k][:, h4 * 128:(h4 + 1) * 128],
                        rhs=vp[:, i + jj, g, 0:65], start=(jj == 0), stop=(jj == 2)),
                        reads=['E%d' % (eb + jj), 'vp'], writes=['ps0'])
            pso = PS[0][:, 0:264].rearrange("p (h d) -> p h d", h=4)
            S.op('dve', lambda e, g=g, pso=pso: e.tensor_tensor(out=den[:, 0:4], in0=pso[:, :, 64],
                                                                in1=snk[:, 4 * g:4 * g + 4], op=ALU.add),
                 reads=['ps0', 'snk'], writes=['den'])
            S.op('dve', lambda e: e.reciprocal(out=den[:, 4:8], in_=den[:, 0:4]), reads=['den'], writes=['den'])
            S.op('dve', lambda e, g=g, pso=pso: e.tensor_tensor(
                out=Ot[:, g * 256:(g + 1) * 256].rearrange("p (h d) -> p h d", h=4), in0=pso[:, :, 0:64],
                in1=den[:, 4:8].unsqueeze(2).to_broadcast([128, 4, 64]), op=ALU.mult),
                reads=['ps0', 'den'], writes=['Ot'])
        pst = PS[1][:, :].bitcast(BF16)
        for cb in range(4):
            S.op('pe', lambda e, cb=cb, pst=pst: e.transpose(out=pst[:, cb * 128:(cb + 1) * 128],
                                                              in_=Ot[:, cb * 128:(cb + 1) * 128], identity=ident[:, :]),
                 reads=['Ot', 'ident'], writes=['ps1'])
        S.op('act', lambda e, i=i, pst=pst: e.copy(out=mixB[:, :, i * 128:(i + 1) * 128],
                                                   in_=pst[:, 0:512].rearrange("p (c q) -> p c q", c=4)),
             reads=['ps1'], writes=['mixB'])
    for cb in range(4):
        S.dma('sp', ctx['mixT_d'][512 + cb * 128:512 + (cb + 1) * 128, :], mixB[:, cb, :], reads=['mixB'],
              writes=['mixT_d'])
    S.barrier()
    AR.release()


def phase_ffn(ctx, l, xsrc, final):
    S, AR, PS, ins = ctx['S'], ctx['AR'], ctx['PS'], ctx['ins']
    G = 1024
    NT = G // 128
    AR.mark()
    ctx['sc_ss'] = AR.alloc([128, 4], F32, 'ss')
    ctx['sc_xn'] = AR.alloc([128, D], BF16, 'xn')
    ctx['sc_junk'] = ctx['sc_xn']
    ctx['junk_key'] = 'sc_xn'
    h2T = AR.alloc([128, 16, G], BF16, 'h2T')
    wg = [AR.alloc([128, 16, 128], BF16, 'wg%d' % i) for i in range(4)]
    sil = [AR.alloc([128, 512], F32, 'sil%d' % i) for i in range(2)]
    ctx['tp_bank'] = [0, 1]
    ctx['tp_i'] = 0
    w_out, w_gu, w_down = ins['w_out'], ins['w_gate_up'], ins['w_down']
    xdst = ctx['x1_ext']
    cnt = {'wo': 0, 'wg': 0, 'wd': 0, 'pb': 0, 'xb': 0}
    if final:
        fxt = [AR.alloc([128, D], F32, 'fxt%d' % i) for i in range(3)]
        gfin = AR.alloc([128, D], F32, 'gfin')
        fjunk = AR.alloc([128, D], BF16, 'fjunk')
        ssf = AR.alloc([128, 3, 4], F32, 'ssf')
        S.dma('sp', gfin[:, :], ins['final_norm_g'].rearrange("(o d) -> o d", o=1).to_broadcast([128, D]),
              writes=['gfin'])
        fcnt = {'n': 0}

        def final_tile(i):
            b = fcnt['n'] % 3
            fcnt['n'] += 1
            r0 = i * 128
            ss = ssf[:, b, :]
            S.dma('sp', fxt[b][:, :], xdst[HALO + r0:HALO + r0 + 128, :], reads=['x1_ext'], writes=['fxt%d' % b])
            S.op('act', lambda e, b=b, ss=ss: e.activation(out=fjunk[:, :], in_=fxt[b][:, :], func=AF.Square,
                                                           accum_out=ss[:, 0:1]),
                 reads=['fxt%d' % b], writes=['fjunk', 'ssf%d' % b])
            S.op('dve', lambda e, ss=ss: e.tensor_scalar(out=ss[:, 1:2], in0=ss[:, 0:1], scalar1=1.0 / D, scalar2=EPS,
                                                         op0=ALU.mult, op1=ALU.add), reads=['ssf%d' % b],
                 writes=['ssf%d' % b])
            S.op('act', lambda e, ss=ss: e.activation(out=ss[:, 3:4], in_=ss[:, 1:2], func=AF.Sqrt),
                 reads=['ssf%d' % b], writes=['ssf%d' % b])
            S.op('dve', lambda e, ss=ss: e.reciprocal(out=ss[:, 2:3], in_=ss[:, 3:4]), reads=['ssf%d' % b],
                 writes=['ssf%d' % b])
            S.op('dve', lambda e, b=b, ss=ss: e.scalar_tensor_tensor(out=fxt[b][:, :], in0=fxt[b][:, :],
                                                                     scalar=ss[:, 2:3], in1=gfin[:, :], op0=ALU.mult,
                                                                     op1=ALU.mult),
                 reads=['fxt%d' % b, 'ssf%d' % b, 'gfin'], writes=['fxt%d' % b])
            S.dma('act', ctx['y_out'][r0:r0 + 128, :], fxt[b][:, :], reads=['fxt%d' % b], writes=['y_out'])
        fpending = []
    AR.mark()
    mixg = AR.alloc([128, 16, 512], BF16, 'mixg')
    wo = [AR.alloc([128, 16, 512], BF16, 'wo%d' % i) for i in range(2)]
    xm = [AR.alloc([128, D], F32, 'xm%d' % i) for i in range(4)]
    AR.release()
    AR.mark()
    actT = AR.alloc([128, NFF, G], BF16, 'actT')
    wd = [AR.alloc([128, 4, 512], BF16, 'wd%d' % i) for i in range(3)]
    xb = [AR.alloc([128, 512], F32, 'xb%d' % i) for i in range(4)]
    for gi in range(T // G):
        for sg in range(G // 512):
            tok0 = gi * G + sg * 512
            S.dma('sp', mixg[:, :, :], ctx['mixT_d'][:, tok0:tok0 + 512].rearrange("(k p) t -> p k t", p=128),
                  reads=['mixT_d'], writes=['mixg'])
            for t in range(4):
                r0 = HALO + tok0 + t * 128
                S.dma('sp', xm[t][:, :], xsrc[r0:r0 + 128, :], writes=['xm%d' % t])
            for cg in range(4):
                w = cnt['wo'] % 2
                cnt['wo'] += 1
                wload(ctx, wo[w][:, :, :], w_out[l][:, cg * 512:(cg + 1) * 512].rearrange("(k p) c -> p k c", p=128),
                      'wo%d' % w)
                for t in range(4):
                    bank = 2 + cnt['pb'] % 4
                    cnt['pb'] += 1
                    for kc in range(16):
                        S.op('pe', lambda e, kc=kc, t=t, w=w, bank=bank: e.matmul(
                            PS[bank][:, :], lhsT=mixg[:, kc, t * 128:(t + 1) * 128], rhs=wo[w][:, kc, :],
                            start=(kc == 0), stop=(kc == 15)), reads=['mixg', 'wo%d' % w], writes=['ps%d' % bank])
                    S.op('dve', lambda e, t=t, cg=cg, bank=bank: e.tensor_tensor(
                        out=xm[t][:, cg * 512:(cg + 1) * 512], in0=PS[bank][:, :],
                        in1=xm[t][:, cg * 512:(cg + 1) * 512], op=ALU.add),
                        reads=['ps%d' % bank, 'xm%d' % t], writes=['xm%d' % t])
            for t in range(4):
                rms_to_T(ctx, xm[t], ctx['gcolF'][:, l, :], h2T, sg * 512 + t * 128, 'xm%d' % t)
                S.dma('act', ctx['xmid'][tok0 + t * 128:tok0 + (t + 1) * 128, :], xm[t][:, :], reads=['xm%d' % t],
                      writes=['xmid'])
        S.barrier()
        for j in range(NFF):
            if final and fpending and j % 5 == 2:
                final_tile(fpending.pop(0))
            w = (cnt['wg'] % 2) * 2
            cnt['wg'] += 1
            wload(ctx, wg[w][:, :, :], w_gu[l][:, j * 128:(j + 1) * 128].rearrange("(k p) c -> p k c", p=128),
                  'wg%d' % w)
            wload(ctx, wg[w + 1][:, :, :],
                  w_gu[l][:, DFF + j * 128:DFF + (j + 1) * 128].rearrange("(k p) c -> p k c", p=128), 'wg%d' % (w + 1))
            for hf in range(G // 512):
                pb = cnt['pb'] % 4
                cnt['pb'] += 1
                ba, bb = 2 * pb, 2 * pb + 1
                for kc in range(16):
                    S.op('pe', lambda e, kc=kc, w=w, ba=ba, hf=hf: e.matmul(
                        PS[ba][:, :], lhsT=wg[w][:, kc, :], rhs=h2T[:, kc, hf * 512:(hf + 1) * 512],
                        start=(kc == 0), stop=(kc == 15)), reads=['wg%d' % w, 'hT'], writes=['ps%d' % ba])
                for kc in range(16):
                    S.op('pe', lambda e, kc=kc, w=w, bb=bb, hf=hf: e.matmul(
                        PS[bb][:, :], lhsT=wg[w + 1][:, kc, :], rhs=h2T[:, kc, hf * 512:(hf + 1) * 512],
                        start=(kc == 0), stop=(kc == 15)), reads=['wg%d' % (w + 1), 'hT'], writes=['ps%d' % bb])
                si = cnt['pb'] % 2
                S.op('act', lambda e, ba=ba, si=si: e.activation(out=sil[si][:, :], in_=PS[ba][:, :], func=AF.Silu),
                     reads=['ps%d' % ba], writes=['sil%d' % si])
                S.op('dve', lambda e, bb=bb, si=si, j=j, hf=hf: e.tensor_tensor(
                    out=actT[:, j, hf * 512:(hf + 1) * 512], in0=sil[si][:, :], in1=PS[bb][:, :], op=ALU.mult),
                    reads=['ps%d' % bb, 'sil%d' % si], writes=['actT'])
        for cg in range(4):
            for k4 in range(NFF // 4):
                w = cnt['wd'] % 3
                cnt['wd'] += 1
                wload(ctx, wd[w][:, :, :],
                      w_down[l][k4 * 512:(k4 + 1) * 512, cg * 512:(cg + 1) * 512].rearrange("(k p) c -> p k c", p=128),
                      'wd%d' % w)
                for kk in range(4):
                    k = k4 * 4 + kk
                    for t in range(NT):
                        S.op('pe', lambda e, k=k, kk=kk, t=t, w=w: e.matmul(
                            PS[t][:, :], lhsT=actT[:, k, t * 128:(t + 1) * 128], rhs=wd[w][:, kk, :],
                            start=(k == 0), stop=(k == NFF - 1)),
                            reads=['actT', 'wd%d' % w], writes=['ps%d' % t])
            for t in range(NT):
                r0 = gi * G + t * 128
                b = cnt['xb'] % 4
                cnt['xb'] += 1
                S.dma('sp', xb[b][:, :], ctx['xmid'][r0:r0 + 128, cg * 512:(cg + 1) * 512], reads=['xmid'],
                      writes=['xb%d' % b])
                S.op('dve', lambda e, t=t, b=b: e.tensor_tensor(out=xb[b][:, :], in0=PS[t][:, :], in1=xb[b][:, :],
                                                                op=ALU.add),
                     reads=['ps%d' % t, 'xb%d' % b], writes=['xb%d' % b])
                S.dma('act', xdst[HALO + r0:HALO + r0 + 128, cg * 512:(cg + 1) * 512], xb[b][:, :],
                      reads=['xb%d' % b], writes=['x1_ext'])
                if not final and r0 == 0:
                    S.dma('act', ctx['edge_in'][0:128, cg * 512:(cg + 1) * 512], xb[b][:, :], reads=['xb%d' % b],
                          writes=['edge_in'])
                if not final and r0 == T - 128:
                    S.dma('act', ctx['edge_in'][128:256, cg * 512:(cg + 1) * 512], xb[b][:, :], reads=['xb%d' % b],
                          writes=['edge_in'])
        S.barrier()
        if final:
            fpending.extend(range(gi * NT, (gi + 1) * NT))
    if final:
        for i in fpending:
            final_tile(i)
        S.barrier()
    AR.release()
    AR.release()


def halo_start(ctx):
    S = ctx['S']
    S.allgather(ctx['edge_in'][:, :], ctx['edge_all'][:, :], reads=['edge_in'], writes=['edge_all'])
    S.sticky.add('edge_all')


def halo_finish(ctx):
    S, AR, ins = ctx['S'], ctx['AR'], ctx['ins']
    AR.mark()
    hb = [AR.alloc([128, D], F32, 'hb%d' % i) for i in range(2)]
    fl = AR.alloc([128, 2], F32, 'fl')
    S.dma('sp', fl[:, :], ins['t_flags'][:, :], writes=['fl'])
    S.dma('sp', hb[0][:, :], ctx['edge_all'][128:256, :], reads=['edge_all'], writes=['hb0'])
    S.dma('sp', hb[1][:, :], ctx['edge_all'][256:384, :], reads=['edge_all'], writes=['hb1'])
    for i in range(2):
        S.op('dve', lambda e, i=i: e.tensor_scalar(out=hb[i][:, :], in0=hb[i][:, :], scalar1=fl[:, i:i + 1],
                                                   scalar2=None, op0=ALU.mult), reads=['hb%d' % i, 'fl'],
             writes=['hb%d' % i])
    S.dma('sp', ctx['x1_ext'][0:HALO, :], hb[0][:, :], reads=['hb0'], writes=['x1_ext'])
    S.dma('sp', ctx['x1_ext'][HALO + T:TE, :], hb[1][:, :], reads=['hb1'], writes=['x1_ext'])
    S.sticky.discard('edge_all')
    S.barrier()
    AR.release()


def core_info(c):
    if c < 4:
        return dict(kind='p', seq=c // 2, pos0=2048 * (c % 2), L=4096, rank=c % 2)
    return dict(kind='s', seq=c - 4, pos0=0, L=2048, rank=c % 2)


def host_tables(c):
    ci = core_info(c)
    pos0, L, rank = ci['pos0'], ci['L'], ci['rank']
    bf = ml_dtypes.bfloat16
    tb = {}
    tb['t_ident'] = np.eye(128, dtype=np.float32).astype(bf)
    pos = (pos0 - HALO + np.arange(TE)).astype(np.float32)
    inv_freq = (np.float32(500000.0) ** (-np.arange(0, 16, 2, dtype=np.float32) / np.float32(16))).astype(np.float32)
    ang = (pos[:, None] * inv_freq[None, :]).astype(np.float32)
    ang = np.concatenate([ang, ang], axis=1).T
    cs = np.cos(ang).astype(np.float32)
    sn = np.sin(ang).astype(np.float32)
    sn[:8] *= -1.0
    tb['t_rope'] = np.stack([cs, sn]).astype(np.float32)
    pm = np.zeros((16, 16), np.float32)
    for m in range(16):
        pm[(m + 8) % 16, m] = 1.0
    tb['t_perm'] = pm
    kk = np.arange(128)[:, None]
    qq = np.arange(128)[None, :]
    lv = 1.0 if pos0 > 0 else 0.0
    rv = 1.0 if pos0 + T < L else 0.0
    mP = (kk >= qq).astype(np.float32)
    mN = (kk <= qq).astype(np.float32)
    tb['t_mask'] = np.stack([mP * lv, mP, mN, mN * rv]).astype(bf)
    t = pos0 + np.arange(T)
    ic = np.zeros((4, T), np.float32)
    for g, w in enumerate((2, 4, 8, 16)):
        lo = np.clip(t - w // 2, 0, L)
        hi = np.clip(t + w // 2, 0, L)
        ic[g] = (1.0 / (hi - lo).astype(np.float32)).astype(np.float32)
    tb['t_invcnt'] = ic
    fl = np.zeros((128, 2), np.float32)
    fl[:, 0] = lv
    fl[:, 1] = rv
    tb['t_flags'] = fl
    tb.update(fft_tables(c))
    return tb


def fft_tables(c):
    ci = core_info(c)
    pos0h = 2048 * ci['rank']
    L, kind, rank = ci['L'], ci['kind'], ci['rank']
    bf = ml_dtypes.bfloat16
    tb = {}
    N = NFFT
    NA = 33
    fa = np.arange(NA)
    sB = np.arange(64)
    ph = -2.0 * np.pi * np.outer(sB, fa) / 64.0
    w1 = np.concatenate([np.cos(ph), np.sin(ph)], axis=1)
    tb['t_w1f'] = w1.astype(np.float32).astype(bf)
    w1d = w1[:32].copy()
    if kind == 's':
        if rank == 0:
            w1d[16:32] = 0.0
        else:
            w1d[0:16] = 0.0
    tb['t_w1d'] = w1d.astype(np.float32).astype(bf)
    sA = np.arange(128)
    fb = np.arange(128)
    tm = np.zeros((5, 128, 3, 8, 128), np.float32)
    for g in range(5):
        for j in range(8):
            if g * 8 + j >= NA:
                continue
            f = (g * 8 + j) + 64 * fb
            ph = -2.0 * np.pi * np.outer(sA, f) / N
            tm[g, :, 0, j, :] = np.cos(ph)
            tm[g, :, 1, j, :] = np.sin(ph)
            tm[g, :, 2, j, :] = -np.sin(ph)
    tb['t_m'] = tm.reshape(5, 128, 3 * 8 * 128).astype(bf)
    ph = 2.0 * np.pi * np.outer(fb, np.arange(128)) / 128.0
    tb['t_g'] = np.concatenate([np.cos(ph), np.sin(ph), -np.sin(ph)], axis=1).astype(np.float32).astype(bf)
    tA = np.arange(128)[:, None]
    tB = np.arange(16)[None, :]
    tt = (pos0h + 128 * tB + tA).reshape(-1)
    ph = 2.0 * np.pi * np.outer(fa, tt) / N
    wgt = np.where((fa == 0) | (fa == 32), 1.0, 2.0)[:, None]
    tb['t_h'] = np.concatenate([wgt * np.cos(ph), -wgt * np.sin(ph)], axis=0).astype(np.float32).astype(bf)
    s = np.arange(N)
    idx = np.where(s < N // 2, s, N - s)
    valid = np.where(s < N // 2, s < L, (N - s) <= L - 1)
    tl = np.linspace(0.0, 1.0, L, dtype=np.float32)
    idc = np.clip(idx, 0, L - 1)
    tt = tl[idc]
    bands = np.linspace(1e-4, 15.0, 16, dtype=np.float32)[None, :]
    wpos = (np.float32(2.0 * math.pi / L) * idc.astype(np.float32))[:, None]
    z = np.concatenate([tt[:, None], np.cos(bands * wpos), -np.sin(bands * wpos)], axis=1).astype(np.float32)
    z = np.where(valid[:, None], z, 0.0).astype(np.float32)
    tb['t_pos'] = np.ascontiguousarray(z.T)
    tb['t_tdec'] = np.where(valid, tt, 1.0e4).astype(np.float32)[None, :]
    return tb


_CACHE = {}


def kernel(**inputs):
    xp = np.asarray(inputs['x_prompt'], np.float32)
    xs = np.asarray(inputs['x_sample'], np.float32)
    if 'nc' not in _CACHE:
        nc, S, ctx, es = build()
        full_program(ctx)
        S.emit()
        _CACHE['nc'] = nc
        _CACHE['tabs'] = [host_tables(c) for c in range(8)]
    nc = _CACHE['nc']
    wnames = ["mix_norm_g", "w_in", "pool_w", "pool_scale", "attn_sink", "sconv_w", "hy_short_w", "hy_w1", "hy_b1",
              "hy_w2", "hy_b2", "hy_freq", "hy_bias", "w_out", "ffn_norm_g", "w_gate_up",
              "w_down", "final_norm_g"]
    w3f = np.asarray(inputs["hy_w3"], np.float32).reshape(DEPTH, 64, 4, 4, 128)
    dcf = np.asarray(inputs["hy_decay"], np.float32).reshape(DEPTH, 4, 4, 128)
    shared = {k: np.ascontiguousarray(np.asarray(inputs[k], np.float32)) for k in wnames}
    in_maps = []
    for c in range(8):
        ci = core_info(c)
        seq = xp[ci['seq']] if ci['kind'] == 'p' else xs[ci['seq']]
        xe = np.zeros((TE, D), np.float32)
        lo = ci['pos0'] - HALO
        hi = ci['pos0'] + T + HALO
        a, b = max(lo, 0), min(hi, ci['L'])
        xe[a - lo:b - lo] = seq[a:b]
        m = dict(shared)
        m['x_ext'] = xe
        m['hy_w3_sh'] = np.ascontiguousarray(w3f[:, :, :, c % 4, :]).reshape(DEPTH, 64, 512)
        m['hy_decay_sh'] = np.ascontiguousarray(dcf[:, :, c % 4, :]).reshape(DEPTH, 512)
        m.update(_CACHE['tabs'][c])
        in_maps.append(m)
    res = run_bass_kernel_spmd(nc, in_maps, core_ids=list(range(8)))
    _CACHE['last'] = res
    yp = np.zeros((2, 4096, D), np.float32)
    ys = np.zeros((4, 2048, D), np.float32)
    for c in range(8):
        ci = core_info(c)
        y = np.asarray(res.results[c]['y_out'], np.float32)
        if ci['kind'] == 'p':
            yp[ci['seq'], ci['pos0']:ci['pos0'] + T] = y
        else:
            ys[ci['seq']] = y
    return (yp, ys)


SKIP_HYENA = False


def full_program(ctx):
    S, AR = ctx['S'], ctx['AR']
    if not SKIP_HYENA:
        phase_filters(ctx)
    for l in range(DEPTH):
        xsrc = ctx['x_ext'] if l == 0 else ctx['x1_ext']
        AR.mark()
        ctx['hT'] = AR.alloc([128, 16, TE], BF16, 'hT')
        if l == 0:
            phase_norm(ctx, l, xsrc)
        else:
            phase_norm(ctx, l, xsrc, tiles=list(range(1, 17)))
            halo_finish(ctx)
            phase_norm(ctx, l, xsrc, tiles=[0, 17])
        phase_local(ctx, l)
        phase_attn(ctx, l)
        AR.release()
        if SKIP_HYENA:
            AR.mark()
            zt = AR.alloc([128, T], BF16, 'zt')
            S.op('pool', lambda e: e.memset(zt[:, :], 0.0), writes=['zt'])
            for c in range(4):
                S.dma('sp', ctx['mixT_d'][1536 + c * 128:1536 + (c + 1) * 128, :], zt[:, :], reads=['zt'],
                      writes=['mixT_d'])
            S.barrier()
            AR.release()
        else:
            phase_hyena(ctx, l)
        phase_ffn(ctx, l, xsrc, final=(l == DEPTH - 1))
        if l == 0:
            halo_start(ctx)


NA = 33
FGROUPS = [(0, 8), (8, 8), (16, 8), (24, 8), (32, 1)]


def hy_alloc(ctx):
    AR = ctx['AR']
    h = {}
    h['InB'] = AR.alloc([64, 128 * 128], BF16, 'InB')
    h['A'] = AR.alloc([128, 2, NA, 128], BF16, 'A')
    h['Mt'] = [AR.alloc([128, 3, 8, 128], BF16, 'Mt%d' % i) for i in range(len(FGROUPS))]
    h['w1d'] = AR.alloc([32, 2 * NA], BF16, 'w1d')
    h['w1f'] = AR.alloc([64, 2 * NA], BF16, 'w1f')
    S, ins = ctx['S'], ctx['ins']
    S.dma('sp', h['w1d'][:, :], ins['t_w1d'][:, :], writes=['w1d'])
    S.dma('sp', h['w1f'][:, :], ins['t_w1f'][:, :], writes=['w1f'])
    for g in range(len(FGROUPS)):
        S.dma('sp', h['Mt'][g][:, :, :, :], ins['t_m'][g].rearrange("p (a j f) -> p a j f", a=3, j=8),
              writes=['Mt%d' % g])
    return h


def fwd_transform(ctx, h, K, w1, w1key, on_group, after_stage1=None, pre_stage2=None):
    S, PS, ins = ctx['S'], ctx['PS'], ctx['ins']
    In, A = h['InB'], h['A']

    if pre_stage2 is not None:
        pre_stage2()
    for c4 in range(32):
        bank = c4 % 2
        for cc in range(4):
            c = c4 * 4 + cc
            S.op('pe', lambda e, c=c, cc=cc, bank=bank: e.matmul(PS[bank][:, cc * 128:cc * 128 + 2 * NA],
                                                               lhsT=In[0:K, c * 128:(c + 1) * 128], rhs=w1[0:K, :],
                                                               start=True, stop=True),
                 reads=['InB', w1key], writes=['ps%d' % bank])
        src = sap(PS[bank], 0, 128, [(NA, 2), (1, NA), (128, 4)])
        S.op('act', lambda e, c4=c4, src=src: e.copy(out=A[:, :, :, c4 * 4:(c4 + 1) * 4], in_=src),
             reads=['ps%d' % bank], writes=['A'])
    if after_stage1 is not None:
        after_stage1()
    for g, (fa0, nj) in enumerate(FGROUPS):
        m = g
        Mt = h['Mt'][m]
        b0 = 4 * (g % 2)
        for j in range(nj):
            fa = fa0 + j
            bre = b0 + j // 4
            bim = b0 + 2 + j // 4
            col = (j % 4) * 128
            for (bank, la, lb) in ((bre, 0, 2), (bim, 1, 0)):
                S.op('pe', lambda e, bank=bank, la=la, j=j, fa=fa, col=col, Mt=Mt: e.matmul(
                    PS[bank][:, col:col + 128], lhsT=Mt[:, la, j, :], rhs=A[:, 0, fa, :], start=True, stop=False),
                    reads=['Mt%d' % m, 'A'], writes=['ps%d' % bank])
                S.op('pe', lambda e, bank=bank, lb=lb, j=j, fa=fa, col=col, Mt=Mt: e.matmul(
                    PS[bank][:, col:col + 128], lhsT=Mt[:, lb, j, :], rhs=A[:, 1, fa, :], start=False, stop=True),
                    reads=['Mt%d' % m, 'A'], writes=['ps%d' % bank])
        on_group(g, b0, fa0, nj)


def kf_row(l, o, cc, g):
    base = (l * 2 + o) * 2560
    if g < 4:
        return base + (g // 2) * 1024 + cc * 256 + (g % 2) * 128
    return base + 2048 + cc * 128


def phase_filters(ctx):
    S, AR, PS, ins = ctx['S'], ctx['AR'], ctx['PS'], ctx['ins']
    AR.mark()
    h = hy_alloc(ctx)
    h2T = [AR.alloc([64, NFFT], BF16, 'h2T%d' % i) for i in range(2)]
    tdb = AR.alloc([128, NFFT], F32, 'tdb')
    pos = [AR.alloc([33, 512], F32, 'pos%d' % i) for i in range(2)]
    w1 = [AR.alloc([33, 64], F32, 'fw1%d' % i) for i in range(2)]
    w2 = [AR.alloc([64, 64], F32, 'fw2%d' % i) for i in range(2)]
    w3 = [AR.alloc([64, 512], BF16, 'fw3%d' % i) for i in range(2)]
    prm = AR.alloc([64, 2, 8], F32, 'prm')
    negd = AR.alloc([128, 2, 8], F32, 'negd')
    arg = [AR.alloc([64, 512], F32, 'arg%d' % i) for i in range(2)]
    argi = [AR.alloc([64, 512], mybir.dt.int32, 'argi%d' % i) for i in range(2)]
    argf = [AR.alloc([64, 512], F32, 'argf%d' % i) for i in range(2)]
    h1 = [AR.alloc([64, 512], F32, 'h1%d' % i) for i in range(2)]
    ex = [AR.alloc([128, 512], F32, 'ex%d' % i) for i in range(2)]
    kt = [AR.alloc([128, 512], F32, 'kt%d' % i) for i in range(2)]
    kk = AR.alloc([128, NFFT], BF16, 'kk')
    acc = AR.alloc([128, 4, 20], F32, 'acc')
    kfo = [AR.alloc([128, 2, 8, 128], BF16, 'kfo%d' % i) for i in range(2)]
    PI = math.pi
    G4 = [[0, 1, 2, 3], [4, 5, 6, 7]]
    for k in range(2):
        S.op('pool', lambda e, k=k: e.memset(kfo[k][:, :, :, :], 0.0), writes=['kfo%d' % k])
    for l in range(DEPTH):
        S.dma('sp', w1[l][:, :], ins['hy_w1'][l], writes=['fw%d' % l])
        S.dma('sp', w2[l][:, :], ins['hy_w2'][l], writes=['fw%d' % l])
        wload(ctx, w3[l][:, :], ins['hy_w3_sh'][l], 'fw3%d' % l)
        S.dma('sp', prm[:, l, 0:1], ins['hy_freq'][l].rearrange("(p o) -> p o", o=1), writes=['prm%d' % l])
        S.dma('sp', prm[:, l, 1:2], ins['hy_b1'][l].rearrange("(p o) -> p o", o=1), writes=['prm%d' % l])
        S.dma('sp', prm[:, l, 2:3], ins['hy_b2'][l].rearrange("(p o) -> p o", o=1), writes=['prm%d' % l])
        S.dma('sp', negd[:, l, 0:4], ins['hy_decay_sh'][l].rearrange("(q p) -> p q", p=128), writes=['negd%d' % l],
              allow_slow_non_contiguous=True)
    S.dma('sp', tdb[:, :], ins['t_tdec'][0:1, :].to_broadcast([128, NFFT]), writes=['tdb'])
    for l in range(DEPTH):
        S.op('dve', lambda e, l=l: e.tensor_scalar(out=prm[:, l, 3:4], in0=prm[:, l, 0:1], scalar1=1.0 / (2.0 * PI),
                                                   scalar2=None, op0=ALU.mult), reads=['prm%d' % l],
             writes=['prm%d' % l])
        S.op('dve', lambda e, l=l: e.tensor_tensor(out=prm[:, l, 4:6], in0=prm[:, l, 1:3],
                                                   in1=prm[:, l, 3:4].to_broadcast([64, 2]), op=ALU.mult),
             reads=['prm%d' % l], writes=['prm%d' % l])
        S.op('act', lambda e, l=l: e.activation(out=negd[:, l, 4:8], in_=negd[:, l, 0:4], func=AF.Abs),
             reads=['negd%d' % l], writes=['negd%d' % l])
        S.op('dve', lambda e, l=l: e.tensor_scalar(out=negd[:, l, 0:4], in0=negd[:, l, 4:8], scalar1=-1.0,
                                                   scalar2=None, op0=ALU.mult), reads=['negd%d' % l],
             writes=['negd%d' % l])
    for sb in range(16):
        pb = sb % 2
        S.dma('sp', pos[pb][:, :], ins['t_pos'][:, sb * 512:(sb + 1) * 512], writes=['pos%d' % pb])
        for stage in range(2):
            for l in range(DEPTH):
                b = l
                if stage == 0:
                    src, skey, wt, kq = pos[pb], 'pos%d' % pb, w1[l], 33
                else:
                    src, skey, wt, kq = h1[b], 'h1%d' % b, w2[l], 64
                bank = 6 + b
                a, ai, af = arg[b], argi[b], argf[b]
                S.op('pe', lambda e, src=src, wt=wt, bank=bank, kq=kq: e.matmul(
                    PS[bank][0:64, :], lhsT=wt[0:kq, :], rhs=src[0:kq, :], start=True, stop=True),
                    reads=['fw%d' % l, skey], writes=['ps%d' % bank])
                S.op('act', lambda e, stage=stage, a=a, bank=bank, l=l: e.activation(
                    out=a[:, :], in_=PS[bank][0:64, :], func=AF.Identity, scale=prm[:, l, 3:4],
                    bias=prm[:, l, 4 + stage:5 + stage]),
                    reads=['ps%d' % bank, 'prm%d' % l], writes=['arg%d' % b])
                S.op('dve', lambda e, a=a, ai=ai: e.tensor_copy(out=ai[:, :], in_=a[:, :]), reads=['arg%d' % b],
                     writes=['argi%d' % b])
                S.op('dve', lambda e, ai=ai, af=af: e.tensor_copy(out=af[:, :], in_=ai[:, :]), reads=['argi%d' % b],
                     writes=['argf%d' % b])
                S.op('dve', lambda e, a=a, af=af: e.tensor_tensor(out=a[:, :], in0=a[:, :], in1=af[:, :],
                                                                  op=ALU.subtract),
                     reads=['arg%d' % b, 'argf%d' % b], writes=['arg%d' % b])
                if stage == 0:
                    S.op('act', lambda e, a=a, b=b: e.activation(out=h1[b][:, :], in_=a[:, :], func=AF.Sin,
                                                                 scale=2.0 * PI * (1.0 - 1e-6)),
                         reads=['arg%d' % b], writes=['h1%d' % b])
                else:
                    S.op('act', lambda e, a=a, sb=sb, l=l: e.activation(
                        out=h2T[l][:, sb * 512:(sb + 1) * 512], in_=a[:, :], func=AF.Sin,
                        scale=2.0 * PI * (1.0 - 1e-6)), reads=['arg%d' % b], writes=['h2T%d' % l])
    kl = ctx['kf_loc']
    kd = ctx['kf_d']
    it = 0
    for l in range(DEPTH):
        for o in range(2):
            kb = it % 2
            it += 1
            for sb in range(16):
                dr = 0 if sb < 8 else 1
                q = o * 2 + dr
                t = sb % 2
                bank = 4 + t
                S.op('pe', lambda e, q=q, sb=sb, bank=bank, l=l: e.matmul(
                    PS[bank][:, :], lhsT=w3[l][:, q * 128:(q + 1) * 128], rhs=h2T[l][:, sb * 512:(sb + 1) * 512],
                    start=True, stop=True), reads=['fw3%d' % l, 'h2T%d' % l], writes=['ps%d' % bank])
                S.op('act', lambda e, t=t, sb=sb, q=q, l=l: e.activation(
                    out=ex[t][:, :], in_=tdb[:, sb * 512:(sb + 1) * 512], func=AF.Exp, scale=negd[:, l, q:q + 1]),
                    reads=['tdb', 'negd%d' % l], writes=['ex%d' % t])
                S.op('dve', lambda e, t=t, bank=bank: e.tensor_tensor(out=kt[t][:, :], in0=PS[bank][:, :],
                                                                      in1=ex[t][:, :], op=ALU.mult),
                     reads=['ps%d' % bank, 'ex%d' % t], writes=['kt%d' % t])
                S.op('dve', lambda e, sb=sb, t=t, aq=l * 2 + o: e.tensor_reduce(
                    out=acc[:, aq, sb:sb + 1], in_=kt[t][:, :], axis=AX.X, op=ALU.add,
                    apply_absolute_value=True), reads=['kt%d' % t], writes=['acc'])
                S.op('act', lambda e, sb=sb, t=t: e.copy(out=kk[:, sb * 512:(sb + 1) * 512], in_=kt[t][:, :]),
                     reads=['kt%d' % t], writes=['kk'])
            ai_ = l * 2 + o
            S.op('dve', lambda e, ai_=ai_: e.tensor_reduce(out=acc[:, ai_, 16:17], in_=acc[:, ai_, 0:16], axis=AX.X,
                                                           op=ALU.add), reads=['acc'], writes=['acc'])
            S.op('dve', lambda e, ai_=ai_: e.tensor_scalar(out=acc[:, ai_, 17:18], in0=acc[:, ai_, 16:17],
                                                           scalar1=float(NFFT), scalar2=None, op0=ALU.mult),
                 reads=['acc'], writes=['acc'])
            S.op('dve', lambda e, ai_=ai_: e.reciprocal(out=acc[:, ai_, 18:19], in_=acc[:, ai_, 17:18]),
                 reads=['acc'], writes=['acc'])
            S.dma('sp', ctx['rn_loc'][ai_].rearrange("(p o) -> p o", o=1), acc[:, ai_, 18:19], reads=['acc'],
                  writes=['rn_loc'])
            S.dma('act', ctx['kk_d'][kb], kk[:, :], reads=['kk'], writes=['kk_d%d' % kb])
            S.dma('sp', h['InB'][0:64, :].rearrange("p (c s) -> p c s", s=128),
                  ctx['kk_d'][kb].rearrange("c (t s) -> t c s", s=128), reads=['kk_d%d' % kb], writes=['InB'])

            def store(g, b0, fa0, nj, l=l, o=o):
                k = g % 2
                for bi in range(4):
                    r, jh = bi // 2, bi % 2
                    n = min(4, nj - jh * 4)
                    if n <= 0:
                        continue
                    S.op('act', lambda e, bi=bi, r=r, jh=jh, k=k, n=n: e.copy(
                        out=kfo[k][:, r, jh * 4:jh * 4 + n, :],
                        in_=PS[b0 + bi][:, 0:n * 128].rearrange("p (j c) -> p j c", j=n)),
                        reads=['ps%d' % (b0 + bi)], writes=['kfo%d' % k])
                r0 = ((l * 2 + o) * 5 + g) * 128
                S.dma('act', kl[r0:r0 + 128, :].rearrange("p (r j c) -> p r j c", r=2, j=8), kfo[k][:, :, :, :],
                      reads=['kfo%d' % k], writes=['kf_loc'])
            fwd_transform(ctx, h, 64, h['w1f'], 'w1f', store)
            lb = (l * 2 + o) * 640
            db = (l * 2 + o) * 2560
            for (lo_, n_, do_) in ((0, 256, 0), (256, 256, 1024), (512, 128, 2048)):
                S.allgather(kl[lb + lo_:lb + lo_ + n_, :], kd[db + do_:db + do_ + 4 * n_, :],
                            reads=['kf_loc'], writes=['kf_d'], groups=G4, big=True)
    S.allgather(ctx['rn_loc'][:, :], ctx['rn_d'][:, :], reads=['rn_loc'], writes=['rn_d'], groups=G4, big=True)
    S.sticky.update(['kf_d', 'rn_d'])
    S.barrier()
    AR.release()


def phase_hyena(ctx, l):
    S, AR, PS, ins = ctx['S'], ctx['AR'], ctx['PS'], ctx['ins']
    AR.mark()
    h = hy_alloc(ctx)
    InB = h['InB']
    NS = 2 * NA
    Y1 = AR.alloc([128, 128, NS], BF16, 'Y1')
    Y2 = AR.alloc([128, 128, NS], BF16, 'Y2')
    kfg = [AR.alloc([128, 2, 8, 128], BF16, 'kfg%d' % i) for i in range(2)]
    tmp = [AR.alloc([128, 512], F32, 'tmp%d' % i) for i in range(4)]
    G = AR.alloc([128, 3 * 128], BF16, 'G')
    H = AR.alloc([NS, 2048], BF16, 'H')
    rn = AR.alloc([128, 16], F32, 'rn')
    bia = AR.alloc([128, 2, 4], F32, 'bia')
    Bb = AR.alloc([NS, 128 * 128], BF16, 'Bb')
    gt = [AR.alloc([128, T], BF16, 'gt%d' % i) for i in range(2)]
    vt = [AR.alloc([128, T], BF16, 'vt%d' % i) for i in range(2)]
    bv = AR.alloc([128, T], F32, 'bv')
    yt = AR.alloc([128, T], F32, 'yt')
    ut = AR.alloc([128, T], BF16, 'ut')
    S.dma('sp', G[:, :], ins['t_g'][:, :], writes=['G'])
    S.dma('sp', H[:, :], ins['t_h'][:, :], writes=['H'])
    S.dma('sp', rn[:, :], ctx['rn_d'].rearrange("i p -> p i"), reads=['rn_d'], writes=['rn'],
          allow_slow_non_contiguous=True)
    for o in range(2):
        S.dma('sp', bia[:, o, :], ins['hy_bias'][l][o].rearrange("(c p) -> p c", p=128), writes=['bia'],
              allow_slow_non_contiguous=True)
    it = 0
    for o in range(2):
        src_all, src_own, skey_all, skey_own = ((ctx['hv_all'], ctx['hv_in'], 'hv_all', 'hv_in') if o == 0 else
                                                (ctx['hu_all'], ctx['hu_in'], 'hu_all', 'hu_in'))

        def load_in(cc, src_all=src_all, skey_all=skey_all):
            for r in range(2):
                S.dma('sp', InB[16 * r:16 * r + 16, :], src_all[r * 64 + cc * 16:r * 64 + (cc + 1) * 16, :],
                      reads=[skey_all], writes=['InB'])
        load_in(0)
        for cc in range(4):
            ib = it % 2
            it += 1
            S.dma('sp', gt[ib][:, :], ctx['g12_d'][o, cc * 128:(cc + 1) * 128, :], reads=['g12_d'],
                  writes=['gt%d' % ib])
            S.dma('sp', vt[ib][:, :].rearrange("c (t s) -> c t s", s=128),
                  src_own[cc * 16:(cc + 1) * 16, :].rearrange("t (c s) -> c t s", s=128), reads=[skey_own],
                  writes=['vt%d' % ib])

            def load_kf(g, l=l, o=o, cc=cc):
                k = g % 2
                nj_ = FGROUPS[g][1]
                r0 = kf_row(l, o, cc, g)
                S.dma('sp', kfg[k][:, :, 0:nj_, :],
                      ctx['kf_d'][r0:r0 + 128, :].rearrange("p (r j c) -> p r j c", r=2, j=8)[:, :, 0:nj_, :],
                      reads=['kf_d'], writes=['kfg%d' % k])

            def pre2(load_kf=load_kf):
                load_kf(0)
                load_kf(1)

            def prod(g, b0, fa0, nj, l=l, o=o, cc=cc, load_kf=load_kf):
                k = g % 2
                for hb in range((nj + 3) // 4):
                    n = min(4, nj - hb * 4)
                    xre, xim = PS[b0 + hb], PS[b0 + 2 + hb]
                    kre = kfg[k][:, 0, hb * 4:hb * 4 + n, :]
                    kim = kfg[k][:, 1, hb * 4:hb * 4 + n, :]
                    v3 = lambda t, n=n: t[:, 0:n * 128].rearrange("p (j c) -> p j c", j=n)
                    for ti, (xa, ka) in enumerate(((xre, kre), (xim, kim), (xre, kim), (xim, kre))):
                        S.op('dve', lambda e, ti=ti, xa=xa, ka=ka, v3=v3: e.tensor_tensor(
                            out=v3(tmp[ti]), in0=v3(xa), in1=ka, op=ALU.mult),
                            reads=['ps%d' % (b0 + hb), 'ps%d' % (b0 + 2 + hb), 'kfg%d' % k], writes=['tmp%d' % ti])
                    f0 = fa0 + hb * 4
                    v3t = lambda t, n=n: t[:, 0:n * 128].rearrange("p (j c) -> p c j", j=n)
                    S.op('pool', lambda e, f0=f0, n=n, v3t=v3t: e.tensor_tensor(
                        out=Y1[:, :, f0:f0 + n], in0=v3t(tmp[0]), in1=v3t(tmp[1]), op=ALU.subtract),
                        reads=['tmp0', 'tmp1'], writes=['Y1'])
                    S.op('dve', lambda e, f0=f0, n=n, v3t=v3t: e.tensor_tensor(
                        out=Y1[:, :, NA + f0:NA + f0 + n], in0=v3t(tmp[2]), in1=v3t(tmp[3]), op=ALU.add),
                        reads=['tmp2', 'tmp3'], writes=['Y1'])
                    S.op('act', lambda e, f0=f0, n=n: e.activation(out=Y2[:, :, f0:f0 + n],
                                                                   in_=Y1[:, :, NA + f0:NA + f0 + n], func=AF.Copy,
                                                                   scale=-1.0), reads=['Y1'], writes=['Y2'])
                    S.op('act', lambda e, f0=f0, n=n: e.copy(out=Y2[:, :, NA + f0:NA + f0 + n],
                                                             in_=Y1[:, :, f0:f0 + n]), reads=['Y1'], writes=['Y2'])
                if g + 2 < len(FGROUPS):
                    load_kf(g + 2)

            def after1(cc=cc, load_in=load_in):
                if cc < 3:
                    load_in(cc + 1)
            fwd_transform(ctx, h, 32, h['w1d'], 'w1d', prod, after_stage1=after1, pre_stage2=pre2)
            Bv = Bb[0:NS, :].rearrange("p (t c) -> p t c", t=128)
            S.op('dve', lambda e, o=o, cc=cc, ib=ib: e.tensor_scalar(
                out=bv[:, :], in0=vt[ib][:, :], scalar1=bia[:, o, cc:cc + 1], scalar2=None, op0=ALU.mult),
                reads=['vt%d' % ib, 'bia'], writes=['bv'])
            for c4 in range(32):
                bank = c4 % 4
                for ci in range(4):
                    c = c4 * 4 + ci
                    S.op('pe', lambda e, bank=bank, c=c, ci=ci: e.matmul(
                        PS[bank][0:NS, ci * 128:(ci + 1) * 128], lhsT=Y1[:, c, :], rhs=G[:, 0:128],
                        start=True, stop=False), reads=['Y1', 'G'], writes=['ps%d' % bank])
                    S.op('pe', lambda e, bank=bank, c=c, ci=ci: e.matmul(
                        PS[bank][0:NS, ci * 128:(ci + 1) * 128], lhsT=Y2[:, c, :], rhs=G[:, 128:256],
                        start=False, stop=True), reads=['Y2', 'G'], writes=['ps%d' % bank])
                srcv = PS[bank][0:NS, :].rearrange("p (c t) -> p t c", c=4)
                if c4 % 2 == 0:
                    S.op('act', lambda e, c4=c4, srcv=srcv: e.copy(out=Bv[:, :, c4 * 4:c4 * 4 + 4], in_=srcv),
                         reads=['ps%d' % bank], writes=['Bb'])
                else:
                    S.op('dve', lambda e, c4=c4, srcv=srcv: e.tensor_copy(out=Bv[:, :, c4 * 4:c4 * 4 + 4], in_=srcv),
                         reads=['ps%d' % bank], writes=['Bb'])
            for tA in range(128):
                bank = 4 + tA // 32
                col = (tA % 32) * 16
                S.op('pe', lambda e, tA=tA, bank=bank, col=col: e.matmul(
                    PS[bank][:, col:col + 16], lhsT=Bv[:, tA, :], rhs=H[:, tA * 16:(tA + 1) * 16],
                    start=True, stop=True), reads=['Bb', 'H'], writes=['ps%d' % bank])
            ridx = cc * 4 + l * 2 + o
            for b in range(4):
                dst = lambda t, b=b: t[:, :].rearrange("p (tb ta) -> p ta tb", ta=128)[:, 32 * b:32 * b + 32, :]
                S.op('dve', lambda e, b=b, dst=dst, ridx=ridx: e.scalar_tensor_tensor(
                    out=dst(yt), in0=PS[4 + b][:, :].rearrange("p (ta tb) -> p ta tb", tb=16),
                    scalar=rn[:, ridx:ridx + 1], in1=dst(bv), op0=ALU.mult, op1=ALU.add),
                    reads=['ps%d' % (4 + b), 'rn', 'bv'], writes=['yt'])
            S.op('pool', lambda e, ib=ib: e.tensor_tensor(out=ut[:, :], in0=yt[:, :], in1=gt[ib][:, :], op=ALU.mult),
                 reads=['yt', 'gt%d' % ib], writes=['ut'])
            if o == 0:
                S.dma('pool', ctx['hu_in'][cc * 16:(cc + 1) * 16, :].rearrange("t (c s) -> c t s", s=128),
                      ut[:, :].rearrange("c (t s) -> c t s", s=128), reads=['ut'], writes=['hu_in'])
            else:
                S.dma('pool', ctx['mixT_d'][1536 + cc * 128:1536 + (cc + 1) * 128, :], ut[:, :], reads=['ut'],
                      writes=['mixT_d'])
        if o == 0:
            S.allgather(ctx['hu_in'][:, :], ctx['hu_all'][:, :], reads=['hu_in'], writes=['hu_all'])
    S.barrier()
    AR.release()
```

```python
import math
import numpy as np
import ml_dtypes
import concourse.bass as bass
import concourse.mybir as mybir
from concourse.bass_utils import run_bass_kernel_spmd

F32 = mybir.dt.float32
BF16 = mybir.dt.bfloat16
AF = mybir.ActivationFunctionType
ALU = mybir.AluOpType
AX = mybir.AxisListType

D = 2048
DEPTH = 2
T = 2048
HALO = 128
TE = T + 2 * HALO
DIN = 4352
DFF = 5632
NFF = DFF // 128
EPS = 1e-6
NFFT = 8192
PAIRS = [[0, 1], [2, 3], [4, 5], [6, 7]]
ENGS = ['pe', 'act', 'dve', 'pool', 'sp']


class Sched:
    def __init__(self, nc, es):
        self.nc = nc
        self.ops = {e: [] for e in ENGS}
        self.cnt = {e: 0 for e in ENGS}
        self.sem = {e: es.enter_context(nc.semaphore('s_' + e)) for e in ['pe', 'act', 'dve', 'pool']}
        self.R = 8
        self.dsem = {q: [es.enter_context(nc.semaphore('d_%s%d' % (q, i))) for i in range(self.R)]
                     for q in ['sp', 'act', 'pool']}
        self.dcnt = {q: 0 for q in ['sp', 'act', 'pool']}
        self.ccsem = es.enter_context(nc.semaphore('ccsem'))
        self.cccnt = 0
        self.ccsem2 = es.enter_context(nc.semaphore('ccsem2'))
        self.sticky = set()
        self.waited = {e: {} for e in ENGS}
        self.lastw = {}
        self.reads = {}
        self.same_engine_sync = True

    def _wait(self, eng, tok):
        sk, sh, val = tok
        if self.waited[eng].get(sk, 0) >= val:
            return
        if sk == eng and (eng == 'pe' or not self.same_engine_sync):
            return
        self.waited[eng][sk] = val
        self.ops[eng].append(lambda e, sh=sh, val=val: e.wait_ge(sh, val))

    def _deps(self, eng, reads, writes):
        for k in reads:
            if k in self.lastw:
                self._wait(eng, self.lastw[k])
        for k in writes:
            if k in self.lastw:
                self._wait(eng, self.lastw[k])
            for tok in self.reads.get(k, {}).values():
                self._wait(eng, tok)

    def _commit(self, tok, reads, writes):
        for k in writes:
            self.lastw[k] = tok
            self.reads[k] = {}
        for k in reads:
            d = self.reads.setdefault(k, {})
            if tok[0] not in d or d[tok[0]][2] < tok[2]:
                d[tok[0]] = tok

    def op(self, eng, fn, reads=(), writes=()):
        self._deps(eng, reads, writes)
        self.cnt[eng] += 1
        val = self.cnt[eng]
        sh = self.sem[eng]
        self.ops[eng].append(lambda e, fn=fn, sh=sh: fn(e).then_inc(sh, 1))
        self._commit((eng, sh, val), reads, writes)

    def dma(self, q, out, in_, reads=(), writes=(), **kw):
        i = self.dcnt[q]
        self.dcnt[q] += 1
        slot = i % self.R
        val = 16 * (i // self.R + 1)
        sh = self.dsem[q][slot]
        sk = 'd_%s%d' % (q, slot)
        if val > 16:
            self._wait(q, (sk, sh, val - 16))
        self._deps(q, reads, writes)
        self.ops[q].append(lambda e, out=out, in_=in_, sh=sh, kw=kw:
                           e.dma_start(out=out, in_=in_, **kw).then_inc(sh, 16))
        self._commit((sk, sh, val), reads, writes)

    def allgather(self, in_ap, out_ap, reads=(), writes=(), groups=None, big=False):
        q = 'pool'
        self._deps(q, reads, writes)
        if big:
            self.cc2cnt = getattr(self, 'cc2cnt', 0) + 1
            sh, sk, val = self.ccsem2, 'cc2', self.cc2cnt
        else:
            self.cccnt += 1
            sh, sk, val = self.ccsem, 'cc', self.cccnt
        groups = groups or PAIRS
        self.ops[q].append(lambda e, sh=sh, groups=groups: e.collective_compute(
            "AllGather", ALU.bypass, replica_groups=groups,
            ins=[in_ap], outs=[out_ap]).then_inc(sh))
        self._commit((sk, sh, val), reads, writes)

    def barrier(self):
        toks = []
        for e in ['pe', 'act', 'dve', 'pool']:
            if self.cnt[e] > 0:
                toks.append((e, self.sem[e], self.cnt[e]))
        for q in ['sp', 'act', 'pool']:
            n = self.dcnt[q]
            for slot in range(self.R):
                uses = (n - slot + self.R - 1) // self.R if n > slot else 0
                if uses > 0:
                    toks.append(('d_%s%d' % (q, slot), self.dsem[q][slot], 16 * uses))
        if self.cccnt:
            toks.append(('cc', self.ccsem, self.cccnt))
        for e in ENGS:
            for tok in toks:
                if tok[0] == e and e == 'pe':
                    continue
                sk, sh, val = tok
                if self.waited[e].get(sk, 0) >= val:
                    continue
                self.waited[e][sk] = val
                self.ops[e].append(lambda en, sh=sh, val=val: en.wait_ge(sh, val))
        self.lastw = {k: v for k, v in self.lastw.items() if k in self.sticky}
        self.reads = {}

    def emit(self):
        self.barrier()
        nc = self.nc
        ops = self.ops
        with nc.Block() as block:
            @block.tensor
            def _(e):
                for f in ops['pe']:
                    f(e)

            @block.scalar
            def _(e):
                for f in ops['act']:
                    f(e)

            @block.vector
            def _(e):
                for f in ops['dve']:
                    f(e)

            @block.gpsimd
            def _(e):
                for f in ops['pool']:
                    f(e)

            @block.sync
            def _(e):
                for f in ops['sp']:
                    f(e)


class Arena:
    def __init__(self, nc, base, limit):
        self.nc = nc
        self.base = base
        self.limit = limit
        self.off = base
        self.n = 0
        self.marks = []

    def mark(self):
        self.marks.append(self.off)

    def release(self):
        self.off = self.marks.pop()

    def alloc(self, shape, dtype, name=None):
        nbytes = int(np.prod(shape[1:])) * (2 if dtype == BF16 else 4)
        nbytes = (nbytes + 63) // 64 * 64
        off = self.off
        assert off + nbytes <= self.limit, ("SBUF arena overflow", name, off, nbytes, self.limit)
        self.off += nbytes
        self.n += 1
        return self.nc.alloc_sbuf_tensor_at("%s_%d" % (name or 't', self.n), list(shape), dtype, offset=off)


def sap(t, part0, nparts, dims, off=0):
    full = t[:]
    pstep = full.ap[0][0]
    return bass.AP(t, full.offset + part0 * pstep + off, [[pstep, nparts]] + [[s, c] for (s, c) in dims])


def build(dbg=None):
    nc = bass.Bass("TRN2", target_bir_lowering=False)
    from contextlib import ExitStack
    es = ExitStack()
    S = Sched(nc, es)
    dbg = dbg or {}
    ins = {}

    def din(name, shape, dt=F32):
        ins[name] = nc.dram_tensor(name, list(shape), dt, kind="ExternalInput")
        return ins[name]

    x_ext = din("x_ext", [TE, D])
    mix_g = din("mix_norm_g", [DEPTH, D])
    w_in = din("w_in", [DEPTH, D, DIN])
    pool_w = din("pool_w", [DEPTH, 4, 128, 128])
    pool_scale = din("pool_scale", [DEPTH, 512])
    attn_sink = din("attn_sink", [DEPTH, 8])
    sconv_w = din("sconv_w", [DEPTH, 3, 512])
    hy_short_w = din("hy_short_w", [DEPTH, 3, 1536])
    hy_w1 = din("hy_w1", [DEPTH, 33, 64])
    hy_b1 = din("hy_b1", [DEPTH, 64])
    hy_w2 = din("hy_w2", [DEPTH, 64, 64])
    hy_b2 = din("hy_b2", [DEPTH, 64])
    hy_w3 = din("hy_w3_sh", [DEPTH, 64, 512])
    hy_freq = din("hy_freq", [DEPTH, 64])
    hy_decay = din("hy_decay_sh", [DEPTH, 512])
    hy_bias = din("hy_bias", [DEPTH, 2, 512])
    w_out = din("w_out", [DEPTH, D, D])
    ffn_g = din("ffn_norm_g", [DEPTH, D])
    w_gu = din("w_gate_up", [DEPTH, D, 2 * DFF])
    w_down = din("w_down", [DEPTH, DFF, D])
    fin_g = din("final_norm_g", [D])
    t_ident = din("t_ident", [128, 128], BF16)
    t_rope = din("t_rope", [2, 16, TE])
    t_perm = din("t_perm", [16, 16])
    t_mask = din("t_mask", [4, 128, 128], BF16)
    t_invcnt = din("t_invcnt", [4, T])
    t_flags = din("t_flags", [128, 2])
    t_w1d = din("t_w1d", [32, 66], BF16)
    t_w1f = din("t_w1f", [64, 66], BF16)
    t_m = din("t_m", [5, 128, 3 * 8 * 128], BF16)
    t_g = din("t_g", [128, 3 * 128], BF16)
    t_h = din("t_h", [66, 128 * 16], BF16)
    t_pos = din("t_pos", [33, NFFT])
    t_tdec = din("t_tdec", [1, NFFT])

    y_out = nc.dram_tensor("y_out", [T, D], F32, kind="ExternalOutput")
    dbg_out = {}
    for k, shp in dbg.items():
        dbg_out[k] = nc.dram_tensor("dbg_" + k, list(shp[0]), shp[1], kind="ExternalOutput")

    x1_ext = nc.dram_tensor("x1_ext", [TE, D], F32)
    xmid = nc.dram_tensor("xmid", [T, D], F32)
    mixT_d = nc.dram_tensor("mixT_d", [D, T], BF16)
    edge_in = nc.dram_tensor("edge_in", [256, D], F32)
    edge_all = nc.dram_tensor("edge_all", [512, D], F32)
    hv_in = nc.dram_tensor("hv_in", [64, 128 * 128], BF16)
    hv_all = nc.dram_tensor("hv_all", [128, 128 * 128], BF16)
    hu_in = nc.dram_tensor("hu_in", [64, 128 * 128], BF16)
    hu_all = nc.dram_tensor("hu_all", [128, 128 * 128], BF16)
    g12_d = nc.dram_tensor("g12_d", [2, 512, T], BF16)
    kk_d = nc.dram_tensor("kk_d", [2, 128, NFFT], BF16)
    kf_d = nc.dram_tensor("kf_d", [4 * 2560, 2 * 8 * 128], BF16)
    kf_loc = nc.dram_tensor("kf_loc", [4 * 640, 2 * 8 * 128], BF16)
    rn_loc = nc.dram_tensor("rn_loc", [4, 128], F32)
    rn_d = nc.dram_tensor("rn_d", [16, 128], F32)

    AR = Arena(nc, 16640, 229312)
    PS = [nc.alloc_psum_tensor("ps%d" % i, [128, 512], F32) for i in range(8)]

    def psb(i):
        return PS[i]

    ident = AR.alloc([128, 128], BF16, "ident")
    S.dma('sp', ident[:, :], t_ident[:, :], writes=['ident'])
    gcolM = AR.alloc([128, DEPTH, 16], F32, "gcolM")
    gcolF = AR.alloc([128, DEPTH, 16], F32, "gcolF")
    for l in range(DEPTH):
        S.dma('sp', gcolM[:, l, :], mix_g[l].rearrange("(k p) -> p k", p=128), writes=['gcol'],
              allow_slow_non_contiguous=True)
        S.dma('sp', gcolF[:, l, :], ffn_g[l].rearrange("(k p) -> p k", p=128), writes=['gcol'],
              allow_slow_non_contiguous=True)
    ctx = dict(nc=nc, S=S, AR=AR, PS=PS, ident=ident, gcolM=gcolM, gcolF=gcolF, ins=ins,
               dbg=dbg, dbg_out=dbg_out)
    ctx.update(x_ext=x_ext, x1_ext=x1_ext, xmid=xmid, mixT_d=mixT_d, y_out=y_out,
               edge_in=edge_in, edge_all=edge_all, hv_in=hv_in, hv_all=hv_all,
               hu_in=hu_in, hu_all=hu_all, g12_d=g12_d, kk_d=kk_d, kf_d=kf_d, kf_loc=kf_loc,
               rn_loc=rn_loc, rn_d=rn_d)
    return nc, S, ctx, es


def rms_to_T(ctx, xt, gcol_l, hT, col0, uid):
    S, AR, PS, ident = ctx['S'], ctx['AR'], ctx['PS'], ctx['ident']
    sc = ctx['sc_junk']
    ss = ctx['sc_ss']
    xn = ctx['sc_xn']
    S.op('act', lambda e: e.activation(out=sc[:, :], in_=xt[:, :], func=AF.Square, accum_out=ss[:, 0:1]),
         reads=[uid], writes=[ctx.get('junk_key', 'sc_junk'), 'sc_ss'])
    S.op('dve', lambda e: e.tensor_scalar(out=ss[:, 1:2], in0=ss[:, 0:1], scalar1=1.0 / D, scalar2=EPS,
                                          op0=ALU.mult, op1=ALU.add), reads=['sc_ss'], writes=['sc_ss1'])
    S.op('act', lambda e: e.activation(out=ss[:, 3:4], in_=ss[:, 1:2], func=AF.Sqrt),
         reads=['sc_ss1'], writes=['sc_ss3'])
    S.op('dve', lambda e: e.reciprocal(out=ss[:, 2:3], in_=ss[:, 3:4]), reads=['sc_ss3'], writes=['sc_ss2'])
    S.op('act', lambda e: e.activation(out=xn[:, :], in_=xt[:, :], func=AF.Copy, scale=ss[:, 2:3]),
         reads=[uid, 'sc_ss2'], writes=['sc_xn'])
    for q in range(4):
        bank = ctx['tp_bank'][ctx['tp_i'] % 2]
        ctx['tp_i'] += 1
        pst = PS[bank][:, :].bitcast(BF16)
        for j in range(4):
            kc = q * 4 + j
            S.op('pe', lambda e, kc=kc, j=j, pst=pst: e.transpose(out=pst[:, j * 128:(j + 1) * 128],
                                                                    in_=xn[:, kc * 128:(kc + 1) * 128],
                                                                    identity=ident[:, :]),
                 reads=['sc_xn', 'ident'], writes=['ps%d' % bank])
        S.op('dve', lambda e, q=q, pst=pst: e.tensor_tensor(
            out=hT[:, q * 4:(q + 1) * 4, col0:col0 + 128],
            in0=pst[:, 0:512].rearrange("p (a b) -> p a b", a=4),
            in1=gcol_l[:, q * 4:(q + 1) * 4].unsqueeze(2).to_broadcast([128, 4, 128]),
            op=ALU.mult), reads=['ps%d' % bank, 'gcol'], writes=['hT'])


def wload(ctx, dst, src, key):
    ctx['S'].dma('pool', dst, src, writes=[key])


def proj(ctx, wblk, wkey, c0, M, hT, tok0, ntok, bank):
    S, PS = ctx['S'], ctx['PS']
    for kc in range(16):
        S.op('pe', lambda e, kc=kc: e.matmul(PS[bank][0:M, 0:ntok], lhsT=wblk[:, kc, c0:c0 + M],
                                               rhs=hT[:, kc, tok0:tok0 + ntok],
                                               start=(kc == 0), stop=(kc == 15)),
             reads=[wkey, 'hT'], writes=['ps%d' % bank])


TOKG = [(0, 512), (512, 512), (1024, 512), (1536, 512), (2048, 256)]


def phase_norm(ctx, l, xsrc, tiles=None):
    S, AR = ctx['S'], ctx['AR']
    hT = ctx['hT']
    AR.mark()
    ctx['sc_junk'] = AR.alloc([128, D], F32, 'junk')
    ctx['junk_key'] = 'sc_junk'
    ctx['sc_ss'] = AR.alloc([128, 4], F32, 'ss')
    ctx['sc_xn'] = AR.alloc([128, D], BF16, 'xn')
    xt = [AR.alloc([128, D], F32, 'xt%d' % i) for i in range(2)]
    ctx['tp_bank'] = [0, 1]
    ctx['tp_i'] = 0
    for n, i in enumerate(tiles if tiles is not None else range(TE // 128)):
        b = n % 2
        S.dma('sp', xt[b][:, :], xsrc[i * 128:(i + 1) * 128, :], writes=['xt%d' % b])
        rms_to_T(ctx, xt[b], ctx['gcolM'][:, l, :], hT, i * 128, 'xt%d' % b)
    S.barrier()
    AR.release()


def conv3(ctx, eng, out, zin, wcol, c_lo, n, rkey, wkey_out):
    S = ctx['S']
    S.op(eng, lambda e: e.tensor_scalar(out=out[:, 0:n], in0=zin[:, c_lo:c_lo + n], scalar1=wcol[:, 1:2],
                                        scalar2=None, op0=ALU.mult), reads=[rkey, 'cw'], writes=[wkey_out])
    S.op(eng, lambda e: e.scalar_tensor_tensor(out=out[:, 0:n], in0=zin[:, c_lo - 1:c_lo - 1 + n],
                                               scalar=wcol[:, 0:1], in1=out[:, 0:n], op0=ALU.mult, op1=ALU.add),
         reads=[rkey, 'cw', wkey_out], writes=[wkey_out])
    S.op(eng, lambda e: e.scalar_tensor_tensor(out=out[:, 0:n], in0=zin[:, c_lo + 1:c_lo + 1 + n],
                                               scalar=wcol[:, 2:3], in1=out[:, 0:n], op0=ALU.mult, op1=ALU.add),
         reads=[rkey, 'cw', wkey_out], writes=[wkey_out])


def phase_local(ctx, l):
    S, AR, PS, ins = ctx['S'], ctx['AR'], ctx['PS'], ctx['ins']
    hT = ctx['hT']
    w_in = ins['w_in']
    AR.mark()
    wb = [AR.alloc([128, 16, 128], BF16, 'wb%d' % i) for i in range(3)]
    zc = [AR.alloc([128, TE], F32, 'zc%d' % i) for i in range(3)]
    sA = AR.alloc([128, TE], F32, 'sA')
    sB = AR.alloc([128, TE], F32, 'sB')
    invc = AR.alloc([128, T], F32, 'invc')
    dT = AR.alloc([128, T], BF16, 'dT')
    ob = [AR.alloc([128, T], BF16, 'ob%d' % i) for i in range(2)]
    pw = AR.alloc([128, 4, 128], BF16, 'pw')
    pscale = AR.alloc([128, 4], F32, 'pscale')
    cwC = AR.alloc([128, 4, 3], F32, 'cwC')
    cwD = AR.alloc([128, 12, 3], F32, 'cwD')
    wload(ctx, pw[:, :, :], ins['pool_w'][l].rearrange("g c d -> c g d"), 'pw')
    S.dma('sp', pscale[:, :], ins['pool_scale'][l].rearrange("(g p) -> p g", p=128), writes=['pscale'],
          allow_slow_non_contiguous=True)
    for k in range(3):
        S.dma('sp', cwC[:, :, k], ins['sconv_w'][l][k].rearrange("(c p) -> p c", p=128), writes=['cw'],
              allow_slow_non_contiguous=True)
        S.dma('sp', cwD[:, :, k], ins['hy_short_w'][l][k].rearrange("(c p) -> p c", p=128), writes=['cw'],
              allow_slow_non_contiguous=True)
    st = {'wi': 0, 'pb': 0, 'oi': 0}

    def zblock(blk, zi):
        w = st['wi'] % 3
        st['wi'] += 1
        wload(ctx, wb[w][:, :, :], w_in[l][:, blk * 128:(blk + 1) * 128].rearrange("(k p) c -> p k c", p=128),
              'wb%d' % w)
        for (t0, n) in TOKG:
            bank = 2 + st['pb'] % 4
            st['pb'] += 1
            proj(ctx, wb[w], 'wb%d' % w, 0, 128, hT, t0, n, bank)
            S.op('act', lambda e, bank=bank, t0=t0, n=n: e.copy(out=zc[zi][:, t0:t0 + n], in_=PS[bank][:, 0:n]),
                 reads=['ps%d' % bank], writes=['zc%d' % zi])

    def out_rows(row0, obuf, okey):
        S.dma('act', ctx['mixT_d'][row0:row0 + 128, :], obuf[:, :], reads=[okey], writes=['mixT_d'])

    for g in range(4):
        zblock(g, 0)
        z = zc[0]
        S.dma('sp', invc[:, :], ins['t_invcnt'][g:g + 1, :].to_broadcast([128, T]), writes=['invc'])
        S.op('dve', lambda e: e.tensor_tensor(out=sA[:, 1:TE], in0=z[:, 0:TE - 1], in1=z[:, 1:TE], op=ALU.add),
             reads=['zc0'], writes=['sA'])
        cur, ckey, oth, okey = sA, 'sA', sB, 'sB'
        vlo, vhi = 1, TE
        for lev in range(g):
            h = 1 << lev
            lo, hi = vlo + h, vhi - h
            S.op('dve', lambda e, cur=cur, oth=oth, h=h, lo=lo, hi=hi: e.tensor_tensor(
                out=oth[:, lo:hi], in0=cur[:, lo - h:hi - h], in1=cur[:, lo + h:hi + h], op=ALU.add),
                reads=[ckey], writes=[okey])
            vlo, vhi = lo, hi
            cur, ckey, oth, okey = oth, okey, cur, ckey
        S.op('dve', lambda e, cur=cur, oth=oth: e.tensor_tensor(out=oth[:, 0:T], in0=cur[:, HALO:HALO + T], in1=invc[:, :],
                                                       op=ALU.mult), reads=[ckey, 'invc'], writes=[okey])
        S.op('dve', lambda e, oth=oth: e.tensor_tensor(out=dT[:, :], in0=oth[:, 0:T], in1=z[:, HALO:HALO + T],
                                                       op=ALU.subtract), reads=[okey, 'zc0'], writes=['dT'])
        o = st['oi'] % 2
        st['oi'] += 1
        for q in range(4):
            bank = 6 + q % 2
            S.op('pe', lambda e, q=q, bank=bank, g=g: e.matmul(PS[bank][:, :], lhsT=pw[:, g, :],
                                                               rhs=dT[:, q * 512:(q + 1) * 512], start=True, stop=True),
                 reads=['pw', 'dT'], writes=['ps%d' % bank])
            S.op('act', lambda e, q=q, bank=bank, g=g, o=o: e.activation(
                out=ob[o][:, q * 512:(q + 1) * 512], in_=PS[bank][:, :], func=AF.Copy, scale=pscale[:, g:g + 1]),
                reads=['ps%d' % bank, 'pscale'], writes=['ob%d' % o])
        out_rows(g * 128, ob[o], 'ob%d' % o)

    for c in range(4):
        zblock(10 + c, 0)
        zblock(14 + c, 1)
        zblock(18 + c, 2)
        S.op('dve', lambda e: e.tensor_tensor(out=sA[:, :], in0=zc[2][:, :], in1=zc[0][:, :], op=ALU.mult),
             reads=['zc0', 'zc2'], writes=['sA'])
        conv3(ctx, 'dve', sB, sA, cwC[:, c, :], HALO, T, 'sA', 'sB')
        o = st['oi'] % 2
        st['oi'] += 1
        S.op('dve', lambda e, o=o: e.tensor_tensor(out=ob[o][:, :], in0=sB[:, 0:T], in1=zc[1][:, HALO:HALO + T],
                                                   op=ALU.mult), reads=['sB', 'zc1'], writes=['ob%d' % o])
        out_rows(1024 + c * 128, ob[o], 'ob%d' % o)

    for j in range(12):
        zi = j % 3
        zblock(22 + j, zi)
        o = st['oi'] % 2
        st['oi'] += 1
        eng = 'dve'
        conv3(ctx, eng, sA if j % 2 == 0 else sB, zc[zi], cwD[:, j, :], HALO, T, 'zc%d' % zi,
              'sA' if j % 2 == 0 else 'sB')
        src = sA if j % 2 == 0 else sB
        S.op('act', lambda e, o=o, src=src: e.copy(out=ob[o][:, :], in_=src[:, 0:T]),
             reads=['sA' if j % 2 == 0 else 'sB'], writes=['ob%d' % o])
        if j < 4:
            S.dma('act', ctx['hv_in'][j * 16:(j + 1) * 16, :].rearrange("t (c s) -> c t s", s=128),
                  ob[o][:, :].rearrange("c (t s) -> c t s", s=128), reads=['ob%d' % o], writes=['hv_in'])
        else:
            jj = j - 4
            S.dma('act', ctx['g12_d'][jj // 4, (jj % 4) * 128:(jj % 4 + 1) * 128, :], ob[o][:, :],
                  reads=['ob%d' % o], writes=['g12_d'])
    S.barrier()
    AR.release()
    S.allgather(ctx['hv_in'][:, :], ctx['hv_all'][:, :], reads=['hv_in'], writes=['hv_all'])


def phase_attn(ctx, l):
    S, AR, PS, ins, ident = ctx['S'], ctx['AR'], ctx['PS'], ctx['ins'], ctx['ident']
    hT = ctx['hT']
    w_in = ins['w_in']
    AR.mark()
    wb = [AR.alloc([128, 16, 128], BF16, 'wb%d' % i) for i in range(2)]
    qraw = AR.alloc([64, TE], F32, 'qraw')
    rope = AR.alloc([16, 2, TE], F32, 'rope')
    rt1 = AR.alloc([16, 512], F32, 'rt1')
    rt2 = AR.alloc([16, 512], F32, 'rt2')
    perm = AR.alloc([16, 16], F32, 'perm')
    qk = AR.alloc([64, 10, TE], BF16, 'qk')
    vp = AR.alloc([128, 18, 2, 66], BF16, 'vp')
    E = [AR.alloc([128, 512], BF16, 'E%d' % i) for i in range(6)]
    Ot = AR.alloc([128, 512], BF16, 'Ot')
    den = AR.alloc([128, 8], F32, 'den')
    mixB = AR.alloc([128, 4, T], BF16, 'mixB')
    masks = AR.alloc([128, 4, 128], BF16, 'masks')
    snk = AR.alloc([128, 8], F32, 'snk')
    S.dma('sp', rope[:, :, :], ins['t_rope'].rearrange("a d t -> d a t"), writes=['rope'])
    S.dma('sp', perm[:, :], ins['t_perm'][:, :], writes=['perm'])
    S.dma('sp', masks[:, :, :], ins['t_mask'].rearrange("m k q -> k m q"), writes=['masks'])
    S.dma('sp', snk[:, :], ins['attn_sink'][l:l + 1, :].to_broadcast([128, 8]), writes=['snk'])
    S.op('act', lambda e: e.activation(out=snk[:, :], in_=snk[:, :], func=AF.Exp), reads=['snk'], writes=['snk'])
    S.op('pool', lambda e: e.memset(vp[:, :, :, :], 1.0), writes=['vp'])
    for hh in range(10):
        blk = 4 + hh // 2
        w = (hh // 2) % 2
        if hh % 2 == 0:
            wload(ctx, wb[w][:, :, :], w_in[l][:, blk * 128:(blk + 1) * 128].rearrange("(k p) c -> p k c", p=128),
                  'wb%d' % w)
        for gi, (t0, n) in enumerate(TOKG):
            bank = 2 + gi % 3
            proj(ctx, wb[w], 'wb%d' % w, (hh % 2) * 64, 64, hT, t0, n, bank)
            S.op('act', lambda e, bank=bank, t0=t0, n=n: e.copy(out=qraw[:, t0:t0 + n], in_=PS[bank][0:64, 0:n]),
                 reads=['ps%d' % bank], writes=['qraw'])
            S.op('act', lambda e, hh=hh, t0=t0, n=n: e.copy(out=qk[:, hh, t0:t0 + n], in_=qraw[:, t0:t0 + n]),
                 reads=['qraw'], writes=['qk'])
            S.op('pe', lambda e, t0=t0, n=n: e.matmul(PS[5][0:16, 0:n], lhsT=perm[:, :], rhs=qraw[0:16, t0:t0 + n],
                                                      start=True, stop=True),
                 reads=['perm', 'qraw'], writes=['ps5'])
            S.op('dve', lambda e, t0=t0, n=n: e.tensor_tensor(out=rt1[:, 0:n], in0=qraw[0:16, t0:t0 + n],
                                                              in1=rope[:, 0, t0:t0 + n], op=ALU.mult),
                 reads=['qraw', 'rope'], writes=['rt1'])
            S.op('dve', lambda e, t0=t0, n=n: e.tensor_tensor(out=rt2[:, 0:n], in0=PS[5][0:16, 0:n],
                                                              in1=rope[:, 1, t0:t0 + n], op=ALU.mult),
                 reads=['ps5', 'rope'], writes=['rt2'])
            S.op('dve', lambda e, hh=hh, t0=t0, n=n: e.tensor_tensor(out=qk[0:16, hh, t0:t0 + n], in0=rt1[:, 0:n],
                                                                     in1=rt2[:, 0:n], op=ALU.add),
                 reads=['rt1', 'rt2', 'qk'], writes=['qk'])
    wload(ctx, wb[1][:, :, :], w_in[l][:, 9 * 128:10 * 128].rearrange("(k p) c -> p k c", p=128), 'wb1')
    for i in range(18):
        bank = 2 + i % 3
        for kc in range(16):
            S.op('pe', lambda e, kc=kc, i=i, bank=bank: e.matmul(PS[bank][:, 0:128], lhsT=hT[:, kc, i * 128:(i + 1) * 128],
                                                                   rhs=wb[1][:, kc, :], start=(kc == 0), stop=(kc == 15)),
                 reads=['wb1', 'hT'], writes=['ps%d' % bank])
        S.op('act', lambda e, i=i, bank=bank: e.copy(out=vp[:, i, :, 0:64],
                                                     in_=PS[bank][:, 0:128].rearrange("p (g d) -> p g d", g=2)),
             reads=['ps%d' % bank], writes=['vp'])
    for i in range(16):
        for g in range(2):
            eb = 3 * g
            for jj in range(3):
                bank = 2 + eb + jj
                S.op('pe', lambda e, i=i, g=g, jj=jj, bank=bank: e.matmul(
                    PS[bank][:, :].rearrange("p (h q) -> p h q", h=4),
                    lhsT=qk[:, 8 + g, (i + jj) * 128:(i + jj + 1) * 128],
                    rhs=qk[:, 4 * g:4 * g + 4, (i + 1) * 128:(i + 2) * 128], start=True, stop=True),
                    reads=['qk'], writes=['ps%d' % bank])
                S.op('act', lambda e, bank=bank, k=eb + jj: e.activation(out=E[k][:, :], in_=PS[bank][:, :],
                                                                       func=AF.Exp, scale=0.125),
                     reads=['ps%d' % bank], writes=['E%d' % (eb + jj)])
            mP = 0 if i == 0 else 1
            mN = 3 if i == 15 else 2
            S.op('dve', lambda e, k=eb, mP=mP: e.tensor_tensor(
                out=E[k][:, :].rearrange("p (h q) -> p h q", h=4), in0=E[k][:, :].rearrange("p (h q) -> p h q", h=4),
                in1=masks[:, mP, :].unsqueeze(1).to_broadcast([128, 4, 128]), op=ALU.mult),
                reads=['E%d' % eb, 'masks'], writes=['E%d' % eb])
            S.op('dve', lambda e, k=eb + 2, mN=mN: e.tensor_tensor(
                out=E[k][:, :].rearrange("p (h q) -> p h q", h=4), in0=E[k][:, :].rearrange("p (h q) -> p h q", h=4),
                in1=masks[:, mN, :].unsqueeze(1).to_broadcast([128, 4, 128]), op=ALU.mult),
                reads=['E%d' % (eb + 2), 'masks'], writes=['E%d' % (eb + 2)])
            for h4 in range(4):
                for jj in range(3):
                    S.op('pe', lambda e, i=i, g=g, jj=jj, h4=h4, k=eb + jj: e.matmul(
                        PS[0][:, h4 * 66:h4 * 66 + 65], lhsT=E[k][:, h4 * 128:(h4 + 1) * 128],
                        rhs=vp[:, i + jj, g, 0:65], start=(jj == 0), stop=(jj == 2)),
                        reads=['E%d' % (eb + jj), 'vp'], writes=['ps0'])
            pso = PS[0][:, 0:264].rearrange("p (h d) -> p h d", h=4)
            S.op('dve', lambda e, g=g, pso=pso: e.tensor_tensor(out=den[:, 0:4], in0=pso[:, :, 64],
                                                                in1=snk[:, 4 * g:4 * g + 4], op=ALU.add),
                 reads=['ps0', 'snk'], writes=['den'])
            S.op('dve', lambda e: e.reciprocal(out=den[:, 4:8], in_=den[:, 0:4]), reads=['den'], writes=['den'])
            S.op('dve', lambda e, g=g, pso=pso: e.tensor_tensor(
                out=Ot[:, g * 256:(g + 1) * 256].rearrange("p (h d) -> p h d", h=4), in0=pso[:, :, 0:64],
                in1=den[:, 4:8].unsqueeze(2).to_broadcast([128, 4, 64]), op=ALU.mult),
                reads=['ps0', 'den'], writes=['Ot'])
        pst = PS[1][:, :].bitcast(BF16)
        for cb in range(4):
            S.op('pe', lambda e, cb=cb, pst=pst: e.transpose(out=pst[:, cb * 128:(cb + 1) * 128],
                                                              in_=Ot[:, cb * 128:(cb + 1) * 128], identity=ident[:, :]),
                 reads=['Ot', 'ident'], writes=['ps1'])
        S.op('act', lambda e, i=i, pst=pst: e.copy(out=mixB[:, :, i * 128:(i + 1) * 128],
                                                   in_=pst[:, 0:512].rearrange("p (c q) -> p c q", c=4)),
             reads=['ps1'], writes=['mixB'])
    for cb in range(4):
        S.dma('sp', ctx['mixT_d'][512 + cb * 128:512 + (cb + 1) * 128, :], mixB[:, cb, :], reads=['mixB'],
              writes=['mixT_d'])
    S.barrier()
    AR.release()


def phase_ffn(ctx, l, xsrc, final):
    S, AR, PS, ins = ctx['S'], ctx['AR'], ctx['PS'], ctx['ins']
    G = 1024
    NT = G // 128
    AR.mark()
    ctx['sc_ss'] = AR.alloc([128, 4], F32, 'ss')
    ctx['sc_xn'] = AR.alloc([128, D], BF16, 'xn')
    ctx['sc_junk'] = ctx['sc_xn']
    ctx['junk_key'] = 'sc_xn'
    h2T = AR.alloc([128, 16, G], BF16, 'h2T')
    wg = [AR.alloc([128, 16, 128], BF16, 'wg%d' % i) for i in range(4)]
    sil = [AR.alloc([128, 512], F32, 'sil%d' % i) for i in range(2)]
    ctx['tp_bank'] = [0, 1]
    ctx['tp_i'] = 0
    w_out, w_gu, w_down = ins['w_out'], ins['w_gate_up'], ins['w_down']
    xdst = ctx['x1_ext']
    cnt = {'wo': 0, 'wg': 0, 'wd': 0, 'pb': 0, 'xb': 0}
    if final:
        fxt = [AR.alloc([128, D], F32, 'fxt%d' % i) for i in range(3)]
        gfin = AR.alloc([128, D], F32, 'gfin')
        fjunk = AR.alloc([128, D], BF16, 'fjunk')
        ssf = AR.alloc([128, 3, 4], F32, 'ssf')
        S.dma('sp', gfin[:, :], ins['final_norm_g'].rearrange("(o d) -> o d", o=1).to_broadcast([128, D]),
              writes=['gfin'])
        fcnt = {'n': 0}

        def final_tile(i):
            b = fcnt['n'] % 3
            fcnt['n'] += 1
            r0 = i * 128
            ss = ssf[:, b, :]
            S.dma('sp', fxt[b][:, :], xdst[HALO + r0:HALO + r0 + 128, :], reads=['x1_ext'], writes=['fxt%d' % b])
            S.op('act', lambda e, b=b, ss=ss: e.activation(out=fjunk[:, :], in_=fxt[b][:, :], func=AF.Square,
                                                           accum_out=ss[:, 0:1]),
                 reads=['fxt%d' % b], writes=['fjunk', 'ssf%d' % b])
            S.op('dve', lambda e, ss=ss: e.tensor_scalar(out=ss[:, 1:2], in0=ss[:, 0:1], scalar1=1.0 / D, scalar2=EPS,
                                                         op0=ALU.mult, op1=ALU.add), reads=['ssf%d' % b],
                 writes=['ssf%d' % b])
            S.op('act', lambda e, ss=ss: e.activation(out=ss[:, 3:4], in_=ss[:, 1:2], func=AF.Sqrt),
                 reads=['ssf%d' % b], writes=['ssf%d' % b])
            S.op('dve', lambda e, ss=ss: e.reciprocal(out=ss[:, 2:3], in_=ss[:, 3:4]), reads=['ssf%d' % b],
                 writes=['ssf%d' % b])
            S.op('dve', lambda e, b=b, ss=ss: e.scalar_tensor_tensor(out=fxt[b][:, :], in0=fxt[b][:, :],
                                                                     scalar=ss[:, 2:3], in1=gfin[:, :], op0=ALU.mult,
                                                                     op1=ALU.mult),
                 reads=['fxt%d' % b, 'ssf%d' % b, 'gfin'], writes=['fxt%d' % b])
            S.dma('act', ctx['y_out'][r0:r0 + 128, :], fxt[b][:, :], reads=['fxt%d' % b], writes=['y_out'])
        fpending = []
    AR.mark()
    mixg = AR.alloc([128, 16, 512], BF16, 'mixg')
    wo = [AR.alloc([128, 16, 512], BF16, 'wo%d' % i) for i in range(2)]
    xm = [AR.alloc([128, D], F32, 'xm%d' % i) for i in range(4)]
    AR.release()
    AR.mark()
    actT = AR.alloc([128, NFF, G], BF16, 'actT')
    wd = [AR.alloc([128, 4, 512], BF16, 'wd%d' % i) for i in range(3)]
    xb = [AR.alloc([128, 512], F32, 'xb%d' % i) for i in range(4)]
    for gi in range(T // G):
        for sg in range(G // 512):
            tok0 = gi * G + sg * 512
            S.dma('sp', mixg[:, :, :], ctx['mixT_d'][:, tok0:tok0 + 512].rearrange("(k p) t -> p k t", p=128),
                  reads=['mixT_d'], writes=['mixg'])
            for t in range(4):
                r0 = HALO + tok0 + t * 128
                S.dma('sp', xm[t][:, :], xsrc[r0:r0 + 128, :], writes=['xm%d' % t])
            for cg in range(4):
                w = cnt['wo'] % 2
                cnt['wo'] += 1
                wload(ctx, wo[w][:, :, :], w_out[l][:, cg * 512:(cg + 1) * 512].rearrange("(k p) c -> p k c", p=128),
                      'wo%d' % w)
                for t in range(4):
                    bank = 2 + cnt['pb'] % 4
                    cnt['pb'] += 1
                    for kc in range(16):
                        S.op('pe', lambda e, kc=kc, t=t, w=w, bank=bank: e.matmul(
                            PS[bank][:, :], lhsT=mixg[:, kc, t * 128:(t + 1) * 128], rhs=wo[w][:, kc, :],
                            start=(kc == 0), stop=(kc == 15)), reads=['mixg', 'wo%d' % w], writes=['ps%d' % bank])
                    S.op('dve', lambda e, t=t, cg=cg, bank=bank: e.tensor_tensor(
                        out=xm[t][:, cg * 512:(cg + 1) * 512], in0=PS[bank][:, :],
                        in1=xm[t][:, cg * 512:(cg + 1) * 512], op=ALU.add),
                        reads=['ps%d' % bank, 'xm%d' % t], writes=['xm%d' % t])
            for t in range(4):
                rms_to_T(ctx, xm[t], ctx['gcolF'][:, l, :], h2T, sg * 512 + t * 128, 'xm%d' % t)
                S.dma('act', ctx['xmid'][tok0 + t * 128:tok0 + (t + 1) * 128, :], xm[t][:, :], reads=['xm%d' % t],
                      writes=['xmid'])
        S.barrier()
        for j in range(NFF):
            if final and fpending and j % 5 == 2:
                final_tile(fpending.pop(0))
            w = (cnt['wg'] % 2) * 2
            cnt['wg'] += 1
            wload(ctx, wg[w][:, :, :], w_gu[l][:, j * 128:(j + 1) * 128].rearrange("(k p) c -> p k c", p=128),
                  'wg%d' % w)
            wload(ctx, wg[w + 1][:, :, :],
                  w_gu[l][:, DFF + j * 128:DFF + (j + 1) * 128].rearrange("(k p) c -> p k c", p=128), 'wg%d' % (w + 1))
            for hf in range(G // 512):
                pb = cnt['pb'] % 4
                cnt['pb'] += 1
                ba, bb = 2 * pb, 2 * pb + 1
                for kc in range(16):
                    S.op('pe', lambda e, kc=kc, w=w, ba=ba, hf=hf: e.matmul(
                        PS[ba][:, :], lhsT=wg[w][:, kc, :], rhs=h2T[:, kc, hf * 512:(hf + 1) * 512],
                        start=(kc == 0), stop=(kc == 15)), reads=['wg%d' % w, 'hT'], writes=['ps%d' % ba])
                for kc in range(16):
                    S.op('pe', lambda e, kc=kc, w=w, bb=bb, hf=hf: e.matmul(
                        PS[bb][:, :], lhsT=wg[w + 1][:, kc, :], rhs=h2T[:, kc, hf * 512:(hf + 1) * 512],
                        start=(kc == 0), stop=(kc == 15)), reads=['wg%d' % (w + 1), 'hT'], writes=['ps%d' % bb])
                si = cnt['pb'] % 2
                S.op('act', lambda e, ba=ba, si=si: e.activation(out=sil[si][:, :], in_=PS[ba][:, :], func=AF.Silu),
                     reads=['ps%d' % ba], writes=['sil%d' % si])
                S.op('dve', lambda e, bb=bb, si=si, j=j, hf=hf: e.tensor_tensor(
                    out=actT[:, j, hf * 512:(hf + 1) * 512], in0=sil[si][:, :], in1=PS[bb][:, :], op=ALU.mult),
                    reads=['ps%d' % bb, 'sil%d' % si], writes=['actT'])
        for cg in range(4):
            for k4 in range(NFF // 4):
                w = cnt['wd'] % 3
                cnt['wd'] += 1
                wload(ctx, wd[w][:, :, :],
                      w_down[l][k4 * 512:(k4 + 1) * 512, cg * 512:(cg + 1) * 512].rearrange("(k p) c -> p k c", p=128),
                      'wd%d' % w)
                for kk in range(4):
                    k = k4 * 4 + kk
                    for t in range(NT):
                        S.op('pe', lambda e, k=k, kk=kk, t=t, w=w: e.matmul(
                            PS[t][:, :], lhsT=actT[:, k, t * 128:(t + 1) * 128], rhs=wd[w][:, kk, :],
                            start=(k == 0), stop=(k == NFF - 1)),
                            reads=['actT', 'wd%d' % w], writes=['ps%d' % t])
            for t in range(NT):
                r0 = gi * G + t * 128
                b = cnt['xb'] % 4
                cnt['xb'] += 1
                S.dma('sp', xb[b][:, :], ctx['xmid'][r0:r0 + 128, cg * 512:(cg + 1) * 512], reads=['xmid'],
                      writes=['xb%d' % b])
                S.op('dve', lambda e, t=t, b=b: e.tensor_tensor(out=xb[b][:, :], in0=PS[t][:, :], in1=xb[b][:, :],
                                                                op=ALU.add),
                     reads=['ps%d' % t, 'xb%d' % b], writes=['xb%d' % b])
                S.dma('act', xdst[HALO + r0:HALO + r0 + 128, cg * 512:(cg + 1) * 512], xb[b][:, :],
                      reads=['xb%d' % b], writes=['x1_ext'])
                if not final and r0 == 0:
                    S.dma('act', ctx['edge_in'][0:128, cg * 512:(cg + 1) * 512], xb[b][:, :], reads=['xb%d' % b],
                          writes=['edge_in'])
                if not final and r0 == T - 128:
                    S.dma('act', ctx['edge_in'][128:256, cg * 512:(cg + 1) * 512], xb[b][:, :], reads=['xb%d' % b],
                          writes=['edge_in'])
        S.barrier()
        if final:
            fpending.extend(range(gi * NT, (gi + 1) * NT))
    if final:
        for i in fpending:
            final_tile(i)
        S.barrier()
    AR.release()
    AR.release()


def halo_start(ctx):
    S = ctx['S']
    S.allgather(ctx['edge_in'][:, :], ctx['edge_all'][:, :], reads=['edge_in'], writes=['edge_all'])
    S.sticky.add('edge_all')


def halo_finish(ctx):
    S, AR, ins = ctx['S'], ctx['AR'], ctx['ins']
    AR.mark()
    hb = [AR.alloc([128, D], F32, 'hb%d' % i) for i in range(2)]
    fl = AR.alloc([128, 2], F32, 'fl')
    S.dma('sp', fl[:, :], ins['t_flags'][:, :], writes=['fl'])
    S.dma('sp', hb[0][:, :], ctx['edge_all'][128:256, :], reads=['edge_all'], writes=['hb0'])
    S.dma('sp', hb[1][:, :], ctx['edge_all'][256:384, :], reads=['edge_all'], writes=['hb1'])
    for i in range(2):
        S.op('dve', lambda e, i=i: e.tensor_scalar(out=hb[i][:, :], in0=hb[i][:, :], scalar1=fl[:, i:i + 1],
                                                   scalar2=None, op0=ALU.mult), reads=['hb%d' % i, 'fl'],
             writes=['hb%d' % i])
    S.dma('sp', ctx['x1_ext'][0:HALO, :], hb[0][:, :], reads=['hb0'], writes=['x1_ext'])
    S.dma('sp', ctx['x1_ext'][HALO + T:TE, :], hb[1][:, :], reads=['hb1'], writes=['x1_ext'])
    S.sticky.discard('edge_all')
    S.barrier()
    AR.release()


def core_info(c):
    if c < 4:
        return dict(kind='p', seq=c // 2, pos0=2048 * (c % 2), L=4096, rank=c % 2)
    return dict(kind='s', seq=c - 4, pos0=0, L=2048, rank=c % 2)


def host_tables(c):
    ci = core_info(c)
    pos0, L, rank = ci['pos0'], ci['L'], ci['rank']
    bf = ml_dtypes.bfloat16
    tb = {}
    tb['t_ident'] = np.eye(128, dtype=np.float32).astype(bf)
    pos = (pos0 - HALO + np.arange(TE)).astype(np.float32)
    inv_freq = (np.float32(500000.0) ** (-np.arange(0, 16, 2, dtype=np.float32) / np.float32(16))).astype(np.float32)
    ang = (pos[:, None] * inv_freq[None, :]).astype(np.float32)
    ang = np.concatenate([ang, ang], axis=1).T
    cs = np.cos(ang).astype(np.float32)
    sn = np.sin(ang).astype(np.float32)
    sn[:8] *= -1.0
    tb['t_rope'] = np.stack([cs, sn]).astype(np.float32)
    pm = np.zeros((16, 16), np.float32)
    for m in range(16):
        pm[(m + 8) % 16, m] = 1.0
    tb['t_perm'] = pm
    kk = np.arange(128)[:, None]
    qq = np.arange(128)[None, :]
    lv = 1.0 if pos0 > 0 else 0.0
    rv = 1.0 if pos0 + T < L else 0.0
    mP = (kk >= qq).astype(np.float32)
    mN = (kk <= qq).astype(np.float32)
    tb['t_mask'] = np.stack([mP * lv, mP, mN, mN * rv]).astype(bf)
    t = pos0 + np.arange(T)
    ic = np.zeros((4, T), np.float32)
    for g, w in enumerate((2, 4, 8, 16)):
        lo = np.clip(t - w // 2, 0, L)
        hi = np.clip(t + w // 2, 0, L)
        ic[g] = (1.0 / (hi - lo).astype(np.float32)).astype(np.float32)
    tb['t_invcnt'] = ic
    fl = np.zeros((128, 2), np.float32)
    fl[:, 0] = lv
    fl[:, 1] = rv
    tb['t_flags'] = fl
    tb.update(fft_tables(c))
    return tb


def fft_tables(c):
    ci = core_info(c)
    pos0h = 2048 * ci['rank']
    L, kind, rank = ci['L'], ci['kind'], ci['rank']
    bf = ml_dtypes.bfloat16
    tb = {}
    N = NFFT
    NA = 33
    fa = np.arange(NA)
    sB = np.arange(64)
    ph = -2.0 * np.pi * np.outer(sB, fa) / 64.0
    w1 = np.concatenate([np.cos(ph), np.sin(ph)], axis=1)
    tb['t_w1f'] = w1.astype(np.float32).astype(bf)
    w1d = w1[:32].copy()
    if kind == 's':
        if rank == 0:
            w1d[16:32] = 0.0
        else:
            w1d[0:16] = 0.0
    tb['t_w1d'] = w1d.astype(np.float32).astype(bf)
    sA = np.arange(128)
    fb = np.arange(128)
    tm = np.zeros((5, 128, 3, 8, 128), np.float32)
    for g in range(5):
        for j in range(8):
            if g * 8 + j >= NA:
                continue
            f = (g * 8 + j) + 64 * fb
            ph = -2.0 * np.pi * np.outer(sA, f) / N
            tm[g, :, 0, j, :] = np.cos(ph)
            tm[g, :, 1, j, :] = np.sin(ph)
            tm[g, :, 2, j, :] = -np.sin(ph)
    tb['t_m'] = tm.reshape(5, 128, 3 * 8 * 128).astype(bf)
    ph = 2.0 * np.pi * np.outer(fb, np.arange(128)) / 128.0
    tb['t_g'] = np.concatenate([np.cos(ph), np.sin(ph), -np.sin(ph)], axis=1).astype(np.float32).astype(bf)
    tA = np.arange(128)[:, None]
    tB = np.arange(16)[None, :]
    tt = (pos0h + 128 * tB + tA).reshape(-1)
    ph = 2.0 * np.pi * np.outer(fa, tt) / N
    wgt = np.where((fa == 0) | (fa == 32), 1.0, 2.0)[:, None]
    tb['t_h'] = np.concatenate([wgt * np.cos(ph), -wgt * np.sin(ph)], axis=0).astype(np.float32).astype(bf)
    s = np.arange(N)
    idx = np.where(s < N // 2, s, N - s)
    valid = np.where(s < N // 2, s < L, (N - s) <= L - 1)
    tl = np.linspace(0.0, 1.0, L, dtype=np.float32)
    idc = np.clip(idx, 0, L - 1)
    tt = tl[idc]
    bands = np.linspace(1e-4, 15.0, 16, dtype=np.float32)[None, :]
    wpos = (np.float32(2.0 * math.pi / L) * idc.astype(np.float32))[:, None]
    z = np.concatenate([tt[:, None], np.cos(bands * wpos), -np.sin(bands * wpos)], axis=1).astype(np.float32)
    z = np.where(valid[:, None], z, 0.0).astype(np.float32)
    tb['t_pos'] = np.ascontiguousarray(z.T)
    tb['t_tdec'] = np.where(valid, tt, 1.0e4).astype(np.float32)[None, :]
    return tb


_CACHE = {}


def kernel(**inputs):
    xp = np.asarray(inputs['x_prompt'], np.float32)
    xs = np.asarray(inputs['x_sample'], np.float32)
    if 'nc' not in _CACHE:
        nc, S, ctx, es = build()
        full_program(ctx)
        S.emit()
        _CACHE['nc'] = nc
        _CACHE['tabs'] = [host_tables(c) for c in range(8)]
    nc = _CACHE['nc']
    wnames = ["mix_norm_g", "w_in", "pool_w", "pool_scale", "attn_sink", "sconv_w", "hy_short_w", "hy_w1", "hy_b1",
              "hy_w2", "hy_b2", "hy_freq", "hy_bias", "w_out", "ffn_norm_g", "w_gate_up",
              "w_down", "final_norm_g"]
    w3f = np.asarray(inputs["hy_w3"], np.float32).reshape(DEPTH, 64, 4, 4, 128)
    dcf = np.asarray(inputs["hy_decay"], np.float32).reshape(DEPTH, 4, 4, 128)
    shared = {k: np.ascontiguousarray(np.asarray(inputs[k], np.float32)) for k in wnames}
    in_maps = []
    for c in range(8):
        ci = core_info(c)
        seq = xp[ci['seq']] if ci['kind'] == 'p' else xs[ci['seq']]
        xe = np.zeros((TE, D), np.float32)
        lo = ci['pos0'] - HALO
        hi = ci['pos0'] + T + HALO
        a, b = max(lo, 0), min(hi, ci['L'])
        xe[a - lo:b - lo] = seq[a:b]
        m = dict(shared)
        m['x_ext'] = xe
        m['hy_w3_sh'] = np.ascontiguousarray(w3f[:, :, :, c % 4, :]).reshape(DEPTH, 64, 512)
        m['hy_decay_sh'] = np.ascontiguousarray(dcf[:, :, c % 4, :]).reshape(DEPTH, 512)
        m.update(_CACHE['tabs'][c])
        in_maps.append(m)
    res = run_bass_kernel_spmd(nc, in_maps, core_ids=list(range(8)))
    _CACHE['last'] = res
    yp = np.zeros((2, 4096, D), np.float32)
    ys = np.zeros((4, 2048, D), np.float32)
    for c in range(8):
        ci = core_info(c)
        y = np.asarray(res.results[c]['y_out'], np.float32)
        if ci['kind'] == 'p':
            yp[ci['seq'], ci['pos0']:ci['pos0'] + T] = y
        else:
            ys[ci['seq']] = y
    return (yp, ys)


SKIP_HYENA = False


def full_program(ctx):
    S, AR = ctx['S'], ctx['AR']
    if not SKIP_HYENA:
        phase_filters(ctx)
    for l in range(DEPTH):
        xsrc = ctx['x_ext'] if l == 0 else ctx['x1_ext']
        AR.mark()
        ctx['hT'] = AR.alloc([128, 16, TE], BF16, 'hT')
        if l == 0:
            phase_norm(ctx, l, xsrc)
        else:
            phase_norm(ctx, l, xsrc, tiles=list(range(1, 17)))
            halo_finish(ctx)
            phase_norm(ctx, l, xsrc, tiles=[0, 17])
        phase_local(ctx, l)
        phase_attn(ctx, l)
        AR.release()
        if SKIP_HYENA:
            AR.mark()
            zt = AR.alloc([128, T], BF16, 'zt')
            S.op('pool', lambda e: e.memset(zt[:, :], 0.0), writes=['zt'])
            for c in range(4):
                S.dma('sp', ctx['mixT_d'][1536 + c * 128:1536 + (c + 1) * 128, :], zt[:, :], reads=['zt'],
                      writes=['mixT_d'])
            S.barrier()
            AR.release()
        else:
            phase_hyena(ctx, l)
        phase_ffn(ctx, l, xsrc, final=(l == DEPTH - 1))
        if l == 0:
            halo_start(ctx)


NA = 33
FGROUPS = [(0, 8), (8, 8), (16, 8), (24, 8), (32, 1)]


def hy_alloc(ctx):
    AR = ctx['AR']
    h = {}
    h['InB'] = AR.alloc([64, 128 * 128], BF16, 'InB')
    h['A'] = AR.alloc([128, 2, NA, 128], BF16, 'A')
    h['Mt'] = [AR.alloc([128, 3, 8, 128], BF16, 'Mt%d' % i) for i in range(len(FGROUPS))]
    h['w1d'] = AR.alloc([32, 2 * NA], BF16, 'w1d')
    h['w1f'] = AR.alloc([64, 2 * NA], BF16, 'w1f')
    S, ins = ctx['S'], ctx['ins']
    S.dma('sp', h['w1d'][:, :], ins['t_w1d'][:, :], writes=['w1d'])
    S.dma('sp', h['w1f'][:, :], ins['t_w1f'][:, :], writes=['w1f'])
    for g in range(len(FGROUPS)):
        S.dma('sp', h['Mt'][g][:, :, :, :], ins['t_m'][g].rearrange("p (a j f) -> p a j f", a=3, j=8),
              writes=['Mt%d' % g])
    return h


def fwd_transform(ctx, h, K, w1, w1key, on_group, after_stage1=None, pre_stage2=None):
    S, PS, ins = ctx['S'], ctx['PS'], ctx['ins']
    In, A = h['InB'], h['A']

    if pre_stage2 is not None:
        pre_stage2()
    for c4 in range(32):
        bank = c4 % 2
        for cc in range(4):
            c = c4 * 4 + cc
            S.op('pe', lambda e, c=c, cc=cc, bank=bank: e.matmul(PS[bank][:, cc * 128:cc * 128 + 2 * NA],
                                                               lhsT=In[0:K, c * 128:(c + 1) * 128], rhs=w1[0:K, :],
                                                               start=True, stop=True),
                 reads=['InB', w1key], writes=['ps%d' % bank])
        src = sap(PS[bank], 0, 128, [(NA, 2), (1, NA), (128, 4)])
        S.op('act', lambda e, c4=c4, src=src: e.copy(out=A[:, :, :, c4 * 4:(c4 + 1) * 4], in_=src),
             reads=['ps%d' % bank], writes=['A'])
    if after_stage1 is not None:
        after_stage1()
    for g, (fa0, nj) in enumerate(FGROUPS):
        m = g
        Mt = h['Mt'][m]
        b0 = 4 * (g % 2)
        for j in range(nj):
            fa = fa0 + j
            bre = b0 + j // 4
            bim = b0 + 2 + j // 4
            col = (j % 4) * 128
            for (bank, la, lb) in ((bre, 0, 2), (bim, 1, 0)):
                S.op('pe', lambda e, bank=bank, la=la, j=j, fa=fa, col=col, Mt=Mt: e.matmul(
                    PS[bank][:, col:col + 128], lhsT=Mt[:, la, j, :], rhs=A[:, 0, fa, :], start=True, stop=False),
                    reads=['Mt%d' % m, 'A'], writes=['ps%d' % bank])
                S.op('pe', lambda e, bank=bank, lb=lb, j=j, fa=fa, col=col, Mt=Mt: e.matmul(
                    PS[bank][:, col:col + 128], lhsT=Mt[:, lb, j, :], rhs=A[:, 1, fa, :], start=False, stop=True),
                    reads=['Mt%d' % m, 'A'], writes=['ps%d' % bank])
        on_group(g, b0, fa0, nj)


def kf_row(l, o, cc, g):
    base = (l * 2 + o) * 2560
    if g < 4:
        return base + (g // 2) * 1024 + cc * 256 + (g % 2) * 128
    return base + 2048 + cc * 128


def phase_filters(ctx):
    S, AR, PS, ins = ctx['S'], ctx['AR'], ctx['PS'], ctx['ins']
    AR.mark()
    h = hy_alloc(ctx)
    h2T = [AR.alloc([64, NFFT], BF16, 'h2T%d' % i) for i in range(2)]
    tdb = AR.alloc([128, NFFT], F32, 'tdb')
    pos = [AR.alloc([33, 512], F32, 'pos%d' % i) for i in range(2)]
    w1 = [AR.alloc([33, 64], F32, 'fw1%d' % i) for i in range(2)]
    w2 = [AR.alloc([64, 64], F32, 'fw2%d' % i) for i in range(2)]
    w3 = [AR.alloc([64, 512], BF16, 'fw3%d' % i) for i in range(2)]
    prm = AR.alloc([64, 2, 8], F32, 'prm')
    negd = AR.alloc([128, 2, 8], F32, 'negd')
    arg = [AR.alloc([64, 512], F32, 'arg%d' % i) for i in range(2)]
    argi = [AR.alloc([64, 512], mybir.dt.int32, 'argi%d' % i) for i in range(2)]
    argf = [AR.alloc([64, 512], F32, 'argf%d' % i) for i in range(2)]
    h1 = [AR.alloc([64, 512], F32, 'h1%d' % i) for i in range(2)]
    ex = [AR.alloc([128, 512], F32, 'ex%d' % i) for i in range(2)]
    kt = [AR.alloc([128, 512], F32, 'kt%d' % i) for i in range(2)]
    kk = AR.alloc([128, NFFT], BF16, 'kk')
    acc = AR.alloc([128, 4, 20], F32, 'acc')
    kfo = [AR.alloc([128, 2, 8, 128], BF16, 'kfo%d' % i) for i in range(2)]
    PI = math.pi
    G4 = [[0, 1, 2, 3], [4, 5, 6, 7]]
    for k in range(2):
        S.op('pool', lambda e, k=k: e.memset(kfo[k][:, :, :, :], 0.0), writes=['kfo%d' % k])
    for l in range(DEPTH):
        S.dma('sp', w1[l][:, :], ins['hy_w1'][l], writes=['fw%d' % l])
        S.dma('sp', w2[l][:, :], ins['hy_w2'][l], writes=['fw%d' % l])
        wload(ctx, w3[l][:, :], ins['hy_w3_sh'][l], 'fw3%d' % l)
        S.dma('sp', prm[:, l, 0:1], ins['hy_freq'][l].rearrange("(p o) -> p o", o=1), writes=['prm%d' % l])
        S.dma('sp', prm[:, l, 1:2], ins['hy_b1'][l].rearrange("(p o) -> p o", o=1), writes=['prm%d' % l])
        S.dma('sp', prm[:, l, 2:3], ins['hy_b2'][l].rearrange("(p o) -> p o", o=1), writes=['prm%d' % l])
        S.dma('sp', negd[:, l, 0:4], ins['hy_decay_sh'][l].rearrange("(q p) -> p q", p=128), writes=['negd%d' % l],
              allow_slow_non_contiguous=True)
    S.dma('sp', tdb[:, :], ins['t_tdec'][0:1, :].to_broadcast([128, NFFT]), writes=['tdb'])
    for l in range(DEPTH):
        S.op('dve', lambda e, l=l: e.tensor_scalar(out=prm[:, l, 3:4], in0=prm[:, l, 0:1], scalar1=1.0 / (2.0 * PI),
                                                   scalar2=None, op0=ALU.mult), reads=['prm%d' % l],
             writes=['prm%d' % l])
        S.op('dve', lambda e, l=l: e.tensor_tensor(out=prm[:, l, 4:6], in0=prm[:, l, 1:3],
                                                   in1=prm[:, l, 3:4].to_broadcast([64, 2]), op=ALU.mult),
             reads=['prm%d' % l], writes=['prm%d' % l])
        S.op('act', lambda e, l=l: e.activation(out=negd[:, l, 4:8], in_=negd[:, l, 0:4], func=AF.Abs),
             reads=['negd%d' % l], writes=['negd%d' % l])
        S.op('dve', lambda e, l=l: e.tensor_scalar(out=negd[:, l, 0:4], in0=negd[:, l, 4:8], scalar1=-1.0,
                                                   scalar2=None, op0=ALU.mult), reads=['negd%d' % l],
             writes=['negd%d' % l])
    for sb in range(16):
        pb = sb % 2
        S.dma('sp', pos[pb][:, :], ins['t_pos'][:, sb * 512:(sb + 1) * 512], writes=['pos%d' % pb])
        for stage in range(2):
            for l in range(DEPTH):
                b = l
                if stage == 0:
                    src, skey, wt, kq = pos[pb], 'pos%d' % pb, w1[l], 33
                else:
                    src, skey, wt, kq = h1[b], 'h1%d' % b, w2[l], 64
                bank = 6 + b
                a, ai, af = arg[b], argi[b], argf[b]
                S.op('pe', lambda e, src=src, wt=wt, bank=bank, kq=kq: e.matmul(
                    PS[bank][0:64, :], lhsT=wt[0:kq, :], rhs=src[0:kq, :], start=True, stop=True),
                    reads=['fw%d' % l, skey], writes=['ps%d' % bank])
                S.op('act', lambda e, stage=stage, a=a, bank=bank, l=l: e.activation(
                    out=a[:, :], in_=PS[bank][0:64, :], func=AF.Identity, scale=prm[:, l, 3:4],
                    bias=prm[:, l, 4 + stage:5 + stage]),
                    reads=['ps%d' % bank, 'prm%d' % l], writes=['arg%d' % b])
                S.op('dve', lambda e, a=a, ai=ai: e.tensor_copy(out=ai[:, :], in_=a[:, :]), reads=['arg%d' % b],
                     writes=['argi%d' % b])
                S.op('dve', lambda e, ai=ai, af=af: e.tensor_copy(out=af[:, :], in_=ai[:, :]), reads=['argi%d' % b],
                     writes=['argf%d' % b])
                S.op('dve', lambda e, a=a, af=af: e.tensor_tensor(out=a[:, :], in0=a[:, :], in1=af[:, :],
                                                                  op=ALU.subtract),
                     reads=['arg%d' % b, 'argf%d' % b], writes=['arg%d' % b])
                if stage == 0:
                    S.op('act', lambda e, a=a, b=b: e.activation(out=h1[b][:, :], in_=a[:, :], func=AF.Sin,
                                                                 scale=2.0 * PI * (1.0 - 1e-6)),
                         reads=['arg%d' % b], writes=['h1%d' % b])
                else:
                    S.op('act', lambda e, a=a, sb=sb, l=l: e.activation(
                        out=h2T[l][:, sb * 512:(sb + 1) * 512], in_=a[:, :], func=AF.Sin,
                        scale=2.0 * PI * (1.0 - 1e-6)), reads=['arg%d' % b], writes=['h2T%d' % l])
    kl = ctx['kf_loc']
    kd = ctx['kf_d']
    it = 0
    for l in range(DEPTH):
        for o in range(2):
            kb = it % 2
            it += 1
            for sb in range(16):
                dr = 0 if sb < 8 else 1
                q = o * 2 + dr
                t = sb % 2
                bank = 4 + t
                S.op('pe', lambda e, q=q, sb=sb, bank=bank, l=l: e.matmul(
                    PS[bank][:, :], lhsT=w3[l][:, q * 128:(q + 1) * 128], rhs=h2T[l][:, sb * 512:(sb + 1) * 512],
                    start=True, stop=True), reads=['fw3%d' % l, 'h2T%d' % l], writes=['ps%d' % bank])
                S.op('act', lambda e, t=t, sb=sb, q=q, l=l: e.activation(
                    out=ex[t][:, :], in_=tdb[:, sb * 512:(sb + 1) * 512], func=AF.Exp, scale=negd[:, l, q:q + 1]),
                    reads=['tdb', 'negd%d' % l], writes=['ex%d' % t])
                S.op('dve', lambda e, t=t, bank=bank: e.tensor_tensor(out=kt[t][:, :], in0=PS[bank][:, :],
                                                                      in1=ex[t][:, :], op=ALU.mult),
                     reads=['ps%d' % bank, 'ex%d' % t], writes=['kt%d' % t])
                S.op('dve', lambda e, sb=sb, t=t, aq=l * 2 + o: e.tensor_reduce(
                    out=acc[:, aq, sb:sb + 1], in_=kt[t][:, :], axis=AX.X, op=ALU.add,
                    apply_absolute_value=True), reads=['kt%d' % t], writes=['acc'])
                S.op('act', lambda e, sb=sb, t=t: e.copy(out=kk[:, sb * 512:(sb + 1) * 512], in_=kt[t][:, :]),
                     reads=['kt%d' % t], writes=['kk'])
            ai_ = l * 2 + o
            S.op('dve', lambda e, ai_=ai_: e.tensor_reduce(out=acc[:, ai_, 16:17], in_=acc[:, ai_, 0:16], axis=AX.X,
                                                           op=ALU.add), reads=['acc'], writes=['acc'])
            S.op('dve', lambda e, ai_=ai_: e.tensor_scalar(out=acc[:, ai_, 17:18], in0=acc[:, ai_, 16:17],
                                                           scalar1=float(NFFT), scalar2=None, op0=ALU.mult),
                 reads=['acc'], writes=['acc'])
            S.op('dve', lambda e, ai_=ai_: e.reciprocal(out=acc[:, ai_, 18:19], in_=acc[:, ai_, 17:18]),
                 reads=['acc'], writes=['acc'])
            S.dma('sp', ctx['rn_loc'][ai_].rearrange("(p o) -> p o", o=1), acc[:, ai_, 18:19], reads=['acc'],
                  writes=['rn_loc'])
            S.dma('act', ctx['kk_d'][kb], kk[:, :], reads=['kk'], writes=['kk_d%d' % kb])
            S.dma('sp', h['InB'][0:64, :].rearrange("p (c s) -> p c s", s=128),
                  ctx['kk_d'][kb].rearrange("c (t s) -> t c s", s=128), reads=['kk_d%d' % kb], writes=['InB'])

            def store(g, b0, fa0, nj, l=l, o=o):
                k = g % 2
                for bi in range(4):
                    r, jh = bi // 2, bi % 2
                    n = min(4, nj - jh * 4)
                    if n <= 0:
                        continue
                    S.op('act', lambda e, bi=bi, r=r, jh=jh, k=k, n=n: e.copy(
                        out=kfo[k][:, r, jh * 4:jh * 4 + n, :],
                        in_=PS[b0 + bi][:, 0:n * 128].rearrange("p (j c) -> p j c", j=n)),
                        reads=['ps%d' % (b0 + bi)], writes=['kfo%d' % k])
                r0 = ((l * 2 + o) * 5 + g) * 128
                S.dma('act', kl[r0:r0 + 128, :].rearrange("p (r j c) -> p r j c", r=2, j=8), kfo[k][:, :, :, :],
                      reads=['kfo%d' % k], writes=['kf_loc'])
            fwd_transform(ctx, h, 64, h['w1f'], 'w1f', store)
            lb = (l * 2 + o) * 640
            db = (l * 2 + o) * 2560
            for (lo_, n_, do_) in ((0, 256, 0), (256, 256, 1024), (512, 128, 2048)):
                S.allgather(kl[lb + lo_:lb + lo_ + n_, :], kd[db + do_:db + do_ + 4 * n_, :],
                            reads=['kf_loc'], writes=['kf_d'], groups=G4, big=True)
    S.allgather(ctx['rn_loc'][:, :], ctx['rn_d'][:, :], reads=['rn_loc'], writes=['rn_d'], groups=G4, big=True)
    S.sticky.update(['kf_d', 'rn_d'])
    S.barrier()
    AR.release()


def phase_hyena(ctx, l):
    S, AR, PS, ins = ctx['S'], ctx['AR'], ctx['PS'], ctx['ins']
    AR.mark()
    h = hy_alloc(ctx)
    InB = h['InB']
    NS = 2 * NA
    Y1 = AR.alloc([128, 128, NS], BF16, 'Y1')
    Y2 = AR.alloc([128, 128, NS], BF16, 'Y2')
    kfg = [AR.alloc([128, 2, 8, 128], BF16, 'kfg%d' % i) for i in range(2)]
    tmp = [AR.alloc([128, 512], F32, 'tmp%d' % i) for i in range(4)]
    G = AR.alloc([128, 3 * 128], BF16, 'G')
    H = AR.alloc([NS, 2048], BF16, 'H')
    rn = AR.alloc([128, 16], F32, 'rn')
    bia = AR.alloc([128, 2, 4], F32, 'bia')
    Bb = AR.alloc([NS, 128 * 128], BF16, 'Bb')
    gt = [AR.alloc([128, T], BF16, 'gt%d' % i) for i in range(2)]
    vt = [AR.alloc([128, T], BF16, 'vt%d' % i) for i in range(2)]
    bv = AR.alloc([128, T], F32, 'bv')
    yt = AR.alloc([128, T], F32, 'yt')
    ut = AR.alloc([128, T], BF16, 'ut')
    S.dma('sp', G[:, :], ins['t_g'][:, :], writes=['G'])
    S.dma('sp', H[:, :], ins['t_h'][:, :], writes=['H'])
    S.dma('sp', rn[:, :], ctx['rn_d'].rearrange("i p -> p i"), reads=['rn_d'], writes=['rn'],
          allow_slow_non_contiguous=True)
    for o in range(2):
        S.dma('sp', bia[:, o, :], ins['hy_bias'][l][o].rearrange("(c p) -> p c", p=128), writes=['bia'],
              allow_slow_non_contiguous=True)
    it = 0
    for o in range(2):
        src_all, src_own, skey_all, skey_own = ((ctx['hv_all'], ctx['hv_in'], 'hv_all', 'hv_in') if o == 0 else
                                                (ctx['hu_all'], ctx['hu_in'], 'hu_all', 'hu_in'))

        def load_in(cc, src_all=src_all, skey_all=skey_all):
            for r in range(2):
                S.dma('sp', InB[16 * r:16 * r + 16, :], src_all[r * 64 + cc * 16:r * 64 + (cc + 1) * 16, :],
                      reads=[skey_all], writes=['InB'])
        load_in(0)
        for cc in range(4):
            ib = it % 2
            it += 1
            S.dma('sp', gt[ib][:, :], ctx['g12_d'][o, cc * 128:(cc + 1) * 128, :], reads=['g12_d'],
                  writes=['gt%d' % ib])
            S.dma('sp', vt[ib][:, :].rearrange("c (t s) -> c t s", s=128),
                  src_own[cc * 16:(cc + 1) * 16, :].rearrange("t (c s) -> c t s", s=128), reads=[skey_own],
                  writes=['vt%d' % ib])

            def load_kf(g, l=l, o=o, cc=cc):
                k = g % 2
                nj_ = FGROUPS[g][1]
                r0 = kf_row(l, o, cc, g)
                S.dma('sp', kfg[k][:, :, 0:nj_, :],
                      ctx['kf_d'][r0:r0 + 128, :].rearrange("p (r j c) -> p r j c", r=2, j=8)[:, :, 0:nj_, :],
                      reads=['kf_d'], writes=['kfg%d' % k])

            def pre2(load_kf=load_kf):
                load_kf(0)
                load_kf(1)

            def prod(g, b0, fa0, nj, l=l, o=o, cc=cc, load_kf=load_kf):
                k = g % 2
                for hb in range((nj + 3) // 4):
                    n = min(4, nj - hb * 4)
                    xre, xim = PS[b0 + hb], PS[b0 + 2 + hb]
                    kre = kfg[k][:, 0, hb * 4:hb * 4 + n, :]
                    kim = kfg[k][:, 1, hb * 4:hb * 4 + n, :]
                    v3 = lambda t, n=n: t[:, 0:n * 128].rearrange("p (j c) -> p j c", j=n)
                    for ti, (xa, ka) in enumerate(((xre, kre), (xim, kim), (xre, kim), (xim, kre))):
                        S.op('dve', lambda e, ti=ti, xa=xa, ka=ka, v3=v3: e.tensor_tensor(
                            out=v3(tmp[ti]), in0=v3(xa), in1=ka, op=ALU.mult),
                            reads=['ps%d' % (b0 + hb), 'ps%d' % (b0 + 2 + hb), 'kfg%d' % k], writes=['tmp%d' % ti])
                    f0 = fa0 + hb * 4
                    v3t = lambda t, n=n: t[:, 0:n * 128].rearrange("p (j c) -> p c j", j=n)
                    S.op('pool', lambda e, f0=f0, n=n, v3t=v3t: e.tensor_tensor(
                        out=Y1[:, :, f0:f0 + n], in0=v3t(tmp[0]), in1=v3t(tmp[1]), op=ALU.subtract),
                        reads=['tmp0', 'tmp1'], writes=['Y1'])
                    S.op('dve', lambda e, f0=f0, n=n, v3t=v3t: e.tensor_tensor(
                        out=Y1[:, :, NA + f0:NA + f0 + n], in0=v3t(tmp[2]), in1=v3t(tmp[3]), op=ALU.add),
                        reads=['tmp2', 'tmp3'], writes=['Y1'])
                    S.op('act', lambda e, f0=f0, n=n: e.activation(out=Y2[:, :, f0:f0 + n],
                                                                   in_=Y1[:, :, NA + f0:NA + f0 + n], func=AF.Copy,
                                                                   scale=-1.0), reads=['Y1'], writes=['Y2'])
                    S.op('act', lambda e, f0=f0, n=n: e.copy(out=Y2[:, :, NA + f0:NA + f0 + n],
                                                             in_=Y1[:, :, f0:f0 + n]), reads=['Y1'], writes=['Y2'])
                if g + 2 < len(FGROUPS):
                    load_kf(g + 2)

            def after1(cc=cc, load_in=load_in):
                if cc < 3:
                    load_in(cc + 1)
            fwd_transform(ctx, h, 32, h['w1d'], 'w1d', prod, after_stage1=after1, pre_stage2=pre2)
            Bv = Bb[0:NS, :].rearrange("p (t c) -> p t c", t=128)
            S.op('dve', lambda e, o=o, cc=cc, ib=ib: e.tensor_scalar(
                out=bv[:, :], in0=vt[ib][:, :], scalar1=bia[:, o, cc:cc + 1], scalar2=None, op0=ALU.mult),
                reads=['vt%d' % ib, 'bia'], writes=['bv'])
            for c4 in range(32):
                bank = c4 % 4
                for ci in range(4):
                    c = c4 * 4 + ci
                    S.op('pe', lambda e, bank=bank, c=c, ci=ci: e.matmul(
                        PS[bank][0:NS, ci * 128:(ci + 1) * 128], lhsT=Y1[:, c, :], rhs=G[:, 0:128],
                        start=True, stop=False), reads=['Y1', 'G'], writes=['ps%d' % bank])
                    S.op('pe', lambda e, bank=bank, c=c, ci=ci: e.matmul(
                        PS[bank][0:NS, ci * 128:(ci + 1) * 128], lhsT=Y2[:, c, :], rhs=G[:, 128:256],
                        start=False, stop=True), reads=['Y2', 'G'], writes=['ps%d' % bank])
                srcv = PS[bank][0:NS, :].rearrange("p (c t) -> p t c", c=4)
                if c4 % 2 == 0:
                    S.op('act', lambda e, c4=c4, srcv=srcv: e.copy(out=Bv[:, :, c4 * 4:c4 * 4 + 4], in_=srcv),
                         reads=['ps%d' % bank], writes=['Bb'])
                else:
                    S.op('dve', lambda e, c4=c4, srcv=srcv: e.tensor_copy(out=Bv[:, :, c4 * 4:c4 * 4 + 4], in_=srcv),
                         reads=['ps%d' % bank], writes=['Bb'])
            for tA in range(128):
                bank = 4 + tA // 32
                col = (tA % 32) * 16
                S.op('pe', lambda e, tA=tA, bank=bank, col=col: e.matmul(
                    PS[bank][:, col:col + 16], lhsT=Bv[:, tA, :], rhs=H[:, tA * 16:(tA + 1) * 16],
                    start=True, stop=True), reads=['Bb', 'H'], writes=['ps%d' % bank])
            ridx = cc * 4 + l * 2 + o
            for b in range(4):
                dst = lambda t, b=b: t[:, :].rearrange("p (tb ta) -> p ta tb", ta=128)[:, 32 * b:32 * b + 32, :]
                S.op('dve', lambda e, b=b, dst=dst, ridx=ridx: e.scalar_tensor_tensor(
                    out=dst(yt), in0=PS[4 + b][:, :].rearrange("p (ta tb) -> p ta tb", tb=16),
                    scalar=rn[:, ridx:ridx + 1], in1=dst(bv), op0=ALU.mult, op1=ALU.add),
                    reads=['ps%d' % (4 + b), 'rn', 'bv'], writes=['yt'])
            S.op('dve', lambda e, ib=ib: e.tensor_tensor(out=ut[:, :], in0=yt[:, :], in1=gt[ib][:, :], op=ALU.mult),
                 reads=['yt', 'gt%d' % ib], writes=['ut'])
            if o == 0:
                S.dma('pool', ctx['hu_in'][cc * 16:(cc + 1) * 16, :].rearrange("t (c s) -> c t s", s=128),
                      ut[:, :].rearrange("c (t s) -> c t s", s=128), reads=['ut'], writes=['hu_in'])
            else:
                S.dma('pool', ctx['mixT_d'][1536 + cc * 128:1536 + (cc + 1) * 128, :], ut[:, :], reads=['ut'],
                      writes=['mixT_d'])
        if o == 0:
            S.allgather(ctx['hu_in'][:, :], ctx['hu_all'][:, :], reads=['hu_in'], writes=['hu_all'])
    S.barrier()
    AR.release()
```
